# Optimizing a Trainium2 kernel written in Bass

```python
import jax, jax.numpy as jnp
from jax import lax
import numpy as np

D_MODEL = 1024
BATCH = 4
SEQ = 4096
DEPTH = 2
DEC_BATCH = 8
DEC_SEQ = 32
PAST_LEN = 1024

CHUNK = 64
D_A = D_MODEL
CONV_A_W = 3
D_B = D_MODEL
CONV_B_W = 4
LRU_BLOCKS = 8
LRU_BW = D_B // LRU_BLOCKS
LRU_C = 8.0
D_C = D_MODEL
C_HEADS = 8
C_HD = D_C // C_HEADS
PEER_HEADS = 8
PEER_DQ = 256
PEER_DH = PEER_DQ // 2
N_KEYS = 128
N_EXPERTS = N_KEYS * N_KEYS
PEER_TOPK = 16
PEER_BLOCK = 128
ALPHA = (2.0 * DEPTH) ** 0.25
BETA = (8.0 * DEPTH) ** -0.25
LN_EPS = 1e-5
RMS_EPS = 1e-6
IN_SPLITS = (D_A, D_A, D_A, D_B, D_B, D_C, D_C, D_C, D_C, D_MODEL, D_MODEL, D_MODEL)
IN_COLS = sum(IN_SPLITS)

kernel_name = 'hybrid_conv_rglru_hgrn2_peer_stream_step'


def layer_norm(x, g, b):
    xf = x.astype(jnp.float32)
    mu = jnp.mean(xf, axis=-1, keepdims=True)
    var = jnp.mean(jnp.square(xf - mu), axis=-1, keepdims=True)
    return ((xf - mu) * lax.rsqrt(var + LN_EPS) * g + b).astype(x.dtype)


def causal_dwconv(u, past, w):
    width = w.shape[0]
    t_len = u.shape[1]
    full = jnp.concatenate([past.astype(u.dtype), u], axis=1)
    y = sum(full[:, k:k + t_len] * w[k] for k in range(width))
    return y, full[:, full.shape[1] - (width - 1):]


def rg_lru(x, w_a, b_a, w_x, b_x, lam, h0):
    bsz, t_len, _ = x.shape
    xf = x.astype(jnp.float32)
    xb = xf.reshape(bsz, t_len, LRU_BLOCKS, LRU_BW)
    r = jax.nn.sigmoid(jnp.einsum('btni,nij->btnj', xb, w_a).reshape(bsz, t_len, D_B) + b_a)
    gi = jax.nn.sigmoid(jnp.einsum('btni,nij->btnj', xb, w_x).reshape(bsz, t_len, D_B) + b_x)
    log_a = -LRU_C * r * jax.nn.softplus(-lam.astype(jnp.float32))
    a = jnp.exp(log_a)
    u = jnp.sqrt(-jnp.expm1(2.0 * log_a)) * (gi * xf)
    u = u.at[:, 0].add(a[:, 0] * h0.astype(jnp.float32))

    def combine(left, right):
        a1, b1 = left
        a2, b2 = right
        return a1 * a2, a2 * b1 + b2

    _, h = lax.associative_scan(combine, (a, u), axis=1)
    return h.astype(x.dtype), h[:, -1].astype(x.dtype)


def hgrn2_chunk(s0, blk):
    q, k, v, logf = blk
    l_len = q.shape[1]
    b = jnp.cumsum(logf, axis=1)
    mask = jnp.tril(jnp.ones((l_len, l_len), dtype=bool))
    diff = b[:, :, None] - b[:, None, :]
    decay = jnp.exp(jnp.where(mask[None, :, :, None, None], diff, -jnp.inf))
    att = jnp.einsum('bthd,bshd,btshd->bhts', q, k, decay)
    o = (jnp.einsum('bthd,bhde->bthe', q * jnp.exp(b), s0)
         + jnp.einsum('bhts,bshe->bthe', att, v))
    b_last = b[:, -1]
    s_new = (jnp.exp(b_last)[..., None] * s0
             + jnp.einsum('bshd,bshe->bhde', k * jnp.exp(b_last[:, None] - b), v))
    return s_new, o


def hgrn2(q, k, v, logf, s0):
    bsz, t_len, nh, _ = q.shape
    l_len = min(CHUNK, t_len)
    n_chunks = t_len // l_len

    def to_chunks(t):
        return t.reshape(bsz, n_chunks, l_len, nh, t.shape[-1]).swapaxes(0, 1)

    s_t, o = lax.scan(hgrn2_chunk, s0, (to_chunks(q), to_chunks(k), to_chunks(v), to_chunks(logf)))
    o = o.swapaxes(0, 1).reshape(bsz, t_len, nh, o.shape[-1])
    return o, s_t


def peer_block(xt, wq, keys, u_tab, v_tab):
    n = xt.shape[0]
    q = (xt @ wq).reshape(n, PEER_HEADS, 2, PEER_DH)
    s = jnp.einsum('nhpd,hpkd->nhpk', q, keys).astype(jnp.float32)
    sv, si = lax.top_k(s, PEER_TOPK)
    cand = (sv[:, :, 0, :, None] + sv[:, :, 1, None, :]).reshape(n, PEER_HEADS, PEER_TOPK * PEER_TOPK)
    cid = (si[:, :, 0, :, None] * N_KEYS + si[:, :, 1, None, :]).reshape(n, PEER_HEADS, PEER_TOPK * PEER_TOPK)
    top_s, top_p = lax.top_k(cand, PEER_TOPK)
    expert = jnp.take_along_axis(cid, top_p, axis=-1)
    g = jax.nn.softmax(top_s, axis=-1)
    h = jax.nn.gelu(jnp.einsum('nd,nhkd->nhk', xt, u_tab[expert]), approximate=False)
    return jnp.einsum('nhk,nhkd->nd', (g * h).astype(xt.dtype), v_tab[expert])


def peer(x, wq, keys, u_tab, v_tab):
    bsz, t_len, d = x.shape
    n = bsz * t_len
    blk = PEER_BLOCK if n % PEER_BLOCK == 0 else n
    xt = x.reshape(n // blk, blk, d)
    out = lax.map(lambda t: peer_block(t, wq, keys, u_tab, v_tab), xt)
    return out.reshape(bsz, t_len, d)


def trunk_layer(x, past_a, past_b, h0, s0, lb, w_in, b_in, conv_a_w, conv_b_w, conv_b_b,
                lru_wa, lru_ba, lru_wx, lru_bx, lru_lambda, hgrn_norm_g, w_out_a, w_out_b,
                w_out_c, w_o, ln1_g, ln1_b, peer_wq, peer_keys, peer_u, peer_v, ln2_g, ln2_b):
    bsz, t_len, _ = x.shape
    z = x @ w_in + b_in
    split_points = np.cumsum(IN_SPLITS)[:-1].tolist()
    a_b, a_c, a_x, b_x, b_g, c_q, c_f, c_i, c_g, g_a, g_b, g_c = jnp.split(z, split_points, axis=-1)

    conv_a, new_a = causal_dwconv(a_c * a_x, past_a, conv_a_w)
    y_a = (a_b * conv_a) @ w_out_a

    conv_b, new_b = causal_dwconv(b_x, past_b, conv_b_w)
    lru_out, h_t = rg_lru(conv_b + conv_b_b, lru_wa, lru_ba, lru_wx, lru_bx, lru_lambda, h0)
    y_b = (jax.nn.gelu(b_g, approximate=False) * lru_out) @ w_out_b

    def heads(t):
        return t.astype(jnp.float32).reshape(bsz, t_len, C_HEADS, C_HD)

    lb_h = lb.reshape(C_HEADS, C_HD)
    f = lb_h + (1.0 - lb_h) * jax.nn.sigmoid(heads(c_f))
    o, s_t = hgrn2(jax.nn.silu(heads(c_q)), 1.0 - f, heads(c_i), jnp.log(f), s0.astype(jnp.float32))
    o = o * lax.rsqrt(jnp.mean(jnp.square(o), axis=-1, keepdims=True) + RMS_EPS) * hgrn_norm_g
    o = o * jax.nn.silu(heads(c_g))
    y_c = o.reshape(bsz, t_len, D_C).astype(x.dtype) @ w_out_c

    merged = jax.nn.sigmoid(g_a) * y_a + jax.nn.sigmoid(g_b) * y_b + jax.nn.sigmoid(g_c) * y_c
    x = layer_norm(ALPHA * x + merged @ w_o, ln1_g, ln1_b)
    x = layer_norm(ALPHA * x + peer(x, peer_wq, peer_keys, peer_u, peer_v), ln2_g, ln2_b)
    return x, new_a, new_b, h_t, s_t.astype(x.dtype)


def setup_inputs(seed: int = 0) -> dict:
    key = jax.random.key(seed)
    ks = jax.random.split(key, 32)
    nrm = jax.random.normal
    f32 = jnp.float32
    a0 = jax.random.uniform(ks[15], (DEPTH, D_B), f32, 0.9, 0.999)
    a_root = a0 ** (1.0 / LRU_C)
    lru_lambda = jnp.log(a_root) - jnp.log1p(-a_root)
    return {
        'x_prompt': nrm(ks[0], (BATCH, SEQ, D_MODEL), f32),
        'x_sample': nrm(ks[1], (DEC_BATCH, DEC_SEQ, D_MODEL), f32),
        'state_conv_a': nrm(ks[2], (DEPTH, DEC_BATCH, CONV_A_W - 1, D_A), f32),
        'state_conv_b': nrm(ks[3], (DEPTH, DEC_BATCH, CONV_B_W - 1, D_B), f32),
        'state_lru': 0.5 * nrm(ks[4], (DEPTH, DEC_BATCH, D_B), f32),
        'state_hgrn': 0.5 * nrm(ks[5], (DEPTH, DEC_BATCH, C_HEADS, C_HD, C_HD), f32),
        'w_in': nrm(ks[6], (DEPTH, D_MODEL, IN_COLS), f32) * D_MODEL ** -0.5,
        'b_in': 0.02 * nrm(ks[7], (DEPTH, IN_COLS), f32),
        'conv_a_w': nrm(ks[8], (DEPTH, CONV_A_W, D_A), f32) * CONV_A_W ** -0.5,
        'conv_b_w': nrm(ks[9], (DEPTH, CONV_B_W, D_B), f32) * CONV_B_W ** -0.5,
        'conv_b_b': 0.02 * nrm(ks[10], (DEPTH, D_B), f32),
        'lru_wa': nrm(ks[11], (DEPTH, LRU_BLOCKS, LRU_BW, LRU_BW), f32) * LRU_BW ** -0.5,
        'lru_ba': 0.02 * nrm(ks[12], (DEPTH, D_B), f32),
        'lru_wx': nrm(ks[13], (DEPTH, LRU_BLOCKS, LRU_BW, LRU_BW), f32) * LRU_BW ** -0.5,
        'lru_bx': 0.02 * nrm(ks[14], (DEPTH, D_B), f32),
        'lru_lambda': lru_lambda,
        'hgrn_lb_logits': 0.5 * nrm(ks[16], (DEPTH, D_C), f32),
        'hgrn_norm_g': 1.0 + 0.05 * nrm(ks[17], (DEPTH, C_HD), f32),
        'w_out_a': nrm(ks[18], (DEPTH, D_A, D_MODEL), f32) * D_A ** -0.5,
        'w_out_b': nrm(ks[19], (DEPTH, D_B, D_MODEL), f32) * D_B ** -0.5,
        'w_out_c': nrm(ks[20], (DEPTH, D_C, D_MODEL), f32) * D_C ** -0.5,
        'w_o': nrm(ks[21], (DEPTH, D_MODEL, D_MODEL), f32) * (BETA * D_MODEL ** -0.5),
        'ln1_g': 1.0 + 0.05 * nrm(ks[22], (DEPTH, D_MODEL), f32),
        'ln1_b': 0.02 * nrm(ks[23], (DEPTH, D_MODEL), f32),
        'peer_wq': nrm(ks[24], (DEPTH, D_MODEL, PEER_HEADS * PEER_DQ), f32) * D_MODEL ** -0.5,
        'peer_keys': nrm(ks[25], (DEPTH, PEER_HEADS, 2, N_KEYS, PEER_DH), f32) * PEER_DH ** -0.5,
        'peer_u': nrm(ks[26], (DEPTH, N_EXPERTS, D_MODEL), f32) * D_MODEL ** -0.5,
        'peer_v': nrm(ks[27], (DEPTH, N_EXPERTS, D_MODEL), f32) * (BETA * PEER_HEADS ** -0.5),
        'ln2_g': 1.0 + 0.05 * nrm(ks[28], (DEPTH, D_MODEL), f32),
        'ln2_b': 0.02 * nrm(ks[29], (DEPTH, D_MODEL), f32),
    }


def reference(x_prompt, x_sample, state_conv_a, state_conv_b, state_lru, state_hgrn,
              w_in, b_in, conv_a_w, conv_b_w, conv_b_b, lru_wa, lru_ba, lru_wx, lru_bx,
              lru_lambda, hgrn_lb_logits, hgrn_norm_g, w_out_a, w_out_b, w_out_c, w_o,
              ln1_g, ln1_b, peer_wq, peer_keys, peer_u, peer_v, ln2_g, ln2_b):
    p = jax.nn.softmax(hgrn_lb_logits.astype(jnp.float32), axis=0)
    lbs = jnp.cumsum(p, axis=0) - p[0]
    xp, xs = x_prompt, x_sample
    bp = xp.shape[0]
    pa_l, pb_l, ph_l, ps_l = [], [], [], []
    sa_l, sb_l, sh_l, ss_l = [], [], [], []
    for l in range(DEPTH):
        lp = (w_in[l], b_in[l], conv_a_w[l], conv_b_w[l], conv_b_b[l], lru_wa[l], lru_ba[l],
              lru_wx[l], lru_bx[l], lru_lambda[l], hgrn_norm_g[l], w_out_a[l], w_out_b[l],
              w_out_c[l], w_o[l], ln1_g[l], ln1_b[l], peer_wq[l], peer_keys[l], peer_u[l],
              peer_v[l], ln2_g[l], ln2_b[l])
        xp, pa, pb, ph, ps = trunk_layer(
            xp,
            jnp.zeros((bp, CONV_A_W - 1, D_A), xp.dtype),
            jnp.zeros((bp, CONV_B_W - 1, D_B), xp.dtype),
            jnp.zeros((bp, D_B), xp.dtype),
            jnp.zeros((bp, C_HEADS, C_HD, C_HD), xp.dtype),
            lbs[l], *lp)
        xs, sa, sb, sh, ss = trunk_layer(
            xs, state_conv_a[l], state_conv_b[l], state_lru[l], state_hgrn[l], lbs[l], *lp)
        pa_l.append(pa); pb_l.append(pb); ph_l.append(ph); ps_l.append(ps)
        sa_l.append(sa); sb_l.append(sb); sh_l.append(sh); ss_l.append(ss)
    return (xp, xs,
            jnp.stack(pa_l), jnp.stack(pb_l), jnp.stack(ph_l), jnp.stack(ps_l),
            jnp.stack(sa_l), jnp.stack(sb_l), jnp.stack(sh_l), jnp.stack(ss_l))
```

```python
import numpy as np
from contextlib import ExitStack
import concourse.bass as bass
import concourse.mybir as mybir
from concourse.bass_utils import run_bass_kernel_spmd

F32 = mybir.dt.float32
BF16 = mybir.dt.bfloat16
U32 = mybir.dt.uint32
I32 = mybir.dt.int32
AF = mybir.ActivationFunctionType
ALU = mybir.AluOpType
AX = mybir.AxisListType

D = 1024
NCB = 8
DEPTH = 2
ALPHA = (2.0 * DEPTH) ** 0.25
LN_EPS = 1e-5
RMS_EPS = 1e-6
NEG = -1.0e30


class Tl:
    def __init__(self, t, name=""):
        self.t = t
        self.name = name
        self.last_w = None
        self.readers = []

    def __getitem__(self, idx):
        return self.t[idx]


class Alias:
    def __init__(self, base, ap):
        self.base = base
        self.t = ap

    def __getitem__(self, idx):
        return self.t[idx]

    last_w = property(lambda s: s.base.last_w, lambda s, v: setattr(s.base, "last_w", v))
    readers = property(lambda s: s.base.readers, lambda s, v: setattr(s.base, "readers", v))


class Op:
    __slots__ = ("eng", "fn", "deps", "signal", "sem", "semval", "dma", "idx")


class Sched:
    ENG = ("pe", "act", "dve", "pool", "sp")

    def __init__(self, nc, stack, n_dma_sems=16):
        self.nc = nc
        self.ops = []
        self.per_eng = {e: [] for e in self.ENG}
        self.esem = {e: stack.enter_context(nc.semaphore("es_" + e)) for e in self.ENG}
        self.dsems = {}
        self.dcnt = {}
        self.drr = {}
        for e in ("sp", "pool"):
            self.dsems[e] = [stack.enter_context(nc.semaphore("ds_%s%d" % (e, i))) for i in range(n_dma_sems)]
            self.dcnt[e] = [0] * n_dma_sems
            self.drr[e] = 0
        self.dlast = {}
        self.stack = stack

    def sb(self, name, shape, dt=F32):
        n = 1
        for d_ in shape[1:]:
            n *= d_
        self.sbytes = getattr(self, "sbytes", 0) + n * (2 if dt == BF16 else 4)
        t = self.stack.enter_context(self.nc.sbuf_tensor(name, list(shape), dt))
        return Tl(t, name)

    def ps(self, name, shape, dt=F32):
        t = self.stack.enter_context(self.nc.psum_tensor(name, list(shape), dt))
        return Tl(t, name)

    def op(self, eng, fn, reads=(), writes=(), dma=False):
        o = Op()
        o.eng = eng
        o.fn = fn
        o.dma = dma
        o.signal = False
        o.sem = None
        o.semval = None
        o.idx = len(self.ops)
        deps = []
        for r in reads:
            if r is not None and r.last_w is not None:
                deps.append(r.last_w)
        for w in writes:
            if w is None:
                continue
            if w.last_w is not None:
                deps.append(w.last_w)
            deps.extend(w.readers)
        if dma:
            k = self.drr[eng]
            self.drr[eng] = (k + 1) % len(self.dsems[eng])
            prev = self.dlast.get((eng, k))
            if prev is not None:
                deps.append(prev)
            self.dcnt[eng][k] += 16
            o.sem = self.dsems[eng][k]
            o.semval = self.dcnt[eng][k]
            self.dlast[(eng, k)] = o
            o.signal = True
        seen = set()
        dd = []
        for d in deps:
            if d is o or id(d) in seen:
                continue
            seen.add(id(d))
            if (not d.dma) and d.eng == "pe" and eng == "pe" and not dma:
                continue
            dd.append(d)
            if not d.dma:
                d.signal = True
        o.deps = dd
        for r in reads:
            if r is not None:
                r.readers.append(o)
        for w in writes:
            if w is not None:
                w.last_w = o
                w.readers = []
        self.ops.append(o)
        self.per_eng[eng].append(o)
        return o

    def finalize(self):
        for e in self.ENG:
            c = 0
            for o in self.per_eng[e]:
                if o.dma:
                    continue
                if o.signal:
                    c += 1
                    o.sem = self.esem[e]
                    o.semval = c

    def emit_engine(self, ename, eng):
        seen = {}
        for o in self.per_eng[ename]:
            need = {}
            for d in o.deps:
                key = id(d.sem)
                if key not in need or need[key][1] < d.semval:
                    need[key] = (d.sem, d.semval)
            for key, (sem, val) in need.items():
                if seen.get(key, 0) >= val:
                    continue
                eng.wait_ge(sem, val)
                seen[key] = val
            inst = o.fn(eng)
            if o.signal:
                inst.then_inc(o.sem, 16 if o.dma else 1)
        if ename in self.dsems:
            for k, s in enumerate(self.dsems[ename]):
                v = self.dcnt[ename][k]
                if v > 0 and seen.get(id(s), 0) < v:
                    eng.wait_ge(s, v)

    def run(self):
        self.finalize()
        S = self
        with self.nc.Block() as block:
            @block.tensor
            def _(e):
                S.emit_engine("pe", e)

            @block.scalar
            def _(e):
                S.emit_engine("act", e)

            @block.vector
            def _(e):
                S.emit_engine("dve", e)

            @block.gpsimd
            def _(e):
                S.emit_engine("pool", e)

            @block.sync
            def _(e):
                S.emit_engine("sp", e)


class Ring:
    def __init__(self, tiles):
        self.tiles = tiles
        self.i = 0

    def next(self):
        t = self.tiles[self.i]
        self.i = (self.i + 1) % len(self.tiles)
        return t


def build(T, NL=DEPTH, with_sample=True, NWR=5, NGR=3, dbg=None):
    import os
    DBGF = os.environ.get('KDBG', '')
    TB = 128
    nc = bass.Bass("TRN2", target_bir_lowering=False)

    def din(name, shape, dt=F32):
        return nc.dram_tensor(name, list(shape), dt, kind="ExternalInput").ap()

    def dout(name, shape, dt=F32):
        return nc.dram_tensor(name, list(shape), dt, kind="ExternalOutput").ap()

    xp = din("xp", [T, D])
    xsm = din("xs", [32, D])
    st_ca = din("st_ca", [DEPTH, 2, D])
    st_cb = din("st_cb", [DEPTH, 3, D])
    st_h = din("st_h", [DEPTH, D])
    st_S = din("st_S", [DEPTH, 8, 128, 128])
    w_in = din("w_in", [DEPTH, D, 12 * D])
    b_in = din("b_in", [DEPTH, 12 * D])
    conv_a_w = din("conv_a_w", [DEPTH, 3, D])
    conv_b_w = din("conv_b_w", [DEPTH, 4, D])
    conv_b_b = din("conv_b_b", [DEPTH, D])
    lru_wa = din("lru_wa", [DEPTH, 8, 128, 128])
    lru_ba = din("lru_ba", [DEPTH, D])
    lru_wx = din("lru_wx", [DEPTH, 8, 128, 128])
    lru_bx = din("lru_bx", [DEPTH, D])
    lru_lambda = din("lru_lambda", [DEPTH, D])
    lb_logits = din("hgrn_lb_logits", [DEPTH, D])
    norm_g = din("hgrn_norm_g", [DEPTH, 128])
    w_out = [din("w_out_a", [DEPTH, D, D]), din("w_out_b", [DEPTH, D, D]), din("w_out_c", [DEPTH, D, D])]
    w_o = din("w_o", [DEPTH, D, D])
    ln1_g = din("ln1_g", [DEPTH, D])
    ln1_b = din("ln1_b", [DEPTH, D])
    peer_wq = din("peer_wq", [DEPTH, D, 2048])
    peer_keys = din("peer_keys", [DEPTH, 8, 2, 128, 128])
    peer_u = din("peer_u", [DEPTH, 16384, D])
    peer_v = din("peer_v", [DEPTH, 16384, D])
    peer_u2 = peer_u.rearrange("l e d -> (l e) d")
    peer_v2 = peer_v.rearrange("l e d -> (l e) d")
    UT = nc.dram_tensor("UT_scr", [DEPTH, 128, 128, 1024], BF16, kind="Internal").ap()
    VB = nc.dram_tensor("VB_scr", [DEPTH, 128, 128, 1024], BF16, kind="Internal").ap()
    ln2_g = din("ln2_g", [DEPTH, D])
    ln2_b = din("ln2_b", [DEPTH, D])

    yp = dout("yp", [T, D])
    ys = dout("ys", [32, D])
    o_ca = [dout("p_ca", [DEPTH, 2, D]), dout("s_ca", [DEPTH, 2, D])]
    o_cb = [dout("p_cb", [DEPTH, 3, D]), dout("s_cb", [DEPTH, 3, D])]
    o_h = [dout("p_h", [DEPTH, D]), dout("s_h", [DEPTH, D])]
    o_S = [dout("p_S", [DEPTH, 8, 128, 128]), dout("s_S", [DEPTH, 8, 128, 128])]
    dbg_out = {}
    if dbg:
        for nm, shp in dbg.items():
            dbg_out[nm] = dout("dbg_" + nm, shp)

    with ExitStack() as st:
        S = Sched(nc, st)

        def DMA(out_ap, in_ap, reads=(), writes=(), slow=False):
            if slow:
                S.op("sp", lambda e: e.dma_start(out=out_ap, in_=in_ap, allow_slow_non_contiguous=True), reads=reads, writes=writes, dma=True)
            else:
                S.op("sp", lambda e: e.dma_start(out=out_ap, in_=in_ap), reads=reads, writes=writes, dma=True)

        def ACT(out_ap, in_ap, func, reads, writes, bias=None, scale=None):
            kw = {}
            if bias is not None:
                kw["bias"] = bias
            if scale is not None:
                kw["scale"] = scale
            S.op("act", lambda e: e.activation(out=out_ap, in_=in_ap, func=func, **kw), reads=reads, writes=writes)

        def TS(out_ap, in_ap, s1, s2, op0, op1, reads, writes, eng="dve"):
            if op1 is None:
                S.op(eng, lambda e: e.tensor_scalar(out=out_ap, in0=in_ap, scalar1=s1, scalar2=None, op0=op0), reads=reads, writes=writes)
            else:
                S.op(eng, lambda e: e.tensor_scalar(out=out_ap, in0=in_ap, scalar1=s1, scalar2=s2, op0=op0, op1=op1), reads=reads, writes=writes)

        def TT(out_ap, a_ap, b_ap, op, reads, writes, eng="dve"):
            S.op(eng, lambda e: e.tensor_tensor(out=out_ap, in0=a_ap, in1=b_ap, op=op), reads=reads, writes=writes)

        def STT(out_ap, in0, scalar, in1, op0, op1, reads, writes, accum=None, eng="dve"):
            if accum is None:
                S.op(eng, lambda e: e.scalar_tensor_tensor(out=out_ap, in0=in0, scalar=scalar, in1=in1, op0=op0, op1=op1), reads=reads, writes=writes)
            else:
                S.op(eng, lambda e: e.scalar_tensor_tensor(out=out_ap, in0=in0, scalar=scalar, in1=in1, op0=op0, op1=op1, accum_out=accum), reads=reads, writes=writes)

        def CP(out_ap, in_ap, reads, writes, eng="dve"):
            S.op(eng, lambda e: e.tensor_copy(out=out_ap, in_=in_ap), reads=reads, writes=writes)

        def MSET(ap, val, writes, eng="pool"):
            S.op(eng, lambda e: e.memset(ap, val), writes=writes)

        def MM(out_ap, lhsT, rhs, start, stop, reads, writes):
            S.op("pe", lambda e: e.matmul(out_ap, lhsT=lhsT, rhs=rhs, start=start, stop=stop), reads=reads, writes=writes)

        def TR(out_ap, in_ap, k, reads, writes):
            S.op("pe", lambda e: e.transpose(out_ap, in_ap, ident[0:k, 0:k]), reads=list(reads) + [ident], writes=writes)

        ident = S.sb("ident", [128, 128])
        ones = S.sb("ones", [128, 128])
        tri = S.sb("tri", [64, 64])
        cmask = {64: S.sb("cmask64", [128, TB]), 32: S.sb("cmask32", [128, TB])}
        iota16 = S.sb("iota16", [128, 16])
        iota_i = S.sb("iota_i", [128, 16], I32)
        psr = Ring([S.ps("ps%d" % i, [128, 512]) for i in range(6)])
        psB = [S.ps("psB%d" % i, [128, 512]) for i in range(2)]
        wring = Ring([S.sb("wr%d" % i, [128, 1024]) for i in range(NWR)])
        zr = Ring([S.sb("z%d" % i, [128, TB + 4]) for i in range(26)])
        xa = S.sb("xa", [128, D])
        xb = S.sb("xb", [128, D])
        x1 = S.sb("x1", [128, D])
        rbuf = S.sb("rbuf", [128, D])
        xFM = S.sb("xFM", [128, 8, TB])
        x1FM = S.sb("x1FM", [128, 8, TB])
        aABC = S.sb("aABC", [128, 8, 3, TB])
        aV = [[Tl(aABC.t) for _ in range(3)] for _ in range(8)]
        merged = S.sb("merged", [128, 8, TB])
        s2 = S.sb("s2", [128, 128])
        C_all = S.sb("C_all", [128, 128 * 128], BF16)
        if 'noalias' in DBGF:
            pscr = S.sb("pscr", [128, 2048]); cand = S.sb("cand", [128, 2048]); qT = S.sb("qT", [128, 16, TB])
        else:
            pscr = Alias(C_all, C_all[:, 0:4096].bitcast(F32))
            cand = Alias(C_all, C_all[:, 4096:8192].bitcast(F32))
            qT = Alias(C_all, C_all[:, 8192:12288].bitcast(F32).rearrange("p (a b) -> p a b", b=128))
        x1FMb = S.sb("x1FMb", [128, 8, TB], BF16)
        iota128 = S.sb("iota128", [128, 128])
        iota128i = S.sb("iota128i", [128, 128], I32)
        slotT = S.sb("slotT", [128, 3, 128])
        OIr = Ring([S.sb("oi%d" % i, [128, 128], BF16) for i in range(4)])
        OJr = Ring([S.sb("oj%d" % i, [128, 128], BF16) for i in range(4)])
        UTr = Ring([S.sb("utr%d" % i, [128, 1024], BF16) for i in range(3)])
        VBr = Ring([S.sb("vbr%d" % i, [128, 1024], BF16) for i in range(3)])
        Gr = Ring([S.sb("gg%d" % i, [128, 128]) for i in range(3)])
        Wr = Ring([S.sb("wt%d" % i, [128, 128], BF16) for i in range(3)])
        UTbuf = [[Tl(None) for _ in range(128)] for _ in range(NL)]
        VBbuf = [[Tl(None) for _ in range(128)] for _ in range(NL)]
        lnp = [S.sb("lnp%d" % i, [128, D]) for i in range(4)]
        keysT = [S.sb("keysT%d" % l, [128, 16, 128]) for l in range(NL)]
        PA = [S.sb("PA%d" % l, [128, 96]) for l in range(NL)]
        PB = [S.sb("PB%d" % l, [128, 128]) for l in range(NL)]
        PD = [S.sb("PD%d" % l, [128, 4, 8]) for l in range(NL)]
        stg = S.sb("stg", [128, 128])
        Sst = [S.sb("Sst%d" % l, [128, 8, 128]) for l in range(NL)]
        SstV = [[Tl(Sst[l].t) for _ in range(8)] for l in range(NL)]
        cAst = [S.sb("cAst%d" % l, [128, 8, 2]) for l in range(NL)]
        cAV = [[Tl(cAst[l].t) for _ in range(8)] for l in range(NL)]
        cBst = [S.sb("cBst%d" % l, [128, 8, 3]) for l in range(NL)]
        cBV = [[Tl(cBst[l].t) for _ in range(8)] for l in range(NL)]
        hst = [S.sb("hst%d" % l, [128, 8]) for l in range(NL)]
        hV = [[Tl(hst[l].t) for _ in range(8)] for l in range(NL)]
        small = S.sb("small", [128, 64])
        sv = S.sb("sv", [128, 16, 16])
        si = S.sb("si", [128, 16, 16], U32)
        sif = S.sb("sif", [128, 16, 16])
        tops = S.sb("tops", [128, 8, 16])
        topp = S.sb("topp", [128, 8, 16], U32)
        pij = S.sb("pij", [128, 2, 8, 16], U32)
        pijf = S.sb("pijf", [128, 2, 8, 16])
        esel = S.sb("esel", [128, 2, 8, 16])
        eidf = S.sb("eidf", [128, 128])
        eid = S.sb("eid", [128, 128], U32)
        gate = S.sb("gate", [128, 8, 16])
        mvt = S.sb("mvt", [128, 16])
        bst = S.sb("bst", [128, 12])

        print('SBUF bytes/partition', S.sbytes, flush=True)
        MSET(ident[:], 0.0, [ident])
        S.op("pool", lambda e: e.affine_select(out=ident[:], in_=ident[:], pattern=[[-1, 128]], compare_op=ALU.not_equal, fill=1.0, base=0, channel_multiplier=1), reads=[ident], writes=[ident])
        MSET(ones[:], 1.0, [ones])
        MSET(tri[:], 1.0, [tri])
        S.op("pool", lambda e: e.affine_select(out=tri[:], in_=tri[:], pattern=[[1, 64]], compare_op=ALU.is_ge, fill=0.0, base=0, channel_multiplier=-1), reads=[tri], writes=[tri])
        for L in (64, 32):
            MSET(cmask[L][:], 1.0, [cmask[L]])
            for c in range(TB // L):
                MSET(cmask[L][:, c * L:c * L + 1], 0.0, [cmask[L]])
        S.op("pool", lambda e: e.iota(iota_i[:], pattern=[[1, 16]], base=0, channel_multiplier=0), writes=[iota_i])
        CP(iota16[:], iota_i[:], [iota_i], [iota16])
        S.op("pool", lambda e: e.iota(iota128i[:], pattern=[[1, 128]], base=0, channel_multiplier=0), writes=[iota128i])
        CP(iota128[:], iota128i[:], [iota128i], [iota128])

        for l in range(NL):
            MSET(stg[:], 0.0, [stg])
            DMA(stg[0:96, :], b_in[l].rearrange("(r p) -> r p", p=128), writes=[stg])
            p = psr.next()
            TR(p[:, 0:128], stg[:], 128, [stg], [p])
            CP(PA[l][:], p[:, 0:96], [p], [PA[l]])
            MSET(stg[:], 0.0, [stg])
            DMA(stg[0:24, :], conv_a_w[l].rearrange("k (c p) -> (k c) p", p=128), writes=[stg])
            DMA(stg[24:56, :], conv_b_w[l].rearrange("k (c p) -> (k c) p", p=128), writes=[stg])
            DMA(stg[56:64, :], conv_b_b[l].rearrange("(c p) -> c p", p=128), writes=[stg])
            DMA(stg[64:72, :], lru_ba[l].rearrange("(c p) -> c p", p=128), writes=[stg])
            DMA(stg[72:80, :], lru_bx[l].rearrange("(c p) -> c p", p=128), writes=[stg])
            DMA(stg[80:88, :], lru_lambda[l].rearrange("(c p) -> c p", p=128), writes=[stg])
            DMA(stg[88:96, :], lb_logits[0].rearrange("(c p) -> c p", p=128), writes=[stg])
            DMA(stg[96:104, :], lb_logits[1].rearrange("(c p) -> c p", p=128), writes=[stg])
            DMA(stg[104:105, :], norm_g[l:l + 1, :], writes=[stg])
            p = psr.next()
            TR(p[:, 0:128], stg[:], 128, [stg], [p])
            CP(PB[l][:], p[:, 0:128], [p], [PB[l]])
            ACT(PD[l][:, 0, :], PB[l][:, 80:88], AF.Exp, [PB[l]], [PD[l]], scale=-1.0)
            ACT(PD[l][:, 0, :], PD[l][:, 0, :], AF.Ln, [PD[l]], [PD[l]], bias=1.0)
            TS(PD[l][:, 1, :], PD[l][:, 0, :], -16.0, None, ALU.mult, None, [PD[l]], [PD[l]])
            TS(PD[l][:, 0, :], PD[l][:, 0, :], -8.0, None, ALU.mult, None, [PD[l]], [PD[l]])
            if l == 0:
                MSET(PD[l][:, 2, :], 0.0, [PD[l]], eng="dve")
                MSET(PD[l][:, 3, :], 1.0, [PD[l]], eng="dve")
            else:
                TT(PD[l][:, 2, :], PB[l][:, 96:104], PB[l][:, 88:96], ALU.subtract, [PB[l]], [PD[l]])
                ACT(PD[l][:, 2, :], PD[l][:, 2, :], AF.Sigmoid, [PD[l]], [PD[l]])
                TS(PD[l][:, 3, :], PD[l][:, 2, :], -1.0, 1.0, ALU.mult, ALU.add, [PD[l]], [PD[l]])
            for hp in range(16):
                w = wring.next()
                DMA(w[:, 0:128], peer_keys[l, hp // 2, hp % 2], writes=[w])
                p = psr.next()
                TR(p[:, 0:128], w[:, 0:128], 128, [w], [p])
                CP(keysT[l][:, hp, :], p[:, 0:128], [p], [keysT[l]], eng="dve")

        import os
        DBGF = os.environ.get('KDBG', '')
        for l in range(NL if 'noprep' not in DBGF else 0):
            ug = peer_u[l].rearrange("(i j) d -> j i d", j=128)
            vg = peer_v[l].rearrange("(i j) d -> j i d", j=128)
            for j in range(128):
                g1 = wring.next()
                DMA(g1[:], ug[j], writes=[g1])
                ut = UTr.next()
                for c4 in range(2):
                    p = psr.next()
                    for k in range(4):
                        c = c4 * 4 + k
                        TR(p[:, k * 128:(k + 1) * 128], g1[:, c * 128:(c + 1) * 128], 128, [g1], [p])
                    S.op("act", (lambda e, ut=ut, p=p, c4=c4: e.copy(out=ut[:, c4 * 512:(c4 + 1) * 512], in_=p[:, :])), reads=[p], writes=[ut])
                DMA(UT[l, j], ut[:], reads=[ut], writes=[UTbuf[l][j]])
                g2 = wring.next()
                DMA(g2[:], vg[j], writes=[g2])
                vb = VBr.next()
                CP(vb[:], g2[:], [g2], [vb], eng="pool")
                DMA(VB[l, j], vb[:], reads=[vb], writes=[VBbuf[l][j]])

        def init_states_zero():
            for l in range(NL):
                MSET(Sst[l][:], 0.0, SstV[l])
                MSET(cAst[l][:], 0.0, cAV[l])
                MSET(cBst[l][:], 0.0, cBV[l])
                MSET(hst[l][:], 0.0, hV[l])

        def init_states_sample():
            for l in range(NL):
                DMA(Sst[l][:], st_S[l].rearrange("h d e -> d h e"), writes=SstV[l])
                for cb in range(8):
                    DMA(cAst[l][:, cb, :], st_ca[l][:, cb * 128:(cb + 1) * 128].rearrange("k p -> p k"), writes=[cAV[l][cb]], slow=True)
                    DMA(cBst[l][:, cb, :], st_cb[l][:, cb * 128:(cb + 1) * 128].rearrange("k p -> p k"), writes=[cBV[l][cb]], slow=True)
                DMA(hst[l][:], st_h[l].rearrange("(c p) -> p c", p=128), writes=hV[l], slow=True)

        def out_states(which):
            for l in range(NL):
                DMA(o_S[which][l].rearrange("h d e -> d h e"), Sst[l][:], reads=SstV[l])
                for cb in range(8):
                    DMA(o_ca[which][l][:, cb * 128:(cb + 1) * 128].rearrange("k p -> p k"), cAst[l][:, cb, :], reads=[cAV[l][cb]], slow=True)
                    DMA(o_cb[which][l][:, cb * 128:(cb + 1) * 128].rearrange("k p -> p k"), cBst[l][:, cb, :], reads=[cBV[l][cb]], slow=True)
                DMA(o_h[which][l].rearrange("(c p) -> p c", p=128), hst[l][:], reads=hV[l], slow=True)

        def wl_cols(src2d, c0):
            w = wring.next()
            DMA(w[:].rearrange("p (c n) -> p c n", n=128), src2d[:, c0:c0 + 128].rearrange("(c p) n -> p c n", p=128), writes=[w])
            return w

        def in_proj(l, col0, xf, nt):
            w = wl_cols(w_in[l], col0)
            p = psr.next()
            for c in range(8):
                MM(p[:, 0:nt], w[:, c * 128:(c + 1) * 128], xf[:, c, 0:nt], c == 0, c == 7, [w, xf], [p])
            return p

        def layer_block(l, xin, xout, nt, L, tagdbg=None):
            TP = nt
            nch = nt // L
            for i, src in enumerate((ln1_g, ln1_b, ln2_g, ln2_b)):
                DMA(lnp[i][:], src[l:l + 1, :].to_broadcast([128, D]), writes=[lnp[i]])
            for c4 in range(2):
                p = psr.next()
                for k in range(4):
                    c = c4 * 4 + k
                    TR(p[:, k * 128:k * 128 + TP], xin[0:TP, c * 128:(c + 1) * 128], TP, [xin], [p])
                for k in range(4):
                    c = c4 * 4 + k
                    S.op("act", (lambda e, c=c, k=k, p=p: e.copy(out=xFM[:, c, 0:TP], in_=p[:, k * 128:k * 128 + TP])), reads=[p], writes=[xFM])
            for cb in range(8):
                def zin(G, func):
                    p = in_proj(l, G * D + cb * 128, xFM, nt)
                    z = zr.next()
                    ACT(z[:, 0:nt], p[:, 0:nt], func, [p, PA[l]], [z], bias=PA[l][:, G * 8 + cb:G * 8 + cb + 1])
                    return z
                zB = zin(0, AF.Identity)
                zC = zin(1, AF.Identity)
                zxA = zin(2, AF.Identity)
                ub = zr.next()
                CP(ub[:, 0:2], cAst[l][:, cb, :], [cAV[l][cb]], [ub], eng="pool")
                TT(ub[:, 2:2 + nt], zC[:, 0:nt], zxA[:, 0:nt], ALU.mult, [zC, zxA], [ub])
                CP(cAst[l][:, cb, :], ub[:, nt:nt + 2], [ub], [cAV[l][cb]], eng="pool")
                y = zr.next()
                TS(y[:, 0:nt], ub[:, 0:nt], PB[l][:, cb:cb + 1], None, ALU.mult, None, [ub, PB[l]], [y])
                STT(y[:, 0:nt], ub[:, 1:1 + nt], PB[l][:, 8 + cb:9 + cb], y[:, 0:nt], ALU.mult, ALU.add, [ub, y, PB[l]], [y])
                STT(y[:, 0:nt], ub[:, 2:2 + nt], PB[l][:, 16 + cb:17 + cb], y[:, 0:nt], ALU.mult, ALU.add, [ub, y, PB[l]], [y])
                TT(aABC[:, cb, 0, 0:nt], zB[:, 0:nt], y[:, 0:nt], ALU.mult, [zB, y], [aV[cb][0]])
                zxB = zin(3, AF.Identity)
                ggB = zin(4, AF.Gelu)
                cbuf = zr.next()
                CP(cbuf[:, 0:3], cBst[l][:, cb, :], [cBV[l][cb]], [cbuf], eng="pool")
                CP(cbuf[:, 3:3 + nt], zxB[:, 0:nt], [zxB], [cbuf], eng="pool")
                CP(cBst[l][:, cb, :], cbuf[:, nt:nt + 3], [cbuf], [cBV[l][cb]], eng="pool")
                xl = zr.next()
                TS(xl[:, 0:nt], cbuf[:, 0:nt], PB[l][:, 24 + cb:25 + cb], PB[l][:, 56 + cb:57 + cb], ALU.mult, ALU.add, [cbuf, PB[l]], [xl])
                for k in range(1, 4):
                    STT(xl[:, 0:nt], cbuf[:, k:k + nt], PB[l][:, 24 + 8 * k + cb:25 + 8 * k + cb], xl[:, 0:nt], ALU.mult, ALU.add, [cbuf, xl, PB[l]], [xl])
                w = wring.next()
                DMA(w[:, 0:128], lru_wa[l, cb], writes=[w])
                DMA(w[:, 128:256], lru_wx[l, cb], writes=[w])
                p = psr.next()
                MM(p[:, 0:nt], w[:, 0:128], xl[:, 0:nt], True, True, [w, xl], [p])
                MM(p[:, 128:128 + nt], w[:, 128:256], xl[:, 0:nt], True, True, [w, xl], [p])
                r = zr.next()
                gi = zr.next()
                ACT(r[:, 0:nt], p[:, 0:nt], AF.Sigmoid, [p, PB[l]], [r], bias=PB[l][:, 64 + cb:65 + cb])
                ACT(gi[:, 0:nt], p[:, 128:128 + nt], AF.Sigmoid, [p, PB[l]], [gi], bias=PB[l][:, 72 + cb:73 + cb])
                a = zr.next()
                a2 = zr.next()
                ACT(a[:, 0:nt], r[:, 0:nt], AF.Exp, [r, PD[l]], [a], scale=PD[l][:, 0, cb:cb + 1])
                ACT(a2[:, 0:nt], r[:, 0:nt], AF.Exp, [r, PD[l]], [a2], scale=PD[l][:, 1, cb:cb + 1])
                TS(a2[:, 0:nt], a2[:, 0:nt], -1.0, 1.0, ALU.mult, ALU.add, [a2], [a2])
                ACT(a2[:, 0:nt], a2[:, 0:nt], AF.Sqrt, [a2], [a2])
                TT(gi[:, 0:nt], gi[:, 0:nt], a2[:, 0:nt], ALU.mult, [gi, a2], [gi])
                TT(gi[:, 0:nt], gi[:, 0:nt], xl[:, 0:nt], ALU.mult, [gi, xl], [gi])
                hh = zr.next()
                S.op("dve", (lambda e, hh=hh, a=a, gi=gi, cb=cb: e.tensor_tensor_scan(out=hh[:, 0:nt], data0=a[:, 0:nt], data1=gi[:, 0:nt], initial=hst[l][:, cb:cb + 1], op0=ALU.mult, op1=ALU.add)), reads=[a, gi, hV[l][cb]], writes=[hh])
                CP(hst[l][:, cb:cb + 1], hh[:, nt - 1:nt], [hh], [hV[l][cb]], eng="pool")
                TT(aABC[:, cb, 1, 0:nt], ggB[:, 0:nt], hh[:, 0:nt], ALU.mult, [ggB, hh], [aV[cb][1]])
                q = zin(5, AF.Silu)
                f = zin(6, AF.Sigmoid)
                vv = zin(7, AF.Identity)
                sg = zin(8, AF.Silu)
                TS(f[:, 0:nt], f[:, 0:nt], PD[l][:, 3, cb:cb + 1], PD[l][:, 2, cb:cb + 1], ALU.mult, ALU.add, [f, PD[l]], [f])
                kk = zr.next()
                TS(kk[:, 0:nt], f[:, 0:nt], -1.0, 1.0, ALU.mult, ALU.add, [f], [kk])
                lf = zr.next()
                ACT(lf[:, 0:nt], f[:, 0:nt], AF.Ln, [f], [lf])
                bb = zr.next()
                S.op("dve", (lambda e, bb=bb, lf=lf: e.tensor_tensor_scan(out=bb[:, 0:nt], data0=cmask[L][:, 0:nt], data1=lf[:, 0:nt], initial=0.0, op0=ALU.mult, op1=ALU.add)), reads=[lf, cmask[L]], writes=[bb])
                eb = zr.next()
                ACT(eb[:, 0:nt], bb[:, 0:nt], AF.Exp, [bb], [eb])
                ACT(bb[:, 0:nt], bb[:, 0:nt], AF.Exp, [bb], [bb], scale=-1.0)
                TT(q[:, 0:nt], q[:, 0:nt], eb[:, 0:nt], ALU.mult, [q, eb], [q])
                TT(kk[:, 0:nt], kk[:, 0:nt], bb[:, 0:nt], ALU.mult, [kk, bb], [kk])
                osb = zr.next()
                Sv = SstV[l][cb]
                for c in range(nch):
                    c0 = c * L
                    pa = psr.next()
                    MM(pa[0:L, 0:L], kk[:, c0:c0 + L], q[:, c0:c0 + L], True, True, [kk, q], [pa])
                    att = zr.next()
                    TT(att[0:L, 0:L], pa[0:L, 0:L], tri[0:L, 0:L], ALU.mult, [pa, tri], [att])
                    pv = psr.next()
                    TR(pv[0:L, 0:128], vv[:, c0:c0 + L], 128, [vv], [pv])
                    vtm = zr.next()
                    S.op("act", (lambda e, vtm=vtm, pv=pv: e.copy(out=vtm[0:L, 0:128], in_=pv[0:L, 0:128])), reads=[pv], writes=[vtm])
                    po = psr.next()
                    MM(po[:, 0:L], Sst[l][:, cb, :], q[:, c0:c0 + L], True, False, [Sv, q], [po])
                    MM(po[:, 0:L], vtm[0:L, 0:128], att[0:L, 0:L], False, True, [vtm, att], [po])
                    S.op("act", (lambda e, osb=osb, po=po, c0=c0: e.copy(out=osb[:, c0:c0 + L], in_=po[:, 0:L])), reads=[po], writes=[osb])
                    k2 = zr.next()
                    TS(k2[:, 0:L], kk[:, c0:c0 + L], eb[:, c0 + L - 1:c0 + L], None, ALU.mult, None, [kk, eb], [k2])
                    pk = psr.next()
                    TR(pk[0:L, 0:128], k2[:, 0:L], 128, [k2], [pk])
                    k2t = zr.next()
                    CP(k2t[0:L, 0:128], pk[0:L, 0:128], [pk], [k2t])
                    pn = psr.next()
                    MM(pn[:, 0:128], k2t[0:L, 0:128], vtm[0:L, 0:128], True, True, [k2t, vtm], [pn])
                    STT(Sst[l][:, cb, :], Sst[l][:, cb, :], eb[:, c0 + L - 1:c0 + L], pn[:, 0:128], ALU.mult, ALU.add, [Sv, eb, pn], [Sv])
                sq = zr.next()
                ACT(sq[:, 0:nt], osb[:, 0:nt], AF.Square, [osb], [sq])
                pss = psr.next()
                MM(pss[:, 0:nt], ones[:], sq[:, 0:nt], True, True, [ones, sq], [pss])
                TS(sq[:, 0:nt], pss[:, 0:nt], 1.0 / 128.0, RMS_EPS, ALU.mult, ALU.add, [pss], [sq])
                ACT(sq[:, 0:nt], sq[:, 0:nt], AF.Sqrt, [sq], [sq])
                S.op("dve", (lambda e, sq=sq: e.reciprocal(out=sq[:, 0:nt], in_=sq[:, 0:nt])), reads=[sq], writes=[sq])
                TT(osb[:, 0:nt], osb[:, 0:nt], sq[:, 0:nt], ALU.mult, [osb, sq], [osb])
                STT(aABC[:, cb, 2, 0:nt], osb[:, 0:nt], PB[l][:, 104:105], sg[:, 0:nt], ALU.mult, ALU.mult, [osb, sg, PB[l]], [aV[cb][2]])
            for j in range(8):
                pys = []
                for X in range(3):
                    w = wl_cols(w_out[X][l], j * 128)
                    p = psr.next()
                    for cb in range(8):
                        MM(p[:, 0:nt], w[:, cb * 128:(cb + 1) * 128], aABC[:, cb, X, 0:nt], cb == 0, cb == 7, [w, aV[cb][X]], [p])
                    pys.append(p)
                sgs = []
                for X in range(3):
                    G = 9 + X
                    p = in_proj(l, G * D + j * 128, xFM, nt)
                    z = zr.next()
                    ACT(z[:, 0:nt], p[:, 0:nt], AF.Sigmoid, [p, PA[l]], [z], bias=PA[l][:, G * 8 + j:G * 8 + j + 1])
                    sgs.append(z)
                TT(merged[:, j, 0:nt], sgs[0][:, 0:nt], pys[0][:, 0:nt], ALU.mult, [sgs[0], pys[0]], [merged])
                for X in (1, 2):
                    TT(sgs[X][:, 0:nt], sgs[X][:, 0:nt], pys[X][:, 0:nt], ALU.mult, [sgs[X], pys[X]], [sgs[X]])
                    TT(merged[:, j, 0:nt], merged[:, j, 0:nt], sgs[X][:, 0:nt], ALU.add, [merged, sgs[X]], [merged])
            pw = psB
            for j in range(8):
                w = wring.next()
                DMA(w[:], w_o[l, j * 128:(j + 1) * 128, :], writes=[w])
                for nh in range(2):
                    MM(pw[nh][0:TP, :], merged[:, j, 0:TP], w[:, nh * 512:(nh + 1) * 512], j == 0, j == 7, [merged, w], [pw[nh]])

            def resid_ln(src, pws, gi_, bi_, dst):
                for nh in range(2):
                    STT(rbuf[0:TP, nh * 512:(nh + 1) * 512], src[0:TP, nh * 512:(nh + 1) * 512], ALPHA, pws[nh][0:TP, :], ALU.mult, ALU.add, [src, pws[nh]], [rbuf])
                ln_from_rbuf(gi_, bi_, dst)

            def ln_from_rbuf(gi_, bi_, dst):
                for nh in range(2):
                    S.op("dve", (lambda e, nh=nh: e.bn_stats(out=bst[0:TP, nh * 6:(nh + 1) * 6], in_=rbuf[0:TP, nh * 512:(nh + 1) * 512])), reads=[rbuf], writes=[bst])
                S.op("dve", lambda e: e.bn_aggr(out=mvt[0:TP, 0:2], in_=bst[0:TP, 0:12]), reads=[bst], writes=[mvt])
                TS(mvt[0:TP, 2:3], mvt[0:TP, 1:2], 1.0, LN_EPS, ALU.mult, ALU.add, [mvt], [mvt])
                ACT(mvt[0:TP, 2:3], mvt[0:TP, 2:3], AF.Sqrt, [mvt], [mvt])
                S.op("dve", lambda e: e.reciprocal(out=mvt[0:TP, 3:4], in_=mvt[0:TP, 2:3]), reads=[mvt], writes=[mvt])
                TS(rbuf[0:TP, :], rbuf[0:TP, :], mvt[0:TP, 0:1], mvt[0:TP, 3:4], ALU.subtract, ALU.mult, [rbuf, mvt], [rbuf])
                TT(rbuf[0:TP, :], rbuf[0:TP, :], lnp[gi_][0:TP, :], ALU.mult, [rbuf, lnp[gi_]], [rbuf])
                TT(dst[0:TP, :], rbuf[0:TP, :], lnp[bi_][0:TP, :], ALU.add, [rbuf, lnp[bi_]], [dst])

            resid_ln(xin, pw, 0, 1, x1)
            if tagdbg is not None and ("x1_%d" % l) in dbg_out:
                DMA(dbg_out["x1_%d" % l][tagdbg:tagdbg + TP, :], x1[0:TP, :], reads=[x1])

            for c4 in range(2):
                p = psr.next()
                for k in range(4):
                    c = c4 * 4 + k
                    TR(p[:, k * 128:k * 128 + TP], x1[0:TP, c * 128:(c + 1) * 128], TP, [x1], [p])
                for k in range(4):
                    c = c4 * 4 + k
                    S.op("act", (lambda e, c=c, k=k, p=p: e.copy(out=x1FM[:, c, 0:TP], in_=p[:, k * 128:k * 128 + TP])), reads=[p], writes=[x1FM])
            CP(x1FMb[:, :, 0:TP], x1FM[:, :, 0:TP], [x1FM], [x1FMb], eng="pool")
            for hp in range(16):
                w = wl_cols(peer_wq[l], hp * 128)
                p = psr.next()
                for c in range(8):
                    MM(p[:, 0:nt], w[:, c * 128:(c + 1) * 128], x1FM[:, c, 0:nt], c == 0, c == 7, [w, x1FM], [p])
                S.op("act", (lambda e, hp=hp, p=p: e.copy(out=qT[:, hp, 0:nt], in_=p[:, 0:nt])), reads=[p], writes=[qT])
            for g4 in range(4):
                p = psr.next()
                for k in range(4):
                    hp = g4 * 4 + k
                    MM(p[0:TP, k * 128:(k + 1) * 128], qT[:, hp, 0:TP], keysT[l][:, hp, :], True, True, [qT, keysT[l]], [p])
                CP(pscr[0:TP, g4 * 512:(g4 + 1) * 512], p[0:TP, :], [p], [pscr])
            for hp in range(16):
                sl = pscr[0:TP, hp * 128:(hp + 1) * 128]
                S.op("dve", (lambda e, hp=hp, sl=sl: e.max(out=sv[0:TP, hp, 0:8], in_=sl)), reads=[pscr], writes=[sv])
                S.op("dve", (lambda e, hp=hp, sl=sl: e.max_index(out=si[0:TP, hp, 0:8], in_max=sv[0:TP, hp, 0:8], in_values=sl)), reads=[pscr, sv], writes=[si])
                S.op("dve", (lambda e, hp=hp, sl=sl: e.match_replace(out=s2[0:TP, :], in_to_replace=sv[0:TP, hp, 0:8], in_values=sl, imm_value=NEG)), reads=[pscr, sv], writes=[s2])
                S.op("dve", (lambda e, hp=hp: e.max(out=sv[0:TP, hp, 8:16], in_=s2[0:TP, :])), reads=[s2], writes=[sv])
                S.op("dve", (lambda e, hp=hp: e.max_index(out=si[0:TP, hp, 8:16], in_max=sv[0:TP, hp, 8:16], in_values=s2[0:TP, :])), reads=[s2, sv], writes=[si])
            CP(sif[0:TP], si[0:TP], [si], [sif])
            svv = sv[0:TP].rearrange("n (h p) k -> n h p k", p=2)
            cv = cand[0:TP, :].rearrange("n (h i j) -> n h i j", h=8, i=16)
            TT(cv, svv[:, :, 0, :].unsqueeze(3).to_broadcast([TP, 8, 16, 16]), svv[:, :, 1, :].unsqueeze(2).to_broadcast([TP, 8, 16, 16]), ALU.add, [sv], [cand])
            for h in range(8):
                sl = cand[0:TP, h * 256:(h + 1) * 256]
                sl2 = pscr[0:TP, h * 256:(h + 1) * 256]
                S.op("dve", (lambda e, h=h, sl=sl: e.max(out=tops[0:TP, h, 0:8], in_=sl)), reads=[cand], writes=[tops])
                S.op("dve", (lambda e, h=h, sl=sl: e.max_index(out=topp[0:TP, h, 0:8], in_max=tops[0:TP, h, 0:8], in_values=sl)), reads=[cand, tops], writes=[topp])
                S.op("dve", (lambda e, h=h, sl=sl, sl2=sl2: e.match_replace(out=sl2, in_to_replace=tops[0:TP, h, 0:8], in_values=sl, imm_value=NEG)), reads=[cand, tops], writes=[pscr])
                S.op("dve", (lambda e, h=h, sl2=sl2: e.max(out=tops[0:TP, h, 8:16], in_=sl2)), reads=[pscr], writes=[tops])
                S.op("dve", (lambda e, h=h, sl2=sl2: e.max_index(out=topp[0:TP, h, 8:16], in_max=tops[0:TP, h, 8:16], in_values=sl2)), reads=[pscr, tops], writes=[topp])
            S.op("dve", lambda e: e.tensor_single_scalar(out=pij[0:TP, 0], in_=topp[0:TP], scalar=4, op=ALU.logical_shift_right), reads=[topp], writes=[pij])
            S.op("dve", lambda e: e.tensor_single_scalar(out=pij[0:TP, 1], in_=topp[0:TP], scalar=15, op=ALU.bitwise_and), reads=[topp], writes=[pij])
            CP(pijf[0:TP], pij[0:TP], [pij], [pijf])
            sifv = sif[0:TP].rearrange("n (h p) k -> n h p k", p=2)
            oh = qT[0:TP, :, :].rearrange("n a b -> n (a b)")[:, 0:2048].rearrange("n (h k m) -> n h k m", h=8, k=16)
            for pp in range(2):
                TT(oh, iota16[0:TP, :].unsqueeze(1).unsqueeze(1).to_broadcast([TP, 8, 16, 16]), pijf[0:TP, pp].unsqueeze(3).to_broadcast([TP, 8, 16, 16]), ALU.is_equal, [iota16, pijf], [qT])
                TT(oh, oh, sifv[:, :, pp, :].unsqueeze(2).to_broadcast([TP, 8, 16, 16]), ALU.mult, [qT, sif], [qT])
                S.op("dve", (lambda e, pp=pp: e.tensor_reduce(out=esel[0:TP, pp], in_=oh, axis=AX.X, op=ALU.add)), reads=[qT], writes=[esel])
            TT(gate[0:TP], tops[0:TP], tops[0:TP, :, 0:1].to_broadcast([TP, 8, 16]), ALU.subtract, [tops], [gate])
            ACT(gate[0:TP], gate[0:TP], AF.Exp, [gate], [gate])
            S.op("dve", lambda e: e.tensor_reduce(out=small[0:TP, 0:8], in_=gate[0:TP], axis=AX.X, op=ALU.add), reads=[gate], writes=[small])
            S.op("dve", lambda e: e.reciprocal(out=small[0:TP, 8:16], in_=small[0:TP, 0:8]), reads=[small], writes=[small])
            TT(gate[0:TP], gate[0:TP], small[0:TP, 8:16].unsqueeze(2).to_broadcast([TP, 8, 16]), ALU.mult, [gate, small], [gate])
            p = psr.next()
            srcs = (esel[0:TP, 0].rearrange("n h k -> n (h k)"), esel[0:TP, 1].rearrange("n h k -> n (h k)"), gate[0:TP].rearrange("n h k -> n (h k)"))
            for k in range(3 if 'noslot' not in DBGF else 0):
                TR(p[:, k * 128:k * 128 + TP], srcs[k], TP, [esel, gate], [p])
            for k in range(3 if 'noslot' not in DBGF else 0):
                CP(slotT[:, k, 0:TP], p[:, k * 128:k * 128 + TP], [p], [slotT])
            pc = None
            for n in range(TP if 'nob' not in DBGF else 0):
                oi = OIr.next()
                oj = OJr.next()
                TS(oi[:, :], iota128[:, :], slotT[:, 0, n:n + 1], None, ALU.is_equal, None, [iota128, slotT], [oi])
                TS(oj[:, :], iota128[:, :], slotT[:, 1, n:n + 1], slotT[:, 2, n:n + 1], ALU.is_equal, ALU.mult, [iota128, slotT], [oj])
                if n % 4 == 0:
                    pc = psr.next()
                MM(pc[:, (n % 4) * 128:(n % 4 + 1) * 128], oi[:, :], oj[:, :], True, True, [oi, oj], [pc])
                if n % 4 == 3:
                    n0 = n - 3
                    S.op("act", (lambda e, pc=pc, n0=n0: e.copy(out=C_all[:, n0 * 128:(n0 + 4) * 128], in_=pc[:, :])), reads=[pc], writes=[C_all])
            Cv = C_all[:, :].rearrange("p (n j) -> p n j", j=128)
            for j in range(128 if 'noc' not in DBGF else 0):
                ut = UTr.next()
                DMA(ut[:], UT[l, j], reads=[UTbuf[l][j]], writes=[ut])
                vb = VBr.next()
                DMA(vb[:], VB[l, j], reads=[VBbuf[l][j]], writes=[vb])
                ph = psr.next()
                for c in range(8):
                    MM(ph[:, 0:TP], ut[:, c * 128:(c + 1) * 128], x1FMb[:, c, 0:TP], c == 0, c == 7, [ut, x1FMb], [ph])
                G = Gr.next()
                ACT(G[:, 0:TP], ph[:, 0:TP], AF.Gelu, [ph], [G])
                Wt = Wr.next()
                TT(Wt[:, 0:TP], G[:, 0:TP], Cv[:, 0:TP, j], ALU.mult, [G, C_all], [Wt])
                for nh in range(2):
                    MM(psB[nh][0:TP, :], Wt[:, 0:TP], vb[:, nh * 512:(nh + 1) * 512], j == 0, j == 127, [Wt, vb], [psB[nh]])
            for nh in range(2 if 'noc' not in DBGF else 0):
                STT(rbuf[0:TP, nh * 512:(nh + 1) * 512], x1[0:TP, nh * 512:(nh + 1) * 512], ALPHA, psB[nh][0:TP, :], ALU.mult, ALU.add, [x1, psB[nh]], [rbuf])
            ln_from_rbuf(2, 3, xout)

        init_states_zero()
        nblk = T // TB
        xt = [xa, xb]
        for b in range(nblk):
            DMA(xt[0][:], xp[b * TB:(b + 1) * TB, :], writes=[xt[0]])
            for l in range(NL):
                layer_block(l, xt[l % 2], xt[(l + 1) % 2], TB, 64, tagdbg=b * TB)
            DMA(yp[b * TB:(b + 1) * TB, :], xt[NL % 2][:], reads=[xt[NL % 2]])
        out_states(0)
        if with_sample:
            init_states_sample()
            DMA(xt[0][0:32, :], xsm[:, :], writes=[xt[0]])
            for l in range(NL):
                layer_block(l, xt[l % 2], xt[(l + 1) % 2], 32, 32)
            DMA(ys[:, :], xt[NL % 2][0:32, :], reads=[xt[NL % 2]])
            out_states(1)
        with nc.allow_low_precision("bf16 matmul operands for PEER tables"):
            S.run()
    return nc


W_NAMES = ["w_in", "b_in", "conv_a_w", "conv_b_w", "conv_b_b", "lru_wa", "lru_ba", "lru_wx", "lru_bx", "lru_lambda",
           "hgrn_lb_logits", "hgrn_norm_g", "w_out_a", "w_out_b", "w_out_c", "w_o", "ln1_g", "ln1_b", "peer_wq",
           "peer_keys", "peer_u", "peer_v", "ln2_g", "ln2_b"]


def run(inputs, T, NL=DEPTH, with_sample=True, dbg=None, ncores=8):
    import time
    t0 = time.time()
    nc = build(T, NL, with_sample, dbg=dbg)
    print('build time', time.time() - t0, flush=True)
    f = lambda a: np.ascontiguousarray(np.asarray(a, dtype=np.float32))
    wmaps = {k: f(inputs[k]) for k in W_NAMES}
    in_maps = []
    for c in range(ncores):
        m = dict(wmaps)
        m["xp"] = f(inputs["x_prompt"][c % 4, :T])
        m["xs"] = f(inputs["x_sample"][c])
        m["st_ca"] = f(inputs["state_conv_a"][:, c])
        m["st_cb"] = f(inputs["state_conv_b"][:, c])
        m["st_h"] = f(inputs["state_lru"][:, c])
        m["st_S"] = f(inputs["state_hgrn"][:, c])
        in_maps.append(m)
    res = run_bass_kernel_spmd(nc, in_maps, core_ids=list(range(ncores)))
    print('total time', time.time() - t0, flush=True)
    return res.results


def kernel(**inputs):
    T = inputs["x_prompt"].shape[1]
    r = run(inputs, T)
    yp = np.stack([r[c]["yp"] for c in range(4)], 0)
    ys = np.stack([r[c]["ys"] for c in range(8)], 0)
    def pst(name):
        return np.stack([r[c][name] for c in range(4)], 1)
    def sst(name):
        return np.stack([r[c][name] for c in range(8)], 1)
    return (yp, ys, pst("p_ca"), pst("p_cb"), pst("p_h"), pst("p_S"),
            sst("s_ca"), sst("s_cb"), sst("s_h"), sst("s_S"))
```

```python
import numpy as np
from contextlib import ExitStack
import concourse.bass as bass
import concourse.mybir as mybir
from concourse.bass_utils import run_bass_kernel_spmd

F32 = mybir.dt.float32
BF16 = mybir.dt.bfloat16
U32 = mybir.dt.uint32
I32 = mybir.dt.int32
AF = mybir.ActivationFunctionType
ALU = mybir.AluOpType
AX = mybir.AxisListType

D = 1024
NCB = 8
DEPTH = 2
ALPHA = (2.0 * DEPTH) ** 0.25
LN_EPS = 1e-5
RMS_EPS = 1e-6
NEG = -1.0e30


class Tl:
    def __init__(self, t, name=""):
        self.t = t
        self.name = name
        self.last_w = None
        self.readers = []

    def __getitem__(self, idx):
        return self.t[idx]


class Alias:
    def __init__(self, base, ap):
        self.base = base
        self.t = ap

    def __getitem__(self, idx):
        return self.t[idx]

    last_w = property(lambda s: s.base.last_w, lambda s, v: setattr(s.base, "last_w", v))
    readers = property(lambda s: s.base.readers, lambda s, v: setattr(s.base, "readers", v))


class Op:
    __slots__ = ("eng", "fn", "deps", "signal", "sem", "semval", "dma", "idx", "eidx")


class Sched:
    ENG = ("pe", "act", "dve", "pool", "sp")

    def __init__(self, nc, stack, n_dma_sems=16):
        self.nc = nc
        self.ops = []
        self.per_eng = {e: [] for e in self.ENG}
        self.esem = {e: stack.enter_context(nc.semaphore("es_" + e)) for e in self.ENG}
        self.dsems = {}
        self.dcnt = {}
        self.drr = {}
        for e in ("sp", "pool"):
            self.dsems[e] = [stack.enter_context(nc.semaphore("ds_%s%d" % (e, i))) for i in range(n_dma_sems)]
            self.dcnt[e] = [0] * n_dma_sems
            self.drr[e] = 0
        self.dlast = {}
        self.stack = stack

    def sb(self, name, shape, dt=F32):
        n = 1
        for d_ in shape[1:]:
            n *= d_
        self.sbytes = getattr(self, "sbytes", 0) + n * (2 if dt == BF16 else 4)
        t = self.stack.enter_context(self.nc.sbuf_tensor(name, list(shape), dt))
        return Tl(t, name)

    def ps(self, name, shape, dt=F32):
        t = self.stack.enter_context(self.nc.psum_tensor(name, list(shape), dt))
        return Tl(t, name)

    def op(self, eng, fn, reads=(), writes=(), dma=False):
        o = Op()
        o.eng = eng
        o.fn = fn
        o.dma = dma
        o.signal = False
        o.sem = None
        o.semval = None
        o.idx = len(self.ops)
        o.eidx = len(self.per_eng[eng])
        deps = []
        for r in reads:
            if r is not None and r.last_w is not None:
                deps.append(r.last_w)
        for w in writes:
            if w is None:
                continue
            if w.last_w is not None:
                deps.append(w.last_w)
            deps.extend(w.readers)
        if dma:
            k = self.drr[eng]
            self.drr[eng] = (k + 1) % len(self.dsems[eng])
            prev = self.dlast.get((eng, k))
            if prev is not None:
                deps.append(prev)
            self.dcnt[eng][k] += 16
            o.sem = self.dsems[eng][k]
            o.semval = self.dcnt[eng][k]
            self.dlast[(eng, k)] = o
            o.signal = True
        seen = set()
        dd = []
        for d in deps:
            if d is o or id(d) in seen:
                continue
            seen.add(id(d))
            if (not d.dma) and d.eng == "pe" and eng == "pe" and not dma:
                continue
            dd.append(d)
            if not d.dma:
                d.signal = True
        o.deps = dd
        for r in reads:
            if r is not None:
                r.readers.append(o)
        for w in writes:
            if w is not None:
                w.last_w = o
                w.readers = []
        self.ops.append(o)
        self.per_eng[eng].append(o)
        return o

    def finalize(self):
        for e in self.ENG:
            c = 0
            for o in self.per_eng[e]:
                if o.dma:
                    continue
                if o.signal:
                    c += 1
                    o.sem = self.esem[e]
                    o.semval = c

    def emit_engine(self, ename, eng):
        seen = {}
        for o in self.per_eng[ename]:
            need = {}
            for d in o.deps:
                if (not d.dma) and (not o.dma) and d.eng == ename and ename in ("act", "dve") and o.eidx - d.eidx >= 4:
                    continue
                key = id(d.sem)
                if key not in need or need[key][1] < d.semval:
                    need[key] = (d.sem, d.semval)
            for key, (sem, val) in need.items():
                if seen.get(key, 0) >= val:
                    continue
                eng.wait_ge(sem, val)
                seen[key] = val
            inst = o.fn(eng)
            if o.signal:
                inst.then_inc(o.sem, 16 if o.dma else 1)
        if ename in self.dsems:
            for k, s in enumerate(self.dsems[ename]):
                v = self.dcnt[ename][k]
                if v > 0 and seen.get(id(s), 0) < v:
                    eng.wait_ge(s, v)

    def run(self):
        self.finalize()
        S = self
        with self.nc.Block() as block:
            @block.tensor
            def _(e):
                S.emit_engine("pe", e)

            @block.scalar
            def _(e):
                S.emit_engine("act", e)

            @block.vector
            def _(e):
                S.emit_engine("dve", e)

            @block.gpsimd
            def _(e):
                S.emit_engine("pool", e)

            @block.sync
            def _(e):
                S.emit_engine("sp", e)


class Ring:
    def __init__(self, tiles):
        self.tiles = tiles
        self.i = 0

    def next(self):
        t = self.tiles[self.i]
        self.i = (self.i + 1) % len(self.tiles)
        return t


def build(T, NL=DEPTH, with_sample=True, NWR=3, NGR=3, dbg=None):
    import os
    DBGF = os.environ.get('KDBG', '')
    TB = 128
    nc = bass.Bass("TRN2", target_bir_lowering=False)

    def din(name, shape, dt=F32):
        return nc.dram_tensor(name, list(shape), dt, kind="ExternalInput").ap()

    def dout(name, shape, dt=F32):
        return nc.dram_tensor(name, list(shape), dt, kind="ExternalOutput").ap()

    xp = din("xp", [T, D])
    xsm = din("xs", [32, D])
    st_ca = din("st_ca", [DEPTH, 2, D])
    st_cb = din("st_cb", [DEPTH, 3, D])
    st_h = din("st_h", [DEPTH, D])
    st_S = din("st_S", [DEPTH, 8, 128, 128])
    w_in = din("w_in", [DEPTH, D, 12 * D])
    b_in = din("b_in", [DEPTH, 12 * D])
    conv_a_w = din("conv_a_w", [DEPTH, 3, D])
    conv_b_w = din("conv_b_w", [DEPTH, 4, D])
    conv_b_b = din("conv_b_b", [DEPTH, D])
    lru_wa = din("lru_wa", [DEPTH, 8, 128, 128])
    lru_ba = din("lru_ba", [DEPTH, D])
    lru_wx = din("lru_wx", [DEPTH, 8, 128, 128])
    lru_bx = din("lru_bx", [DEPTH, D])
    lru_lambda = din("lru_lambda", [DEPTH, D])
    lb_logits = din("hgrn_lb_logits", [DEPTH, D])
    norm_g = din("hgrn_norm_g", [DEPTH, 128])
    w_out = [din("w_out_a", [DEPTH, D, D]), din("w_out_b", [DEPTH, D, D]), din("w_out_c", [DEPTH, D, D])]
    w_o = din("w_o", [DEPTH, D, D])
    ln1_g = din("ln1_g", [DEPTH, D])
    ln1_b = din("ln1_b", [DEPTH, D])
    peer_wq = din("peer_wq", [DEPTH, D, 2048])
    peer_keys = din("peer_keys", [DEPTH, 8, 2, 128, 128])
    peer_u = din("peer_u", [DEPTH, 16384, D])
    peer_v = din("peer_v", [DEPTH, 16384, D])
    peer_u2 = peer_u.rearrange("l e d -> (l e) d")
    peer_v2 = peer_v.rearrange("l e d -> (l e) d")
    UT = nc.dram_tensor("UT_scr", [DEPTH, 128, 128, 1024], BF16, kind="Internal").ap()
    VB = nc.dram_tensor("VB_scr", [DEPTH, 128, 128, 1024], BF16, kind="Internal").ap()
    WS = nc.dram_tensor("WS_scr", [DEPTH, 144, 128, 1024], BF16, kind="Internal").ap()
    ln2_g = din("ln2_g", [DEPTH, D])
    ln2_b = din("ln2_b", [DEPTH, D])

    yp = dout("yp", [T, D])
    ys = dout("ys", [32, D])
    o_ca = [dout("p_ca", [DEPTH, 2, D]), dout("s_ca", [DEPTH, 2, D])]
    o_cb = [dout("p_cb", [DEPTH, 3, D]), dout("s_cb", [DEPTH, 3, D])]
    o_h = [dout("p_h", [DEPTH, D]), dout("s_h", [DEPTH, D])]
    o_S = [dout("p_S", [DEPTH, 8, 128, 128]), dout("s_S", [DEPTH, 8, 128, 128])]
    dbg_out = {}
    if dbg:
        for nm, shp in dbg.items():
            dbg_out[nm] = dout("dbg_" + nm, shp)

    with ExitStack() as st:
        S = Sched(nc, st)

        def DMA(out_ap, in_ap, reads=(), writes=(), slow=False):
            if slow:
                S.op("sp", lambda e: e.dma_start(out=out_ap, in_=in_ap, allow_slow_non_contiguous=True), reads=reads, writes=writes, dma=True)
            else:
                S.op("sp", lambda e: e.dma_start(out=out_ap, in_=in_ap), reads=reads, writes=writes, dma=True)

        def ACT(out_ap, in_ap, func, reads, writes, bias=None, scale=None):
            kw = {}
            if bias is not None:
                kw["bias"] = bias
            if scale is not None:
                kw["scale"] = scale
            S.op("act", lambda e: e.activation(out=out_ap, in_=in_ap, func=func, **kw), reads=reads, writes=writes)

        def TS(out_ap, in_ap, s1, s2, op0, op1, reads, writes, eng="dve"):
            if op1 is None:
                S.op(eng, lambda e: e.tensor_scalar(out=out_ap, in0=in_ap, scalar1=s1, scalar2=None, op0=op0), reads=reads, writes=writes)
            else:
                S.op(eng, lambda e: e.tensor_scalar(out=out_ap, in0=in_ap, scalar1=s1, scalar2=s2, op0=op0, op1=op1), reads=reads, writes=writes)

        def TT(out_ap, a_ap, b_ap, op, reads, writes, eng="dve"):
            S.op(eng, lambda e: e.tensor_tensor(out=out_ap, in0=a_ap, in1=b_ap, op=op), reads=reads, writes=writes)

        def STT(out_ap, in0, scalar, in1, op0, op1, reads, writes, accum=None, eng="dve"):
            if accum is None:
                S.op(eng, lambda e: e.scalar_tensor_tensor(out=out_ap, in0=in0, scalar=scalar, in1=in1, op0=op0, op1=op1), reads=reads, writes=writes)
            else:
                S.op(eng, lambda e: e.scalar_tensor_tensor(out=out_ap, in0=in0, scalar=scalar, in1=in1, op0=op0, op1=op1, accum_out=accum), reads=reads, writes=writes)

        def CP(out_ap, in_ap, reads, writes, eng="dve"):
            S.op(eng, lambda e: e.tensor_copy(out=out_ap, in_=in_ap), reads=reads, writes=writes)

        def MSET(ap, val, writes, eng="pool"):
            S.op(eng, lambda e: e.memset(ap, val), writes=writes)

        def MM(out_ap, lhsT, rhs, start, stop, reads, writes):
            S.op("pe", lambda e: e.matmul(out_ap, lhsT=lhsT, rhs=rhs, start=start, stop=stop), reads=reads, writes=writes)

        def TR(out_ap, in_ap, k, reads, writes):
            S.op("pe", lambda e: e.transpose(out_ap, in_ap, ident[0:k, 0:k]), reads=list(reads) + [ident], writes=writes)

        ident = S.sb("ident", [128, 128])
        ones = S.sb("ones", [128, 128])
        tri = S.sb("tri", [64, 64])
        cmask = {64: S.sb("cmask64", [128, TB]), 32: S.sb("cmask32", [128, TB])}
        iota16 = S.sb("iota16", [128, 16])
        iota_i = S.sb("iota_i", [128, 16], I32)
        psr = Ring([S.ps("ps%d" % i, [128, 512]) for i in range(6)])
        psB = [S.ps("psB%d" % i, [128, 512]) for i in range(2)]
        wring = Ring([S.sb("wr%d" % i, [128, 1024]) for i in range(NWR)])
        zps = [Ring([S.sb("z%d_%d" % (k, i), [128, TB + 4]) for i in range(24)]) for k in range(2)]
        zr = zps[0]
        xa = S.sb("xa", [128, D])
        xb = S.sb("xb", [128, D])
        x1 = S.sb("x1", [128, D])
        rbuf = S.sb("rbuf", [128, D])
        aABC = S.sb("aABC", [128, 8, 3, TB], BF16)
        aV = [[Tl(aABC.t) for _ in range(3)] for _ in range(8)]
        merged = S.sb("merged", [128, 8, TB], BF16)
        s2 = S.sb("s2", [128, 128])
        C_all = S.sb("C_all", [128, 128 * 128], BF16)
        if 'noalias' in DBGF:
            pscr = S.sb("pscr", [128, 2048]); cand = S.sb("cand", [128, 2048]); qT = S.sb("qT", [128, 16, TB])
        else:
            pscr = Alias(C_all, C_all[:, 0:4096].bitcast(F32))
            cand = Alias(C_all, C_all[:, 4096:8192].bitcast(F32))
            qT = Alias(C_all, C_all[:, 8192:12288].bitcast(F32).rearrange("p (a b) -> p a b", b=128))
        x1FMb = S.sb("x1FMb", [128, 8, TB], BF16)
        iota128 = S.sb("iota128", [128, 128])
        iota128i = S.sb("iota128i", [128, 128], I32)
        slotT = S.sb("slotT", [128, 3, 128])
        OIr = Ring([S.sb("oi%d" % i, [128, 128], BF16) for i in range(4)])
        OJr = Ring([S.sb("oj%d" % i, [128, 128], BF16) for i in range(4)])
        UTr = Ring([S.sb("utr%d" % i, [128, 1024], BF16) for i in range(3)])
        VBr = Ring([S.sb("vbr%d" % i, [128, 1024], BF16) for i in range(3)])
        Gr = Ring([S.sb("gg%d" % i, [128, 128]) for i in range(3)])
        Wr = Ring([S.sb("wt%d" % i, [128, 128], BF16) for i in range(3)])
        wbr = Ring([S.sb("wbr%d" % i, [128, 1024], BF16) for i in range(5)])
        WSbuf = [[Tl(None) for _ in range(144)] for _ in range(NL)]
        xFMb = S.sb("xFMb", [128, 8, TB], BF16)
        UTbuf = [[Tl(None) for _ in range(128)] for _ in range(NL)]
        VBbuf = [[Tl(None) for _ in range(128)] for _ in range(NL)]
        lnp = [S.sb("lnp%d" % i, [128, D]) for i in range(4)]
        keysT = [S.sb("keysT%d" % l, [128, 16, 128]) for l in range(NL)]
        PA = [S.sb("PA%d" % l, [128, 96]) for l in range(NL)]
        PB = [S.sb("PB%d" % l, [128, 128]) for l in range(NL)]
        PD = [S.sb("PD%d" % l, [128, 4, 8]) for l in range(NL)]
        stg = S.sb("stg", [128, 128])
        Sst = [S.sb("Sst%d" % l, [128, 8, 128]) for l in range(NL)]
        SstV = [[Tl(Sst[l].t) for _ in range(8)] for l in range(NL)]
        cAst = [S.sb("cAst%d" % l, [128, 8, 2]) for l in range(NL)]
        cAV = [[Tl(cAst[l].t) for _ in range(8)] for l in range(NL)]
        cBst = [S.sb("cBst%d" % l, [128, 8, 3]) for l in range(NL)]
        cBV = [[Tl(cBst[l].t) for _ in range(8)] for l in range(NL)]
        hst = [S.sb("hst%d" % l, [128, 8]) for l in range(NL)]
        hV = [[Tl(hst[l].t) for _ in range(8)] for l in range(NL)]
        small = S.sb("small", [128, 64])
        sv = S.sb("sv", [128, 16, 16])
        si = S.sb("si", [128, 16, 16], U32)
        sif = S.sb("sif", [128, 16, 16])
        tops = S.sb("tops", [128, 8, 16])
        topp = S.sb("topp", [128, 8, 16], U32)
        pij = S.sb("pij", [128, 2, 8, 16], U32)
        pijf = S.sb("pijf", [128, 2, 8, 16])
        esel = S.sb("esel", [128, 2, 8, 16])
        eidf = S.sb("eidf", [128, 128])
        eid = S.sb("eid", [128, 128], U32)
        gate = S.sb("gate", [128, 8, 16])
        mvt = S.sb("mvt", [128, 16])
        bst = S.sb("bst", [128, 12])

        print('SBUF bytes/partition', S.sbytes, flush=True)
        MSET(ident[:], 0.0, [ident])
        S.op("pool", lambda e: e.affine_select(out=ident[:], in_=ident[:], pattern=[[-1, 128]], compare_op=ALU.not_equal, fill=1.0, base=0, channel_multiplier=1), reads=[ident], writes=[ident])
        MSET(ones[:], 1.0, [ones])
        MSET(tri[:], 1.0, [tri])
        S.op("pool", lambda e: e.affine_select(out=tri[:], in_=tri[:], pattern=[[1, 64]], compare_op=ALU.is_ge, fill=0.0, base=0, channel_multiplier=-1), reads=[tri], writes=[tri])
        for L in (64, 32):
            MSET(cmask[L][:], 1.0, [cmask[L]])
            for c in range(TB // L):
                MSET(cmask[L][:, c * L:c * L + 1], 0.0, [cmask[L]])
        S.op("pool", lambda e: e.iota(iota_i[:], pattern=[[1, 16]], base=0, channel_multiplier=0), writes=[iota_i])
        CP(iota16[:], iota_i[:], [iota_i], [iota16])
        S.op("pool", lambda e: e.iota(iota128i[:], pattern=[[1, 128]], base=0, channel_multiplier=0), writes=[iota128i])
        CP(iota128[:], iota128i[:], [iota128i], [iota128])

        for l in range(NL):
            MSET(stg[:], 0.0, [stg])
            DMA(stg[0:96, :], b_in[l].rearrange("(r p) -> r p", p=128), writes=[stg])
            p = psr.next()
            TR(p[:, 0:128], stg[:], 128, [stg], [p])
            CP(PA[l][:], p[:, 0:96], [p], [PA[l]])
            MSET(stg[:], 0.0, [stg])
            DMA(stg[0:24, :], conv_a_w[l].rearrange("k (c p) -> (k c) p", p=128), writes=[stg])
            DMA(stg[24:56, :], conv_b_w[l].rearrange("k (c p) -> (k c) p", p=128), writes=[stg])
            DMA(stg[56:64, :], conv_b_b[l].rearrange("(c p) -> c p", p=128), writes=[stg])
            DMA(stg[64:72, :], lru_ba[l].rearrange("(c p) -> c p", p=128), writes=[stg])
            DMA(stg[72:80, :], lru_bx[l].rearrange("(c p) -> c p", p=128), writes=[stg])
            DMA(stg[80:88, :], lru_lambda[l].rearrange("(c p) -> c p", p=128), writes=[stg])
            DMA(stg[88:96, :], lb_logits[0].rearrange("(c p) -> c p", p=128), writes=[stg])
            DMA(stg[96:104, :], lb_logits[1].rearrange("(c p) -> c p", p=128), writes=[stg])
            DMA(stg[104:105, :], norm_g[l:l + 1, :], writes=[stg])
            p = psr.next()
            TR(p[:, 0:128], stg[:], 128, [stg], [p])
            CP(PB[l][:], p[:, 0:128], [p], [PB[l]])
            ACT(PD[l][:, 0, :], PB[l][:, 80:88], AF.Exp, [PB[l]], [PD[l]], scale=-1.0)
            ACT(PD[l][:, 0, :], PD[l][:, 0, :], AF.Ln, [PD[l]], [PD[l]], bias=1.0)
            TS(PD[l][:, 1, :], PD[l][:, 0, :], -16.0, None, ALU.mult, None, [PD[l]], [PD[l]])
            TS(PD[l][:, 0, :], PD[l][:, 0, :], -8.0, None, ALU.mult, None, [PD[l]], [PD[l]])
            if l == 0:
                MSET(PD[l][:, 2, :], 0.0, [PD[l]], eng="dve")
                MSET(PD[l][:, 3, :], 1.0, [PD[l]], eng="dve")
            else:
                TT(PD[l][:, 2, :], PB[l][:, 96:104], PB[l][:, 88:96], ALU.subtract, [PB[l]], [PD[l]])
                ACT(PD[l][:, 2, :], PD[l][:, 2, :], AF.Sigmoid, [PD[l]], [PD[l]])
                TS(PD[l][:, 3, :], PD[l][:, 2, :], -1.0, 1.0, ALU.mult, ALU.add, [PD[l]], [PD[l]])
            for hp in range(16):
                w = wring.next()
                DMA(w[:, 0:128], peer_keys[l, hp // 2, hp % 2], writes=[w])
                p = psr.next()
                TR(p[:, 0:128], w[:, 0:128], 128, [w], [p])
                CP(keysT[l][:, hp, :], p[:, 0:128], [p], [keysT[l]], eng="dve")

        import os
        DBGF = os.environ.get('KDBG', '')
        for l in range(NL if 'noprep' not in DBGF else 0):
            ug = peer_u[l].rearrange("(i j) d -> j i d", j=128)
            vg = peer_v[l].rearrange("(i j) d -> j i d", j=128)
            for j in range(128):
                g1 = wring.next()
                DMA(g1[:], ug[j], writes=[g1])
                ut = UTr.next()
                for c4 in range(2):
                    p = psr.next()
                    for k in range(4):
                        c = c4 * 4 + k
                        TR(p[:, k * 128:(k + 1) * 128], g1[:, c * 128:(c + 1) * 128], 128, [g1], [p])
                    S.op("act", (lambda e, ut=ut, p=p, c4=c4: e.copy(out=ut[:, c4 * 512:(c4 + 1) * 512], in_=p[:, :])), reads=[p], writes=[ut])
                DMA(UT[l, j], ut[:], reads=[ut], writes=[UTbuf[l][j]])
                g2 = wring.next()
                DMA(g2[:], vg[j], writes=[g2])
                vb = VBr.next()
                CP(vb[:], g2[:], [g2], [vb], eng="pool")
                DMA(VB[l, j], vb[:], reads=[vb], writes=[VBbuf[l][j]])

        def init_states_zero():
            for l in range(NL):
                MSET(Sst[l][:], 0.0, SstV[l])
                MSET(cAst[l][:], 0.0, cAV[l])
                MSET(cBst[l][:], 0.0, cBV[l])
                MSET(hst[l][:], 0.0, hV[l])

        def init_states_sample():
            for l in range(NL):
                DMA(Sst[l][:], st_S[l].rearrange("h d e -> d h e"), writes=SstV[l])
                for cb in range(8):
                    DMA(cAst[l][:, cb, :], st_ca[l][:, cb * 128:(cb + 1) * 128].rearrange("k p -> p k"), writes=[cAV[l][cb]], slow=True)
                    DMA(cBst[l][:, cb, :], st_cb[l][:, cb * 128:(cb + 1) * 128].rearrange("k p -> p k"), writes=[cBV[l][cb]], slow=True)
                DMA(hst[l][:], st_h[l].rearrange("(c p) -> p c", p=128), writes=hV[l], slow=True)

        def out_states(which):
            for l in range(NL):
                DMA(o_S[which][l].rearrange("h d e -> d h e"), Sst[l][:], reads=SstV[l])
                for cb in range(8):
                    DMA(o_ca[which][l][:, cb * 128:(cb + 1) * 128].rearrange("k p -> p k"), cAst[l][:, cb, :], reads=[cAV[l][cb]], slow=True)
                    DMA(o_cb[which][l][:, cb * 128:(cb + 1) * 128].rearrange("k p -> p k"), cBst[l][:, cb, :], reads=[cBV[l][cb]], slow=True)
                DMA(o_h[which][l].rearrange("(c p) -> p c", p=128), hst[l][:], reads=hV[l], slow=True)

        def wsrc(l, idx):
            if idx < 96:
                return w_in[l][:, idx * 128:(idx + 1) * 128].rearrange("(c p) n -> p c n", p=128), True
            if idx < 120:
                X, j = (idx - 96) // 8, (idx - 96) % 8
                return w_out[X][l][:, j * 128:(j + 1) * 128].rearrange("(c p) n -> p c n", p=128), True
            if idx < 128:
                j = idx - 120
                return w_o[l, j * 128:(j + 1) * 128, :], False
            hp = idx - 128
            return peer_wq[l][:, hp * 128:(hp + 1) * 128].rearrange("(c p) n -> p c n", p=128), True

        for l in range(NL):
            for idx in range(144):
                src, tiled = wsrc(l, idx)
                w = wring.next()
                if tiled:
                    DMA(w[:].rearrange("p (c n) -> p c n", n=128), src, writes=[w])
                else:
                    DMA(w[:], src, writes=[w])
                wb = wbr.next()
                CP(wb[:], w[:], [w], [wb], eng=("pool" if idx % 2 else "dve"))
                DMA(WS[l, idx], wb[:], reads=[wb], writes=[WSbuf[l][idx]])

        def wl(l, idx):
            wb = wbr.next()
            DMA(wb[:], WS[l, idx], reads=[WSbuf[l][idx]], writes=[wb])
            return wb

        def in_proj(l, idx, nt):
            w = wl(l, idx)
            p = psr.next()
            for c in range(8):
                MM(p[:, 0:nt], w[:, c * 128:(c + 1) * 128], xFMb[:, c, 0:nt], c == 0, c == 7, [w, xFMb], [p])
            return p

        def layer_block(l, xin, xout, nt, L, tagdbg=None):
            TP = nt
            nch = nt // L
            for i, src in enumerate((ln1_g, ln1_b, ln2_g, ln2_b)):
                DMA(lnp[i][:], src[l:l + 1, :].to_broadcast([128, D]), writes=[lnp[i]])
            for c4 in range(2):
                p = psr.next()
                for k in range(4):
                    c = c4 * 4 + k
                    TR(p[:, k * 128:k * 128 + TP], xin[0:TP, c * 128:(c + 1) * 128], TP, [xin], [p])
                for k in range(4):
                    c = c4 * 4 + k
                    S.op("act", (lambda e, c=c, k=k, p=p: e.copy(out=xFMb[:, c, 0:TP], in_=p[:, k * 128:k * 128 + TP])), reads=[p], writes=[xFMb])
            def cb_body(cb, zp):
                def zin(G, func):
                    p = in_proj(l, G * 8 + cb, nt)
                    z = zp.next()
                    ACT(z[:, 0:nt], p[:, 0:nt], func, [p, PA[l]], [z], bias=PA[l][:, G * 8 + cb:G * 8 + cb + 1])
                    return z
                zB = zin(0, AF.Identity)
                yield
                zC = zin(1, AF.Identity)
                yield
                zxA = zin(2, AF.Identity)
                yield
                ub = zp.next()
                yield
                CP(ub[:, 0:2], cAst[l][:, cb, :], [cAV[l][cb]], [ub], eng="pool")
                yield
                TT(ub[:, 2:2 + nt], zC[:, 0:nt], zxA[:, 0:nt], ALU.mult, [zC, zxA], [ub])
                yield
                CP(cAst[l][:, cb, :], ub[:, nt:nt + 2], [ub], [cAV[l][cb]], eng="pool")
                yield
                y = zp.next()
                yield
                TS(y[:, 0:nt], ub[:, 0:nt], PB[l][:, cb:cb + 1], None, ALU.mult, None, [ub, PB[l]], [y])
                yield
                STT(y[:, 0:nt], ub[:, 1:1 + nt], PB[l][:, 8 + cb:9 + cb], y[:, 0:nt], ALU.mult, ALU.add, [ub, y, PB[l]], [y])
                yield
                STT(y[:, 0:nt], ub[:, 2:2 + nt], PB[l][:, 16 + cb:17 + cb], y[:, 0:nt], ALU.mult, ALU.add, [ub, y, PB[l]], [y])
                yield
                TT(aABC[:, cb, 0, 0:nt], zB[:, 0:nt], y[:, 0:nt], ALU.mult, [zB, y], [aV[cb][0]])
                yield
                zxB = zin(3, AF.Identity)
                yield
                ggB = zin(4, AF.Gelu)
                yield
                cbuf = zp.next()
                yield
                CP(cbuf[:, 0:3], cBst[l][:, cb, :], [cBV[l][cb]], [cbuf], eng="pool")
                yield
                CP(cbuf[:, 3:3 + nt], zxB[:, 0:nt], [zxB], [cbuf], eng="pool")
                yield
                CP(cBst[l][:, cb, :], cbuf[:, nt:nt + 3], [cbuf], [cBV[l][cb]], eng="pool")
                yield
                xl = zp.next()
                yield
                TS(xl[:, 0:nt], cbuf[:, 0:nt], PB[l][:, 24 + cb:25 + cb], PB[l][:, 56 + cb:57 + cb], ALU.mult, ALU.add, [cbuf, PB[l]], [xl])
                yield
                for k in range(1, 4):
                    STT(xl[:, 0:nt], cbuf[:, k:k + nt], PB[l][:, 24 + 8 * k + cb:25 + 8 * k + cb], xl[:, 0:nt], ALU.mult, ALU.add, [cbuf, xl, PB[l]], [xl])
                    yield
                w = wring.next()
                yield
                DMA(w[:, 0:128], lru_wa[l, cb], writes=[w])
                yield
                DMA(w[:, 128:256], lru_wx[l, cb], writes=[w])
                yield
                p = psr.next()
                yield
                MM(p[:, 0:nt], w[:, 0:128], xl[:, 0:nt], True, True, [w, xl], [p])
                yield
                MM(p[:, 128:128 + nt], w[:, 128:256], xl[:, 0:nt], True, True, [w, xl], [p])
                yield
                r = zp.next()
                yield
                gi = zp.next()
                yield
                ACT(r[:, 0:nt], p[:, 0:nt], AF.Sigmoid, [p, PB[l]], [r], bias=PB[l][:, 64 + cb:65 + cb])
                yield
                ACT(gi[:, 0:nt], p[:, 128:128 + nt], AF.Sigmoid, [p, PB[l]], [gi], bias=PB[l][:, 72 + cb:73 + cb])
                yield
                a = zp.next()
                yield
                a2 = zp.next()
                yield
                ACT(a[:, 0:nt], r[:, 0:nt], AF.Exp, [r, PD[l]], [a], scale=PD[l][:, 0, cb:cb + 1])
                yield
                ACT(a2[:, 0:nt], r[:, 0:nt], AF.Exp, [r, PD[l]], [a2], scale=PD[l][:, 1, cb:cb + 1])
                yield
                TS(a2[:, 0:nt], a2[:, 0:nt], -1.0, 1.0, ALU.mult, ALU.add, [a2], [a2])
                yield
                ACT(a2[:, 0:nt], a2[:, 0:nt], AF.Sqrt, [a2], [a2])
                yield
                TT(gi[:, 0:nt], gi[:, 0:nt], a2[:, 0:nt], ALU.mult, [gi, a2], [gi])
                yield
                TT(gi[:, 0:nt], gi[:, 0:nt], xl[:, 0:nt], ALU.mult, [gi, xl], [gi])
                yield
                hh = zp.next()
                yield
                S.op("dve", (lambda e, hh=hh, a=a, gi=gi, cb=cb: e.tensor_tensor_scan(out=hh[:, 0:nt], data0=a[:, 0:nt], data1=gi[:, 0:nt], initial=hst[l][:, cb:cb + 1], op0=ALU.mult, op1=ALU.add)), reads=[a, gi, hV[l][cb]], writes=[hh])
                yield
                CP(hst[l][:, cb:cb + 1], hh[:, nt - 1:nt], [hh], [hV[l][cb]], eng="pool")
                yield
                TT(aABC[:, cb, 1, 0:nt], ggB[:, 0:nt], hh[:, 0:nt], ALU.mult, [ggB, hh], [aV[cb][1]])
                yield
                q = zin(5, AF.Silu)
                yield
                f = zin(6, AF.Sigmoid)
                yield
                vv = zin(7, AF.Identity)
                yield
                sg = zin(8, AF.Silu)
                yield
                TS(f[:, 0:nt], f[:, 0:nt], PD[l][:, 3, cb:cb + 1], PD[l][:, 2, cb:cb + 1], ALU.mult, ALU.add, [f, PD[l]], [f])
                yield
                kk = zp.next()
                yield
                TS(kk[:, 0:nt], f[:, 0:nt], -1.0, 1.0, ALU.mult, ALU.add, [f], [kk])
                yield
                lf = zp.next()
                yield
                ACT(lf[:, 0:nt], f[:, 0:nt], AF.Ln, [f], [lf])
                yield
                bb = zp.next()
                yield
                S.op("dve", (lambda e, bb=bb, lf=lf: e.tensor_tensor_scan(out=bb[:, 0:nt], data0=cmask[L][:, 0:nt], data1=lf[:, 0:nt], initial=0.0, op0=ALU.mult, op1=ALU.add)), reads=[lf, cmask[L]], writes=[bb])
                yield
                eb = zp.next()
                yield
                ACT(eb[:, 0:nt], bb[:, 0:nt], AF.Exp, [bb], [eb])
                yield
                ACT(bb[:, 0:nt], bb[:, 0:nt], AF.Exp, [bb], [bb], scale=-1.0)
                yield
                TT(q[:, 0:nt], q[:, 0:nt], eb[:, 0:nt], ALU.mult, [q, eb], [q])
                yield
                TT(kk[:, 0:nt], kk[:, 0:nt], bb[:, 0:nt], ALU.mult, [kk, bb], [kk])
                yield
                osb = zp.next()
                yield
                Sv = SstV[l][cb]
                yield
                for c in range(nch):
                    c0 = c * L
                    yield
                    pa = psr.next()
                    yield
                    MM(pa[0:L, 0:L], kk[:, c0:c0 + L], q[:, c0:c0 + L], True, True, [kk, q], [pa])
                    yield
                    att = zp.next()
                    yield
                    TT(att[0:L, 0:L], pa[0:L, 0:L], tri[0:L, 0:L], ALU.mult, [pa, tri], [att])
                    yield
                    pv = psr.next()
                    yield
                    TR(pv[0:L, 0:128], vv[:, c0:c0 + L], 128, [vv], [pv])
                    yield
                    vtm = zp.next()
                    yield
                    S.op("act", (lambda e, vtm=vtm, pv=pv: e.copy(out=vtm[0:L, 0:128], in_=pv[0:L, 0:128])), reads=[pv], writes=[vtm])
                    yield
                    po = psr.next()
                    yield
                    MM(po[:, 0:L], Sst[l][:, cb, :], q[:, c0:c0 + L], True, False, [Sv, q], [po])
                    yield
                    MM(po[:, 0:L], vtm[0:L, 0:128], att[0:L, 0:L], False, True, [vtm, att], [po])
                    yield
                    S.op("act", (lambda e, osb=osb, po=po, c0=c0: e.copy(out=osb[:, c0:c0 + L], in_=po[:, 0:L])), reads=[po], writes=[osb])
                    yield
                    k2 = zp.next()
                    yield
                    TS(k2[:, 0:L], kk[:, c0:c0 + L], eb[:, c0 + L - 1:c0 + L], None, ALU.mult, None, [kk, eb], [k2])
                    yield
                    pk = psr.next()
                    yield
                    TR(pk[0:L, 0:128], k2[:, 0:L], 128, [k2], [pk])
                    yield
                    k2t = zp.next()
                    yield
                    CP(k2t[0:L, 0:128], pk[0:L, 0:128], [pk], [k2t])
                    yield
                    pn = psr.next()
                    yield
                    MM(pn[:, 0:128], k2t[0:L, 0:128], vtm[0:L, 0:128], True, True, [k2t, vtm], [pn])
                    yield
                    STT(Sst[l][:, cb, :], Sst[l][:, cb, :], eb[:, c0 + L - 1:c0 + L], pn[:, 0:128], ALU.mult, ALU.add, [Sv, eb, pn], [Sv])
                    yield
                sq = zp.next()
                yield
                ACT(sq[:, 0:nt], osb[:, 0:nt], AF.Square, [osb], [sq])
                yield
                pss = psr.next()
                yield
                MM(pss[:, 0:nt], ones[:], sq[:, 0:nt], True, True, [ones, sq], [pss])
                yield
                TS(sq[:, 0:nt], pss[:, 0:nt], 1.0 / 128.0, RMS_EPS, ALU.mult, ALU.add, [pss], [sq])
                yield
                ACT(sq[:, 0:nt], sq[:, 0:nt], AF.Sqrt, [sq], [sq])
                yield
                S.op("dve", (lambda e, sq=sq: e.reciprocal(out=sq[:, 0:nt], in_=sq[:, 0:nt])), reads=[sq], writes=[sq])
                yield
                TT(osb[:, 0:nt], osb[:, 0:nt], sq[:, 0:nt], ALU.mult, [osb, sq], [osb])
                yield
                STT(aABC[:, cb, 2, 0:nt], osb[:, 0:nt], PB[l][:, 104:105], sg[:, 0:nt], ALU.mult, ALU.mult, [osb, sg, PB[l]], [aV[cb][2]])
                yield
            KI = 2
            for g0 in range(0, 8, KI):
                active = [cb_body(cb, zps[cb % KI]) for cb in range(g0, g0 + KI)]
                while active:
                    for g_ in list(active):
                        try:
                            next(g_)
                        except StopIteration:
                            active.remove(g_)
            for j in range(8):
                pys = []
                for X in range(3):
                    w = wl(l, 96 + X * 8 + j)
                    p = psr.next()
                    for cb in range(8):
                        MM(p[:, 0:nt], w[:, cb * 128:(cb + 1) * 128], aABC[:, cb, X, 0:nt], cb == 0, cb == 7, [w, aV[cb][X]], [p])
                    pys.append(p)
                sgs = []
                for X in range(3):
                    G = 9 + X
                    p = in_proj(l, G * 8 + j, nt)
                    z = zr.next()
                    ACT(z[:, 0:nt], p[:, 0:nt], AF.Sigmoid, [p, PA[l]], [z], bias=PA[l][:, G * 8 + j:G * 8 + j + 1])
                    sgs.append(z)
                for X in range(3):
                    TT(sgs[X][:, 0:nt], sgs[X][:, 0:nt], pys[X][:, 0:nt], ALU.mult, [sgs[X], pys[X]], [sgs[X]])
                TT(sgs[0][:, 0:nt], sgs[0][:, 0:nt], sgs[1][:, 0:nt], ALU.add, [sgs[0], sgs[1]], [sgs[0]])
                TT(merged[:, j, 0:nt], sgs[0][:, 0:nt], sgs[2][:, 0:nt], ALU.add, [sgs[0], sgs[2]], [merged])
            pw = psB
            for j in range(8):
                w = wl(l, 120 + j)
                for nh in range(2):
                    MM(pw[nh][0:TP, :], merged[:, j, 0:TP], w[:, nh * 512:(nh + 1) * 512], j == 0, j == 7, [merged, w], [pw[nh]])

            def resid_ln(src, pws, gi_, bi_, dst):
                for nh in range(2):
                    STT(rbuf[0:TP, nh * 512:(nh + 1) * 512], src[0:TP, nh * 512:(nh + 1) * 512], ALPHA, pws[nh][0:TP, :], ALU.mult, ALU.add, [src, pws[nh]], [rbuf])
                ln_from_rbuf(gi_, bi_, dst)

            def ln_from_rbuf(gi_, bi_, dst):
                for nh in range(2):
                    S.op("dve", (lambda e, nh=nh: e.bn_stats(out=bst[0:TP, nh * 6:(nh + 1) * 6], in_=rbuf[0:TP, nh * 512:(nh + 1) * 512])), reads=[rbuf], writes=[bst])
                S.op("dve", lambda e: e.bn_aggr(out=mvt[0:TP, 0:2], in_=bst[0:TP, 0:12]), reads=[bst], writes=[mvt])
                TS(mvt[0:TP, 2:3], mvt[0:TP, 1:2], 1.0, LN_EPS, ALU.mult, ALU.add, [mvt], [mvt])
                ACT(mvt[0:TP, 2:3], mvt[0:TP, 2:3], AF.Sqrt, [mvt], [mvt])
                S.op("dve", lambda e: e.reciprocal(out=mvt[0:TP, 3:4], in_=mvt[0:TP, 2:3]), reads=[mvt], writes=[mvt])
                TS(rbuf[0:TP, :], rbuf[0:TP, :], mvt[0:TP, 0:1], mvt[0:TP, 3:4], ALU.subtract, ALU.mult, [rbuf, mvt], [rbuf])
                TT(rbuf[0:TP, :], rbuf[0:TP, :], lnp[gi_][0:TP, :], ALU.mult, [rbuf, lnp[gi_]], [rbuf])
                TT(dst[0:TP, :], rbuf[0:TP, :], lnp[bi_][0:TP, :], ALU.add, [rbuf, lnp[bi_]], [dst])

            resid_ln(xin, pw, 0, 1, x1)
            if tagdbg is not None and ("x1_%d" % l) in dbg_out:
                DMA(dbg_out["x1_%d" % l][tagdbg:tagdbg + TP, :], x1[0:TP, :], reads=[x1])

            for c4 in range(2):
                p = psr.next()
                for k in range(4):
                    c = c4 * 4 + k
                    TR(p[:, k * 128:k * 128 + TP], x1[0:TP, c * 128:(c + 1) * 128], TP, [x1], [p])
                for k in range(4):
                    c = c4 * 4 + k
                    S.op("act", (lambda e, c=c, k=k, p=p: e.copy(out=x1FMb[:, c, 0:TP], in_=p[:, k * 128:k * 128 + TP])), reads=[p], writes=[x1FMb])
            for hp in range(16):
                w = wl(l, 128 + hp)
                p = psr.next()
                for c in range(8):
                    MM(p[:, 0:nt], w[:, c * 128:(c + 1) * 128], x1FMb[:, c, 0:nt], c == 0, c == 7, [w, x1FMb], [p])
                S.op("act", (lambda e, hp=hp, p=p: e.copy(out=qT[:, hp, 0:nt], in_=p[:, 0:nt])), reads=[p], writes=[qT])
            for g4 in range(4):
                p = psr.next()
                for k in range(4):
                    hp = g4 * 4 + k
                    MM(p[0:TP, k * 128:(k + 1) * 128], qT[:, hp, 0:TP], keysT[l][:, hp, :], True, True, [qT, keysT[l]], [p])
                CP(pscr[0:TP, g4 * 512:(g4 + 1) * 512], p[0:TP, :], [p], [pscr])
            for hp in range(16):
                sl = pscr[0:TP, hp * 128:(hp + 1) * 128]
                S.op("dve", (lambda e, hp=hp, sl=sl: e.max(out=sv[0:TP, hp, 0:8], in_=sl)), reads=[pscr], writes=[sv])
                S.op("dve", (lambda e, hp=hp, sl=sl: e.max_index(out=si[0:TP, hp, 0:8], in_max=sv[0:TP, hp, 0:8], in_values=sl)), reads=[pscr, sv], writes=[si])
                S.op("dve", (lambda e, hp=hp, sl=sl: e.match_replace(out=s2[0:TP, :], in_to_replace=sv[0:TP, hp, 0:8], in_values=sl, imm_value=NEG)), reads=[pscr, sv], writes=[s2])
                S.op("dve", (lambda e, hp=hp: e.max(out=sv[0:TP, hp, 8:16], in_=s2[0:TP, :])), reads=[s2], writes=[sv])
                S.op("dve", (lambda e, hp=hp: e.max_index(out=si[0:TP, hp, 8:16], in_max=sv[0:TP, hp, 8:16], in_values=s2[0:TP, :])), reads=[s2, sv], writes=[si])
            CP(sif[0:TP], si[0:TP], [si], [sif])
            svv = sv[0:TP].rearrange("n (h p) k -> n h p k", p=2)
            cv = cand[0:TP, :].rearrange("n (h i j) -> n h i j", h=8, i=16)
            TT(cv, svv[:, :, 0, :].unsqueeze(3).to_broadcast([TP, 8, 16, 16]), svv[:, :, 1, :].unsqueeze(2).to_broadcast([TP, 8, 16, 16]), ALU.add, [sv], [cand])
            for h in range(8):
                sl = cand[0:TP, h * 256:(h + 1) * 256]
                sl2 = pscr[0:TP, h * 256:(h + 1) * 256]
                S.op("dve", (lambda e, h=h, sl=sl: e.max(out=tops[0:TP, h, 0:8], in_=sl)), reads=[cand], writes=[tops])
                S.op("dve", (lambda e, h=h, sl=sl: e.max_index(out=topp[0:TP, h, 0:8], in_max=tops[0:TP, h, 0:8], in_values=sl)), reads=[cand, tops], writes=[topp])
                S.op("dve", (lambda e, h=h, sl=sl, sl2=sl2: e.match_replace(out=sl2, in_to_replace=tops[0:TP, h, 0:8], in_values=sl, imm_value=NEG)), reads=[cand, tops], writes=[pscr])
                S.op("dve", (lambda e, h=h, sl2=sl2: e.max(out=tops[0:TP, h, 8:16], in_=sl2)), reads=[pscr], writes=[tops])
                S.op("dve", (lambda e, h=h, sl2=sl2: e.max_index(out=topp[0:TP, h, 8:16], in_max=tops[0:TP, h, 8:16], in_values=sl2)), reads=[pscr, tops], writes=[topp])
            S.op("dve", lambda e: e.tensor_single_scalar(out=pij[0:TP, 0], in_=topp[0:TP], scalar=4, op=ALU.logical_shift_right), reads=[topp], writes=[pij])
            S.op("dve", lambda e: e.tensor_single_scalar(out=pij[0:TP, 1], in_=topp[0:TP], scalar=15, op=ALU.bitwise_and), reads=[topp], writes=[pij])
            CP(pijf[0:TP], pij[0:TP], [pij], [pijf])
            sifv = sif[0:TP].rearrange("n (h p) k -> n h p k", p=2)
            oh = qT[0:TP, :, :].rearrange("n a b -> n (a b)")[:, 0:2048].rearrange("n (h k m) -> n h k m", h=8, k=16)
            for pp in range(2):
                TT(oh, iota16[0:TP, :].unsqueeze(1).unsqueeze(1).to_broadcast([TP, 8, 16, 16]), pijf[0:TP, pp].unsqueeze(3).to_broadcast([TP, 8, 16, 16]), ALU.is_equal, [iota16, pijf], [qT])
                TT(oh, oh, sifv[:, :, pp, :].unsqueeze(2).to_broadcast([TP, 8, 16, 16]), ALU.mult, [qT, sif], [qT])
                S.op("dve", (lambda e, pp=pp: e.tensor_reduce(out=esel[0:TP, pp], in_=oh, axis=AX.X, op=ALU.add)), reads=[qT], writes=[esel])
            TT(gate[0:TP], tops[0:TP], tops[0:TP, :, 0:1].to_broadcast([TP, 8, 16]), ALU.subtract, [tops], [gate])
            ACT(gate[0:TP], gate[0:TP], AF.Exp, [gate], [gate])
            S.op("dve", lambda e: e.tensor_reduce(out=small[0:TP, 0:8], in_=gate[0:TP], axis=AX.X, op=ALU.add), reads=[gate], writes=[small])
            S.op("dve", lambda e: e.reciprocal(out=small[0:TP, 8:16], in_=small[0:TP, 0:8]), reads=[small], writes=[small])
            TT(gate[0:TP], gate[0:TP], small[0:TP, 8:16].unsqueeze(2).to_broadcast([TP, 8, 16]), ALU.mult, [gate, small], [gate])
            p = psr.next()
            srcs = (esel[0:TP, 0].rearrange("n h k -> n (h k)"), esel[0:TP, 1].rearrange("n h k -> n (h k)"), gate[0:TP].rearrange("n h k -> n (h k)"))
            for k in range(3 if 'noslot' not in DBGF else 0):
                TR(p[:, k * 128:k * 128 + TP], srcs[k], TP, [esel, gate], [p])
            for k in range(3 if 'noslot' not in DBGF else 0):
                CP(slotT[:, k, 0:TP], p[:, k * 128:k * 128 + TP], [p], [slotT])
            pc = None
            for n in range(TP if 'nob' not in DBGF else 0):
                oi = OIr.next()
                oj = OJr.next()
                TS(oi[:, :], iota128[:, :], slotT[:, 0, n:n + 1], None, ALU.is_equal, None, [iota128, slotT], [oi])
                TS(oj[:, :], iota128[:, :], slotT[:, 1, n:n + 1], slotT[:, 2, n:n + 1], ALU.is_equal, ALU.mult, [iota128, slotT], [oj])
                if n % 4 == 0:
                    pc = psr.next()
                MM(pc[:, (n % 4) * 128:(n % 4 + 1) * 128], oi[:, :], oj[:, :], True, True, [oi, oj], [pc])
                if n % 4 == 3:
                    n0 = n - 3
                    S.op("act", (lambda e, pc=pc, n0=n0: e.copy(out=C_all[:, n0 * 128:(n0 + 4) * 128], in_=pc[:, :])), reads=[pc], writes=[C_all])
            Cv = C_all[:, :].rearrange("p (n j) -> p n j", j=128)
            def ph_stage(j):
                ut = UTr.next()
                DMA(ut[:], UT[l, j], reads=[UTbuf[l][j]], writes=[ut])
                vb = VBr.next()
                DMA(vb[:], VB[l, j], reads=[VBbuf[l][j]], writes=[vb])
                ph = psr.next()
                for c in range(8):
                    MM(ph[:, 0:TP], ut[:, c * 128:(c + 1) * 128], x1FMb[:, c, 0:TP], c == 0, c == 7, [ut, x1FMb], [ph])
                return ph, vb
            cur = ph_stage(0)
            for j in range(128):
                nxt = ph_stage(j + 1) if j + 1 < 128 else None
                ph, vb = cur
                G = Gr.next()
                ACT(G[:, 0:TP], ph[:, 0:TP], AF.Gelu, [ph], [G])
                Wt = Wr.next()
                TT(Wt[:, 0:TP], G[:, 0:TP], Cv[:, 0:TP, j], ALU.mult, [G, C_all], [Wt])
                for nh in range(2):
                    MM(psB[nh][0:TP, :], Wt[:, 0:TP], vb[:, nh * 512:(nh + 1) * 512], j == 0, j == 127, [Wt, vb], [psB[nh]])
                cur = nxt
            for nh in range(2 if 'noc' not in DBGF else 0):
                STT(rbuf[0:TP, nh * 512:(nh + 1) * 512], x1[0:TP, nh * 512:(nh + 1) * 512], ALPHA, psB[nh][0:TP, :], ALU.mult, ALU.add, [x1, psB[nh]], [rbuf])
            ln_from_rbuf(2, 3, xout)

        init_states_zero()
        nblk = T // TB
        xt = [xa, xb]
        for b in range(nblk):
            DMA(xt[0][:], xp[b * TB:(b + 1) * TB, :], writes=[xt[0]])
            for l in range(NL):
                layer_block(l, xt[l % 2], xt[(l + 1) % 2], TB, 64, tagdbg=b * TB)
            DMA(yp[b * TB:(b + 1) * TB, :], xt[NL % 2][:], reads=[xt[NL % 2]])
        out_states(0)
        if with_sample:
            init_states_sample()
            DMA(xt[0][0:32, :], xsm[:, :], writes=[xt[0]])
            for l in range(NL):
                layer_block(l, xt[l % 2], xt[(l + 1) % 2], 32, 32)
            DMA(ys[:, :], xt[NL % 2][0:32, :], reads=[xt[NL % 2]])
            out_states(1)
        with nc.allow_low_precision("bf16 matmul operands for PEER tables"):
            S.run()
    return nc


W_NAMES = ["w_in", "b_in", "conv_a_w", "conv_b_w", "conv_b_b", "lru_wa", "lru_ba", "lru_wx", "lru_bx", "lru_lambda",
           "hgrn_lb_logits", "hgrn_norm_g", "w_out_a", "w_out_b", "w_out_c", "w_o", "ln1_g", "ln1_b", "peer_wq",
           "peer_keys", "peer_u", "peer_v", "ln2_g", "ln2_b"]


def run(inputs, T, NL=DEPTH, with_sample=True, dbg=None, ncores=8):
    import time
    t0 = time.time()
    nc = build(T, NL, with_sample, dbg=dbg)
    print('build time', time.time() - t0, flush=True)
    f = lambda a: np.ascontiguousarray(np.asarray(a, dtype=np.float32))
    wmaps = {k: f(inputs[k]) for k in W_NAMES}
    in_maps = []
    for c in range(ncores):
        m = dict(wmaps)
        m["xp"] = f(inputs["x_prompt"][c % 4, :T])
        m["xs"] = f(inputs["x_sample"][c])
        m["st_ca"] = f(inputs["state_conv_a"][:, c])
        m["st_cb"] = f(inputs["state_conv_b"][:, c])
        m["st_h"] = f(inputs["state_lru"][:, c])
        m["st_S"] = f(inputs["state_hgrn"][:, c])
        in_maps.append(m)
    res = run_bass_kernel_spmd(nc, in_maps, core_ids=list(range(ncores)))
    print('total time', time.time() - t0, flush=True)
    return res.results


def kernel(**inputs):
    T = inputs["x_prompt"].shape[1]
    r = run(inputs, T)
    yp = np.stack([r[c]["yp"] for c in range(4)], 0)
    ys = np.stack([r[c]["ys"] for c in range(8)], 0)
    def pst(name):
        return np.stack([r[c][name] for c in range(4)], 1)
    def sst(name):
        return np.stack([r[c][name] for c in range(8)], 1)
    return (yp, ys, pst("p_ca"), pst("p_cb"), pst("p_h"), pst("p_S"),
            sst("s_ca"), sst("s_cb"), sst("s_h"), sst("s_S"))
```

```python
import numpy as np
from contextlib import ExitStack
import concourse.bass as bass
import concourse.mybir as mybir
from concourse.bass_utils import run_bass_kernel_spmd

F32 = mybir.dt.float32
BF16 = mybir.dt.bfloat16
U32 = mybir.dt.uint32
I32 = mybir.dt.int32
AF = mybir.ActivationFunctionType
ALU = mybir.AluOpType
AX = mybir.AxisListType

D = 1024
NCB = 8
DEPTH = 2
ALPHA = (2.0 * DEPTH) ** 0.25
LN_EPS = 1e-5
RMS_EPS = 1e-6
NEG = -1.0e30


class Tl:
    def __init__(self, t, name=""):
        self.t = t
        self.name = name
        self.last_w = None
        self.readers = []

    def __getitem__(self, idx):
        return self.t[idx]


class Alias:
    def __init__(self, base, ap):
        self.base = base
        self.t = ap

    def __getitem__(self, idx):
        return self.t[idx]

    last_w = property(lambda s: s.base.last_w, lambda s, v: setattr(s.base, "last_w", v))
    readers = property(lambda s: s.base.readers, lambda s, v: setattr(s.base, "readers", v))


class Op:
    __slots__ = ("eng", "fn", "deps", "signal", "sem", "semval", "dma", "idx", "eidx")


class Sched:
    ENG = ("pe", "act", "dve", "pool", "sp")

    def __init__(self, nc, stack, n_dma_sems=16):
        self.nc = nc
        self.ops = []
        self.per_eng = {e: [] for e in self.ENG}
        self.esem = {e: stack.enter_context(nc.semaphore("es_" + e)) for e in self.ENG}
        self.dsems = {}
        self.dcnt = {}
        self.drr = {}
        for e in ("sp", "pool"):
            self.dsems[e] = [stack.enter_context(nc.semaphore("ds_%s%d" % (e, i))) for i in range(n_dma_sems)]
            self.dcnt[e] = [0] * n_dma_sems
            self.drr[e] = 0
        self.dlast = {}
        self.stack = stack

    def sb(self, name, shape, dt=F32):
        n = 1
        for d_ in shape[1:]:
            n *= d_
        self.sbytes = getattr(self, "sbytes", 0) + n * (2 if dt == BF16 else 4)
        t = self.stack.enter_context(self.nc.sbuf_tensor(name, list(shape), dt))
        return Tl(t, name)

    def ps(self, name, shape, dt=F32):
        t = self.stack.enter_context(self.nc.psum_tensor(name, list(shape), dt))
        return Tl(t, name)

    def op(self, eng, fn, reads=(), writes=(), dma=False):
        o = Op()
        o.eng = eng
        o.fn = fn
        o.dma = dma
        o.signal = False
        o.sem = None
        o.semval = None
        o.idx = len(self.ops)
        o.eidx = len(self.per_eng[eng])
        deps = []
        for r in reads:
            if r is not None and r.last_w is not None:
                deps.append(r.last_w)
        for w in writes:
            if w is None:
                continue
            if w.last_w is not None:
                deps.append(w.last_w)
            deps.extend(w.readers)
        if dma:
            k = self.drr[eng]
            self.drr[eng] = (k + 1) % len(self.dsems[eng])
            prev = self.dlast.get((eng, k))
            if prev is not None:
                deps.append(prev)
            self.dcnt[eng][k] += 16
            o.sem = self.dsems[eng][k]
            o.semval = self.dcnt[eng][k]
            self.dlast[(eng, k)] = o
            o.signal = True
        seen = set()
        dd = []
        for d in deps:
            if d is o or id(d) in seen:
                continue
            seen.add(id(d))
            if (not d.dma) and d.eng == "pe" and eng == "pe" and not dma:
                continue
            dd.append(d)
            if not d.dma:
                d.signal = True
        o.deps = dd
        for r in reads:
            if r is not None:
                r.readers.append(o)
        for w in writes:
            if w is not None:
                w.last_w = o
                w.readers = []
        self.ops.append(o)
        self.per_eng[eng].append(o)
        return o

    def finalize(self):
        for e in self.ENG:
            c = 0
            for o in self.per_eng[e]:
                if o.dma:
                    continue
                if o.signal:
                    c += 1
                    o.sem = self.esem[e]
                    o.semval = c

    def emit_engine(self, ename, eng):
        seen = {}
        for o in self.per_eng[ename]:
            need = {}
            for d in o.deps:
                if (not d.dma) and (not o.dma) and d.eng == ename and ename in ("act", "dve") and o.eidx - d.eidx >= 4:
                    continue
                key = id(d.sem)
                if key not in need or need[key][1] < d.semval:
                    need[key] = (d.sem, d.semval)
            for key, (sem, val) in need.items():
                if seen.get(key, 0) >= val:
                    continue
                eng.wait_ge(sem, val)
                seen[key] = val
            inst = o.fn(eng)
            if o.signal:
                inst.then_inc(o.sem, 16 if o.dma else 1)
        if ename in self.dsems:
            for k, s in enumerate(self.dsems[ename]):
                v = self.dcnt[ename][k]
                if v > 0 and seen.get(id(s), 0) < v:
                    eng.wait_ge(s, v)

    def run(self):
        self.finalize()
        S = self
        with self.nc.Block() as block:
            @block.tensor
            def _(e):
                S.emit_engine("pe", e)

            @block.scalar
            def _(e):
                S.emit_engine("act", e)

            @block.vector
            def _(e):
                S.emit_engine("dve", e)

            @block.gpsimd
            def _(e):
                S.emit_engine("pool", e)

            @block.sync
            def _(e):
                S.emit_engine("sp", e)


class Ring:
    def __init__(self, tiles):
        self.tiles = tiles
        self.i = 0

    def next(self):
        t = self.tiles[self.i]
        self.i = (self.i + 1) % len(self.tiles)
        return t


def build(T, NL=DEPTH, with_sample=True, NWR=3, NGR=3, dbg=None):
    import os
    DBGF = os.environ.get('KDBG', '')
    TB = 128
    nc = bass.Bass("TRN2", target_bir_lowering=False)

    def din(name, shape, dt=F32):
        return nc.dram_tensor(name, list(shape), dt, kind="ExternalInput").ap()

    def dout(name, shape, dt=F32):
        return nc.dram_tensor(name, list(shape), dt, kind="ExternalOutput").ap()

    xp = din("xp", [T, D])
    xsm = din("xs", [32, D])
    st_ca = din("st_ca", [DEPTH, 2, D])
    st_cb = din("st_cb", [DEPTH, 3, D])
    st_h = din("st_h", [DEPTH, D])
    st_S = din("st_S", [DEPTH, 8, 128, 128])
    w_in = din("w_in", [DEPTH, D, 12 * D])
    b_in = din("b_in", [DEPTH, 12 * D])
    conv_a_w = din("conv_a_w", [DEPTH, 3, D])
    conv_b_w = din("conv_b_w", [DEPTH, 4, D])
    conv_b_b = din("conv_b_b", [DEPTH, D])
    lru_wa = din("lru_wa", [DEPTH, 8, 128, 128])
    lru_ba = din("lru_ba", [DEPTH, D])
    lru_wx = din("lru_wx", [DEPTH, 8, 128, 128])
    lru_bx = din("lru_bx", [DEPTH, D])
    lru_lambda = din("lru_lambda", [DEPTH, D])
    lb_logits = din("hgrn_lb_logits", [DEPTH, D])
    norm_g = din("hgrn_norm_g", [DEPTH, 128])
    w_out = [din("w_out_a", [DEPTH, D, D]), din("w_out_b", [DEPTH, D, D]), din("w_out_c", [DEPTH, D, D])]
    w_o = din("w_o", [DEPTH, D, D])
    ln1_g = din("ln1_g", [DEPTH, D])
    ln1_b = din("ln1_b", [DEPTH, D])
    peer_wq = din("peer_wq", [DEPTH, D, 2048])
    peer_keys = din("peer_keys", [DEPTH, 8, 2, 128, 128])
    peer_u = din("peer_u", [DEPTH, 16384, D])
    peer_v = din("peer_v", [DEPTH, 16384, D])
    peer_u2 = peer_u.rearrange("l e d -> (l e) d")
    peer_v2 = peer_v.rearrange("l e d -> (l e) d")
    UT = nc.dram_tensor("UT_scr", [DEPTH, 128, 128, 1024], BF16, kind="Internal").ap()
    VB = nc.dram_tensor("VB_scr", [DEPTH, 128, 128, 1024], BF16, kind="Internal").ap()
    WS = nc.dram_tensor("WS_scr", [DEPTH, 144, 128, 1024], BF16, kind="Internal").ap()
    ln2_g = din("ln2_g", [DEPTH, D])
    ln2_b = din("ln2_b", [DEPTH, D])

    yp = dout("yp", [T, D])
    ys = dout("ys", [32, D])
    o_ca = [dout("p_ca", [DEPTH, 2, D]), dout("s_ca", [DEPTH, 2, D])]
    o_cb = [dout("p_cb", [DEPTH, 3, D]), dout("s_cb", [DEPTH, 3, D])]
    o_h = [dout("p_h", [DEPTH, D]), dout("s_h", [DEPTH, D])]
    o_S = [dout("p_S", [DEPTH, 8, 128, 128]), dout("s_S", [DEPTH, 8, 128, 128])]
    dbg_out = {}
    if dbg:
        for nm, shp in dbg.items():
            dbg_out[nm] = dout("dbg_" + nm, shp)

    with ExitStack() as st:
        S = Sched(nc, st)

        def DMA(out_ap, in_ap, reads=(), writes=(), slow=False):
            if slow:
                S.op("sp", lambda e: e.dma_start(out=out_ap, in_=in_ap, allow_slow_non_contiguous=True), reads=reads, writes=writes, dma=True)
            else:
                S.op("sp", lambda e: e.dma_start(out=out_ap, in_=in_ap), reads=reads, writes=writes, dma=True)

        def ACT(out_ap, in_ap, func, reads, writes, bias=None, scale=None):
            kw = {}
            if bias is not None:
                kw["bias"] = bias
            if scale is not None:
                kw["scale"] = scale
            S.op("act", lambda e: e.activation(out=out_ap, in_=in_ap, func=func, **kw), reads=reads, writes=writes)

        def TS(out_ap, in_ap, s1, s2, op0, op1, reads, writes, eng="dve"):
            if op1 is None:
                S.op(eng, lambda e: e.tensor_scalar(out=out_ap, in0=in_ap, scalar1=s1, scalar2=None, op0=op0), reads=reads, writes=writes)
            else:
                S.op(eng, lambda e: e.tensor_scalar(out=out_ap, in0=in_ap, scalar1=s1, scalar2=s2, op0=op0, op1=op1), reads=reads, writes=writes)

        def TT(out_ap, a_ap, b_ap, op, reads, writes, eng="dve"):
            S.op(eng, lambda e: e.tensor_tensor(out=out_ap, in0=a_ap, in1=b_ap, op=op), reads=reads, writes=writes)

        def STT(out_ap, in0, scalar, in1, op0, op1, reads, writes, accum=None, eng="dve"):
            if accum is None:
                S.op(eng, lambda e: e.scalar_tensor_tensor(out=out_ap, in0=in0, scalar=scalar, in1=in1, op0=op0, op1=op1), reads=reads, writes=writes)
            else:
                S.op(eng, lambda e: e.scalar_tensor_tensor(out=out_ap, in0=in0, scalar=scalar, in1=in1, op0=op0, op1=op1, accum_out=accum), reads=reads, writes=writes)

        def CP(out_ap, in_ap, reads, writes, eng="dve"):
            S.op(eng, lambda e: e.tensor_copy(out=out_ap, in_=in_ap), reads=reads, writes=writes)

        def MSET(ap, val, writes, eng="pool"):
            S.op(eng, lambda e: e.memset(ap, val), writes=writes)

        def MM(out_ap, lhsT, rhs, start, stop, reads, writes):
            S.op("pe", lambda e: e.matmul(out_ap, lhsT=lhsT, rhs=rhs, start=start, stop=stop), reads=reads, writes=writes)

        def TR(out_ap, in_ap, k, reads, writes):
            S.op("pe", lambda e: e.transpose(out_ap, in_ap, ident[0:k, 0:k]), reads=list(reads) + [ident], writes=writes)

        ident = S.sb("ident", [128, 128])
        ones = S.sb("ones", [128, 128])
        tri = S.sb("tri", [64, 64])
        cmask = {64: S.sb("cmask64", [128, TB]), 32: S.sb("cmask32", [128, TB])}
        iota16 = S.sb("iota16", [128, 16])
        iota_i = S.sb("iota_i", [128, 16], I32)
        psr = Ring([S.ps("ps%d" % i, [128, 512]) for i in range(4)])
        psB = [S.ps("psB%d" % i, [128, 512]) for i in range(4)]
        wring = Ring([S.sb("wr%d" % i, [128, 1024]) for i in range(NWR)])
        zps = [Ring([S.sb("z%d_%d" % (k, i), [128, TB + 4]) for i in range(24)]) for k in range(2)]
        zr = zps[0]
        xA = [S.sb("xa%d" % i, [128, D]) for i in range(2)]
        xB = [S.sb("xb%d" % i, [128, D]) for i in range(2)]
        x1s = [S.sb("x1_%d" % i, [128, D]) for i in range(2)]
        rbuf = S.sb("rbuf", [128, D])
        aABC = S.sb("aABC", [128, 8, 3, TB], BF16)
        aV = [[Tl(aABC.t) for _ in range(3)] for _ in range(8)]
        merged = S.sb("merged", [128, 8, TB], BF16)
        s2 = S.sb("s2", [128, 128])
        C_all = S.sb("C_all", [128, 128 * 128], BF16)
        if 'noalias' in DBGF:
            pscr = S.sb("pscr", [128, 2048]); cand = S.sb("cand", [128, 2048]); qT = S.sb("qT", [128, 16, TB])
        else:
            pscr = Alias(C_all, C_all[:, 0:4096].bitcast(F32))
            cand = Alias(C_all, C_all[:, 4096:8192].bitcast(F32))
            qT = Alias(C_all, C_all[:, 8192:12288].bitcast(F32).rearrange("p (a b) -> p a b", b=128))
        x1FMb = S.sb("x1FMb", [128, 8, 2 * TB], BF16)
        iota128 = S.sb("iota128", [128, 128])
        iota128i = S.sb("iota128i", [128, 128], I32)
        slotT = S.sb("slotT", [128, 3, 256])
        OIr = Ring([S.sb("oi%d" % i, [128, 128], BF16) for i in range(4)])
        OJr = Ring([S.sb("oj%d" % i, [128, 128], BF16) for i in range(4)])
        UTr = Ring([S.sb("utr%d" % i, [128, 1024], BF16) for i in range(3)])
        VBr = Ring([S.sb("vbr%d" % i, [128, 1024], BF16) for i in range(4)])
        Gr = Ring([S.sb("gg%d" % i, [128, 256]) for i in range(3)])
        Wr = Ring([S.sb("wt%d" % i, [128, 256], BF16) for i in range(3)])
        wbr = Ring([S.sb("wbr%d" % i, [128, 1024], BF16) for i in range(5)])
        WSbuf = [[Tl(None) for _ in range(144)] for _ in range(NL)]
        xFMb = S.sb("xFMb", [128, 8, TB], BF16)
        UTbuf = [[Tl(None) for _ in range(128)] for _ in range(NL)]
        VBbuf = [[Tl(None) for _ in range(128)] for _ in range(NL)]
        lnp = [S.sb("lnp%d" % i, [128, D]) for i in range(4)]
        keysT = [S.sb("keysT%d" % l, [128, 16, 128]) for l in range(NL)]
        PA = [S.sb("PA%d" % l, [128, 96]) for l in range(NL)]
        PB = [S.sb("PB%d" % l, [128, 128]) for l in range(NL)]
        PD = [S.sb("PD%d" % l, [128, 4, 8]) for l in range(NL)]
        stg = S.sb("stg", [128, 128])
        Sst = [S.sb("Sst%d" % l, [128, 8, 128]) for l in range(NL)]
        SstV = [[Tl(Sst[l].t) for _ in range(8)] for l in range(NL)]
        cAst = [S.sb("cAst%d" % l, [128, 8, 2]) for l in range(NL)]
        cAV = [[Tl(cAst[l].t) for _ in range(8)] for l in range(NL)]
        cBst = [S.sb("cBst%d" % l, [128, 8, 3]) for l in range(NL)]
        cBV = [[Tl(cBst[l].t) for _ in range(8)] for l in range(NL)]
        hst = [S.sb("hst%d" % l, [128, 8]) for l in range(NL)]
        hV = [[Tl(hst[l].t) for _ in range(8)] for l in range(NL)]
        small = S.sb("small", [128, 64])
        sv = S.sb("sv", [128, 16, 16])
        si = S.sb("si", [128, 16, 16], U32)
        sif = S.sb("sif", [128, 16, 16])
        tops = S.sb("tops", [128, 8, 16])
        topp = S.sb("topp", [128, 8, 16], U32)
        pij = S.sb("pij", [128, 2, 8, 16], U32)
        pijf = S.sb("pijf", [128, 2, 8, 16])
        esel = S.sb("esel", [128, 2, 8, 16])
        eidf = S.sb("eidf", [128, 128])
        eid = S.sb("eid", [128, 128], U32)
        gate = S.sb("gate", [128, 8, 16])
        mvt = S.sb("mvt", [128, 16])
        bst = S.sb("bst", [128, 12])

        print('SBUF bytes/partition', S.sbytes, flush=True)
        MSET(ident[:], 0.0, [ident])
        S.op("pool", lambda e: e.affine_select(out=ident[:], in_=ident[:], pattern=[[-1, 128]], compare_op=ALU.not_equal, fill=1.0, base=0, channel_multiplier=1), reads=[ident], writes=[ident])
        MSET(ones[:], 1.0, [ones])
        MSET(tri[:], 1.0, [tri])
        S.op("pool", lambda e: e.affine_select(out=tri[:], in_=tri[:], pattern=[[1, 64]], compare_op=ALU.is_ge, fill=0.0, base=0, channel_multiplier=-1), reads=[tri], writes=[tri])
        for L in (64, 32):
            MSET(cmask[L][:], 1.0, [cmask[L]])
            for c in range(TB // L):
                MSET(cmask[L][:, c * L:c * L + 1], 0.0, [cmask[L]])
        S.op("pool", lambda e: e.iota(iota_i[:], pattern=[[1, 16]], base=0, channel_multiplier=0), writes=[iota_i])
        CP(iota16[:], iota_i[:], [iota_i], [iota16])
        S.op("pool", lambda e: e.iota(iota128i[:], pattern=[[1, 128]], base=0, channel_multiplier=0), writes=[iota128i])
        CP(iota128[:], iota128i[:], [iota128i], [iota128])

        for l in range(NL):
            MSET(stg[:], 0.0, [stg])
            DMA(stg[0:96, :], b_in[l].rearrange("(r p) -> r p", p=128), writes=[stg])
            p = psr.next()
            TR(p[:, 0:128], stg[:], 128, [stg], [p])
            CP(PA[l][:], p[:, 0:96], [p], [PA[l]])
            MSET(stg[:], 0.0, [stg])
            DMA(stg[0:24, :], conv_a_w[l].rearrange("k (c p) -> (k c) p", p=128), writes=[stg])
            DMA(stg[24:56, :], conv_b_w[l].rearrange("k (c p) -> (k c) p", p=128), writes=[stg])
            DMA(stg[56:64, :], conv_b_b[l].rearrange("(c p) -> c p", p=128), writes=[stg])
            DMA(stg[64:72, :], lru_ba[l].rearrange("(c p) -> c p", p=128), writes=[stg])
            DMA(stg[72:80, :], lru_bx[l].rearrange("(c p) -> c p", p=128), writes=[stg])
            DMA(stg[80:88, :], lru_lambda[l].rearrange("(c p) -> c p", p=128), writes=[stg])
            DMA(stg[88:96, :], lb_logits[0].rearrange("(c p) -> c p", p=128), writes=[stg])
            DMA(stg[96:104, :], lb_logits[1].rearrange("(c p) -> c p", p=128), writes=[stg])
            DMA(stg[104:105, :], norm_g[l:l + 1, :], writes=[stg])
            p = psr.next()
            TR(p[:, 0:128], stg[:], 128, [stg], [p])
            CP(PB[l][:], p[:, 0:128], [p], [PB[l]])
            ACT(PD[l][:, 0, :], PB[l][:, 80:88], AF.Exp, [PB[l]], [PD[l]], scale=-1.0)
            ACT(PD[l][:, 0, :], PD[l][:, 0, :], AF.Ln, [PD[l]], [PD[l]], bias=1.0)
            TS(PD[l][:, 1, :], PD[l][:, 0, :], -16.0, None, ALU.mult, None, [PD[l]], [PD[l]])
            TS(PD[l][:, 0, :], PD[l][:, 0, :], -8.0, None, ALU.mult, None, [PD[l]], [PD[l]])
            if l == 0:
                MSET(PD[l][:, 2, :], 0.0, [PD[l]], eng="dve")
                MSET(PD[l][:, 3, :], 1.0, [PD[l]], eng="dve")
            else:
                TT(PD[l][:, 2, :], PB[l][:, 96:104], PB[l][:, 88:96], ALU.subtract, [PB[l]], [PD[l]])
                ACT(PD[l][:, 2, :], PD[l][:, 2, :], AF.Sigmoid, [PD[l]], [PD[l]])
                TS(PD[l][:, 3, :], PD[l][:, 2, :], -1.0, 1.0, ALU.mult, ALU.add, [PD[l]], [PD[l]])
            for hp in range(16):
                w = wring.next()
                DMA(w[:, 0:128], peer_keys[l, hp // 2, hp % 2], writes=[w])
                p = psr.next()
                TR(p[:, 0:128], w[:, 0:128], 128, [w], [p])
                CP(keysT[l][:, hp, :], p[:, 0:128], [p], [keysT[l]], eng="dve")

        import os
        DBGF = os.environ.get('KDBG', '')
        for l in range(NL if 'noprep' not in DBGF else 0):
            ug = peer_u[l].rearrange("(i j) d -> j i d", j=128)
            vg = peer_v[l].rearrange("(i j) d -> j i d", j=128)
            for j in range(128):
                g1 = wring.next()
                DMA(g1[:], ug[j], writes=[g1])
                ut = UTr.next()
                for c4 in range(2):
                    p = psr.next()
                    for k in range(4):
                        c = c4 * 4 + k
                        TR(p[:, k * 128:(k + 1) * 128], g1[:, c * 128:(c + 1) * 128], 128, [g1], [p])
                    S.op("act", (lambda e, ut=ut, p=p, c4=c4: e.copy(out=ut[:, c4 * 512:(c4 + 1) * 512], in_=p[:, :])), reads=[p], writes=[ut])
                DMA(UT[l, j], ut[:], reads=[ut], writes=[UTbuf[l][j]])
                g2 = wring.next()
                DMA(g2[:], vg[j], writes=[g2])
                vb = VBr.next()
                CP(vb[:], g2[:], [g2], [vb], eng="pool")
                DMA(VB[l, j], vb[:], reads=[vb], writes=[VBbuf[l][j]])

        def init_states_zero():
            for l in range(NL):
                MSET(Sst[l][:], 0.0, SstV[l])
                MSET(cAst[l][:], 0.0, cAV[l])
                MSET(cBst[l][:], 0.0, cBV[l])
                MSET(hst[l][:], 0.0, hV[l])

        def init_states_sample():
            for l in range(NL):
                DMA(Sst[l][:], st_S[l].rearrange("h d e -> d h e"), writes=SstV[l])
                for cb in range(8):
                    DMA(cAst[l][:, cb, :], st_ca[l][:, cb * 128:(cb + 1) * 128].rearrange("k p -> p k"), writes=[cAV[l][cb]], slow=True)
                    DMA(cBst[l][:, cb, :], st_cb[l][:, cb * 128:(cb + 1) * 128].rearrange("k p -> p k"), writes=[cBV[l][cb]], slow=True)
                DMA(hst[l][:], st_h[l].rearrange("(c p) -> p c", p=128), writes=hV[l], slow=True)

        def out_states(which):
            for l in range(NL):
                DMA(o_S[which][l].rearrange("h d e -> d h e"), Sst[l][:], reads=SstV[l])
                for cb in range(8):
                    DMA(o_ca[which][l][:, cb * 128:(cb + 1) * 128].rearrange("k p -> p k"), cAst[l][:, cb, :], reads=[cAV[l][cb]], slow=True)
                    DMA(o_cb[which][l][:, cb * 128:(cb + 1) * 128].rearrange("k p -> p k"), cBst[l][:, cb, :], reads=[cBV[l][cb]], slow=True)
                DMA(o_h[which][l].rearrange("(c p) -> p c", p=128), hst[l][:], reads=hV[l], slow=True)

        def wsrc(l, idx):
            if idx < 96:
                return w_in[l][:, idx * 128:(idx + 1) * 128].rearrange("(c p) n -> p c n", p=128), True
            if idx < 120:
                X, j = (idx - 96) // 8, (idx - 96) % 8
                return w_out[X][l][:, j * 128:(j + 1) * 128].rearrange("(c p) n -> p c n", p=128), True
            if idx < 128:
                j = idx - 120
                return w_o[l, j * 128:(j + 1) * 128, :], False
            hp = idx - 128
            return peer_wq[l][:, hp * 128:(hp + 1) * 128].rearrange("(c p) n -> p c n", p=128), True

        for l in range(NL):
            for idx in range(144):
                src, tiled = wsrc(l, idx)
                w = wring.next()
                if tiled:
                    DMA(w[:].rearrange("p (c n) -> p c n", n=128), src, writes=[w])
                else:
                    DMA(w[:], src, writes=[w])
                wb = wbr.next()
                CP(wb[:], w[:], [w], [wb], eng=("pool" if idx % 2 else "dve"))
                DMA(WS[l, idx], wb[:], reads=[wb], writes=[WSbuf[l][idx]])

        def wl(l, idx):
            wb = wbr.next()
            DMA(wb[:], WS[l, idx], reads=[WSbuf[l][idx]], writes=[wb])
            return wb

        def in_proj(l, idx, nt):
            w = wl(l, idx)
            p = psr.next()
            for c in range(8):
                MM(p[:, 0:nt], w[:, c * 128:(c + 1) * 128], xFMb[:, c, 0:nt], c == 0, c == 7, [w, xFMb], [p])
            return p

        def ln_from_rbuf(TP, gi_, bi_, dst):
            for nh in range(2):
                S.op("dve", (lambda e, nh=nh: e.bn_stats(out=bst[0:TP, nh * 6:(nh + 1) * 6], in_=rbuf[0:TP, nh * 512:(nh + 1) * 512])), reads=[rbuf], writes=[bst])
            S.op("dve", lambda e: e.bn_aggr(out=mvt[0:TP, 0:2], in_=bst[0:TP, 0:12]), reads=[bst], writes=[mvt])
            TS(mvt[0:TP, 2:3], mvt[0:TP, 1:2], 1.0, LN_EPS, ALU.mult, ALU.add, [mvt], [mvt])
            ACT(mvt[0:TP, 2:3], mvt[0:TP, 2:3], AF.Sqrt, [mvt], [mvt])
            S.op("dve", lambda e: e.reciprocal(out=mvt[0:TP, 3:4], in_=mvt[0:TP, 2:3]), reads=[mvt], writes=[mvt])
            TS(rbuf[0:TP, :], rbuf[0:TP, :], mvt[0:TP, 0:1], mvt[0:TP, 3:4], ALU.subtract, ALU.mult, [rbuf, mvt], [rbuf])
            TT(rbuf[0:TP, :], rbuf[0:TP, :], lnp[gi_][0:TP, :], ALU.mult, [rbuf, lnp[gi_]], [rbuf])
            TT(dst[0:TP, :], rbuf[0:TP, :], lnp[bi_][0:TP, :], ALU.add, [rbuf, lnp[bi_]], [dst])

        def mix_topk(l, xin, x1, nt, L, po, tagdbg=None):
            TP = nt
            nch = nt // L
            for i, src in enumerate((ln1_g, ln1_b, ln2_g, ln2_b)):
                DMA(lnp[i][:], src[l:l + 1, :].to_broadcast([128, D]), writes=[lnp[i]])
            for c4 in range(2):
                p = psr.next()
                for k in range(4):
                    c = c4 * 4 + k
                    TR(p[:, k * 128:k * 128 + TP], xin[0:TP, c * 128:(c + 1) * 128], TP, [xin], [p])
                for k in range(4):
                    c = c4 * 4 + k
                    S.op("act", (lambda e, c=c, k=k, p=p: e.copy(out=xFMb[:, c, 0:TP], in_=p[:, k * 128:k * 128 + TP])), reads=[p], writes=[xFMb])
            def cb_body(cb, zp):
                def zin(G, func):
                    p = in_proj(l, G * 8 + cb, nt)
                    z = zp.next()
                    ACT(z[:, 0:nt], p[:, 0:nt], func, [p, PA[l]], [z], bias=PA[l][:, G * 8 + cb:G * 8 + cb + 1])
                    return z
                zB = zin(0, AF.Identity)
                yield
                zC = zin(1, AF.Identity)
                yield
                zxA = zin(2, AF.Identity)
                yield
                ub = zp.next()
                yield
                CP(ub[:, 0:2], cAst[l][:, cb, :], [cAV[l][cb]], [ub], eng="pool")
                yield
                TT(ub[:, 2:2 + nt], zC[:, 0:nt], zxA[:, 0:nt], ALU.mult, [zC, zxA], [ub])
                yield
                CP(cAst[l][:, cb, :], ub[:, nt:nt + 2], [ub], [cAV[l][cb]], eng="pool")
                yield
                y = zp.next()
                yield
                TS(y[:, 0:nt], ub[:, 0:nt], PB[l][:, cb:cb + 1], None, ALU.mult, None, [ub, PB[l]], [y])
                yield
                STT(y[:, 0:nt], ub[:, 1:1 + nt], PB[l][:, 8 + cb:9 + cb], y[:, 0:nt], ALU.mult, ALU.add, [ub, y, PB[l]], [y])
                yield
                STT(y[:, 0:nt], ub[:, 2:2 + nt], PB[l][:, 16 + cb:17 + cb], y[:, 0:nt], ALU.mult, ALU.add, [ub, y, PB[l]], [y])
                yield
                TT(aABC[:, cb, 0, 0:nt], zB[:, 0:nt], y[:, 0:nt], ALU.mult, [zB, y], [aV[cb][0]])
                yield
                zxB = zin(3, AF.Identity)
                yield
                ggB = zin(4, AF.Gelu)
                yield
                cbuf = zp.next()
                yield
                CP(cbuf[:, 0:3], cBst[l][:, cb, :], [cBV[l][cb]], [cbuf], eng="pool")
                yield
                CP(cbuf[:, 3:3 + nt], zxB[:, 0:nt], [zxB], [cbuf], eng="pool")
                yield
                CP(cBst[l][:, cb, :], cbuf[:, nt:nt + 3], [cbuf], [cBV[l][cb]], eng="pool")
                yield
                xl = zp.next()
                yield
                TS(xl[:, 0:nt], cbuf[:, 0:nt], PB[l][:, 24 + cb:25 + cb], PB[l][:, 56 + cb:57 + cb], ALU.mult, ALU.add, [cbuf, PB[l]], [xl])
                yield
                for k in range(1, 4):
                    STT(xl[:, 0:nt], cbuf[:, k:k + nt], PB[l][:, 24 + 8 * k + cb:25 + 8 * k + cb], xl[:, 0:nt], ALU.mult, ALU.add, [cbuf, xl, PB[l]], [xl])
                    yield
                w = wring.next()
                yield
                DMA(w[:, 0:128], lru_wa[l, cb], writes=[w])
                yield
                DMA(w[:, 128:256], lru_wx[l, cb], writes=[w])
                yield
                p = psr.next()
                yield
                MM(p[:, 0:nt], w[:, 0:128], xl[:, 0:nt], True, True, [w, xl], [p])
                yield
                MM(p[:, 128:128 + nt], w[:, 128:256], xl[:, 0:nt], True, True, [w, xl], [p])
                yield
                r = zp.next()
                yield
                gi = zp.next()
                yield
                ACT(r[:, 0:nt], p[:, 0:nt], AF.Sigmoid, [p, PB[l]], [r], bias=PB[l][:, 64 + cb:65 + cb])
                yield
                ACT(gi[:, 0:nt], p[:, 128:128 + nt], AF.Sigmoid, [p, PB[l]], [gi], bias=PB[l][:, 72 + cb:73 + cb])
                yield
                a = zp.next()
                yield
                a2 = zp.next()
                yield
                ACT(a[:, 0:nt], r[:, 0:nt], AF.Exp, [r, PD[l]], [a], scale=PD[l][:, 0, cb:cb + 1])
                yield
                ACT(a2[:, 0:nt], r[:, 0:nt], AF.Exp, [r, PD[l]], [a2], scale=PD[l][:, 1, cb:cb + 1])
                yield
                TS(a2[:, 0:nt], a2[:, 0:nt], -1.0, 1.0, ALU.mult, ALU.add, [a2], [a2])
                yield
                ACT(a2[:, 0:nt], a2[:, 0:nt], AF.Sqrt, [a2], [a2])
                yield
                TT(gi[:, 0:nt], gi[:, 0:nt], a2[:, 0:nt], ALU.mult, [gi, a2], [gi])
                yield
                TT(gi[:, 0:nt], gi[:, 0:nt], xl[:, 0:nt], ALU.mult, [gi, xl], [gi])
                yield
                hh = zp.next()
                yield
                S.op("dve", (lambda e, hh=hh, a=a, gi=gi, cb=cb: e.tensor_tensor_scan(out=hh[:, 0:nt], data0=a[:, 0:nt], data1=gi[:, 0:nt], initial=hst[l][:, cb:cb + 1], op0=ALU.mult, op1=ALU.add)), reads=[a, gi, hV[l][cb]], writes=[hh])
                yield
                CP(hst[l][:, cb:cb + 1], hh[:, nt - 1:nt], [hh], [hV[l][cb]], eng="pool")
                yield
                TT(aABC[:, cb, 1, 0:nt], ggB[:, 0:nt], hh[:, 0:nt], ALU.mult, [ggB, hh], [aV[cb][1]])
                yield
                q = zin(5, AF.Silu)
                yield
                f = zin(6, AF.Sigmoid)
                yield
                vv = zin(7, AF.Identity)
                yield
                sg = zin(8, AF.Silu)
                yield
                TS(f[:, 0:nt], f[:, 0:nt], PD[l][:, 3, cb:cb + 1], PD[l][:, 2, cb:cb + 1], ALU.mult, ALU.add, [f, PD[l]], [f])
                yield
                kk = zp.next()
                yield
                TS(kk[:, 0:nt], f[:, 0:nt], -1.0, 1.0, ALU.mult, ALU.add, [f], [kk])
                yield
                lf = zp.next()
                yield
                ACT(lf[:, 0:nt], f[:, 0:nt], AF.Ln, [f], [lf])
                yield
                bb = zp.next()
                yield
                S.op("dve", (lambda e, bb=bb, lf=lf: e.tensor_tensor_scan(out=bb[:, 0:nt], data0=cmask[L][:, 0:nt], data1=lf[:, 0:nt], initial=0.0, op0=ALU.mult, op1=ALU.add)), reads=[lf, cmask[L]], writes=[bb])
                yield
                eb = zp.next()
                yield
                ACT(eb[:, 0:nt], bb[:, 0:nt], AF.Exp, [bb], [eb])
                yield
                ACT(bb[:, 0:nt], bb[:, 0:nt], AF.Exp, [bb], [bb], scale=-1.0)
                yield
                TT(q[:, 0:nt], q[:, 0:nt], eb[:, 0:nt], ALU.mult, [q, eb], [q])
                yield
                TT(kk[:, 0:nt], kk[:, 0:nt], bb[:, 0:nt], ALU.mult, [kk, bb], [kk])
                yield
                osb = zp.next()
                yield
                Sv = SstV[l][cb]
                yield
                for c in range(nch):
                    c0 = c * L
                    yield
                    pa = psr.next()
                    yield
                    MM(pa[0:L, 0:L], kk[:, c0:c0 + L], q[:, c0:c0 + L], True, True, [kk, q], [pa])
                    yield
                    att = zp.next()
                    yield
                    TT(att[0:L, 0:L], pa[0:L, 0:L], tri[0:L, 0:L], ALU.mult, [pa, tri], [att])
                    yield
                    pv = psr.next()
                    yield
                    TR(pv[0:L, 0:128], vv[:, c0:c0 + L], 128, [vv], [pv])
                    yield
                    vtm = zp.next()
                    yield
                    S.op("act", (lambda e, vtm=vtm, pv=pv: e.copy(out=vtm[0:L, 0:128], in_=pv[0:L, 0:128])), reads=[pv], writes=[vtm])
                    yield
                    po = psr.next()
                    yield
                    MM(po[:, 0:L], Sst[l][:, cb, :], q[:, c0:c0 + L], True, False, [Sv, q], [po])
                    yield
                    MM(po[:, 0:L], vtm[0:L, 0:128], att[0:L, 0:L], False, True, [vtm, att], [po])
                    yield
                    S.op("act", (lambda e, osb=osb, po=po, c0=c0: e.copy(out=osb[:, c0:c0 + L], in_=po[:, 0:L])), reads=[po], writes=[osb])
                    yield
                    k2 = zp.next()
                    yield
                    TS(k2[:, 0:L], kk[:, c0:c0 + L], eb[:, c0 + L - 1:c0 + L], None, ALU.mult, None, [kk, eb], [k2])
                    yield
                    pk = psr.next()
                    yield
                    TR(pk[0:L, 0:128], k2[:, 0:L], 128, [k2], [pk])
                    yield
                    k2t = zp.next()
                    yield
                    CP(k2t[0:L, 0:128], pk[0:L, 0:128], [pk], [k2t])
                    yield
                    pn = psr.next()
                    yield
                    MM(pn[:, 0:128], k2t[0:L, 0:128], vtm[0:L, 0:128], True, True, [k2t, vtm], [pn])
                    yield
                    STT(Sst[l][:, cb, :], Sst[l][:, cb, :], eb[:, c0 + L - 1:c0 + L], pn[:, 0:128], ALU.mult, ALU.add, [Sv, eb, pn], [Sv])
                    yield
                sq = zp.next()
                yield
                ACT(sq[:, 0:nt], osb[:, 0:nt], AF.Square, [osb], [sq])
                yield
                pss = psr.next()
                yield
                MM(pss[:, 0:nt], ones[:], sq[:, 0:nt], True, True, [ones, sq], [pss])
                yield
                TS(sq[:, 0:nt], pss[:, 0:nt], 1.0 / 128.0, RMS_EPS, ALU.mult, ALU.add, [pss], [sq])
                yield
                ACT(sq[:, 0:nt], sq[:, 0:nt], AF.Sqrt, [sq], [sq])
                yield
                S.op("dve", (lambda e, sq=sq: e.reciprocal(out=sq[:, 0:nt], in_=sq[:, 0:nt])), reads=[sq], writes=[sq])
                yield
                TT(osb[:, 0:nt], osb[:, 0:nt], sq[:, 0:nt], ALU.mult, [osb, sq], [osb])
                yield
                STT(aABC[:, cb, 2, 0:nt], osb[:, 0:nt], PB[l][:, 104:105], sg[:, 0:nt], ALU.mult, ALU.mult, [osb, sg, PB[l]], [aV[cb][2]])
                yield
            KI = 2
            for g0 in range(0, 8, KI):
                active = [cb_body(cb, zps[cb % KI]) for cb in range(g0, g0 + KI)]
                while active:
                    for g_ in list(active):
                        try:
                            next(g_)
                        except StopIteration:
                            active.remove(g_)
            for j in range(8):
                sgs = []
                for X in range(3):
                    G = 9 + X
                    p = in_proj(l, G * 8 + j, nt)
                    z = zr.next()
                    ACT(z[:, 0:nt], p[:, 0:nt], AF.Sigmoid, [p, PA[l]], [z], bias=PA[l][:, G * 8 + j:G * 8 + j + 1])
                    sgs.append(z)
                for X in range(3):
                    w = wl(l, 96 + X * 8 + j)
                    p = psr.next()
                    for cb in range(8):
                        MM(p[:, 0:nt], w[:, cb * 128:(cb + 1) * 128], aABC[:, cb, X, 0:nt], cb == 0, cb == 7, [w, aV[cb][X]], [p])
                    TT(sgs[X][:, 0:nt], sgs[X][:, 0:nt], p[:, 0:nt], ALU.mult, [sgs[X], p], [sgs[X]])
                TT(sgs[0][:, 0:nt], sgs[0][:, 0:nt], sgs[1][:, 0:nt], ALU.add, [sgs[0], sgs[1]], [sgs[0]])
                TT(merged[:, j, 0:nt], sgs[0][:, 0:nt], sgs[2][:, 0:nt], ALU.add, [sgs[0], sgs[2]], [merged])
            pw = psB[0:2]
            for j in range(8):
                w = wl(l, 120 + j)
                for nh in range(2):
                    MM(pw[nh][0:TP, :], merged[:, j, 0:TP], w[:, nh * 512:(nh + 1) * 512], j == 0, j == 7, [merged, w], [pw[nh]])

            def resid_ln(src, pws, gi_, bi_, dst):
                for nh in range(2):
                    STT(rbuf[0:TP, nh * 512:(nh + 1) * 512], src[0:TP, nh * 512:(nh + 1) * 512], ALPHA, pws[nh][0:TP, :], ALU.mult, ALU.add, [src, pws[nh]], [rbuf])
                ln_from_rbuf(TP, gi_, bi_, dst)

            resid_ln(xin, pw, 0, 1, x1)
            if tagdbg is not None and ("x1_%d" % l) in dbg_out:
                DMA(dbg_out["x1_%d" % l][tagdbg:tagdbg + TP, :], x1[0:TP, :], reads=[x1])

            for c4 in range(2):
                p = psr.next()
                for k in range(4):
                    c = c4 * 4 + k
                    TR(p[:, k * 128:k * 128 + TP], x1[0:TP, c * 128:(c + 1) * 128], TP, [x1], [p])
                for k in range(4):
                    c = c4 * 4 + k
                    S.op("act", (lambda e, c=c, k=k, p=p: e.copy(out=x1FMb[:, c, po:po + TP], in_=p[:, k * 128:k * 128 + TP])), reads=[p], writes=[x1FMb])
            for hp in range(16):
                w = wl(l, 128 + hp)
                p = psr.next()
                for c in range(8):
                    MM(p[:, 0:nt], w[:, c * 128:(c + 1) * 128], x1FMb[:, c, po:po + nt], c == 0, c == 7, [w, x1FMb], [p])
                S.op("act", (lambda e, hp=hp, p=p: e.copy(out=qT[:, hp, 0:nt], in_=p[:, 0:nt])), reads=[p], writes=[qT])
            for g4 in range(4):
                p = psr.next()
                for k in range(4):
                    hp = g4 * 4 + k
                    MM(p[0:TP, k * 128:(k + 1) * 128], qT[:, hp, 0:TP], keysT[l][:, hp, :], True, True, [qT, keysT[l]], [p])
                CP(pscr[0:TP, g4 * 512:(g4 + 1) * 512], p[0:TP, :], [p], [pscr])
            for hp in range(16):
                sl = pscr[0:TP, hp * 128:(hp + 1) * 128]
                S.op("dve", (lambda e, hp=hp, sl=sl: e.max(out=sv[0:TP, hp, 0:8], in_=sl)), reads=[pscr], writes=[sv])
                S.op("dve", (lambda e, hp=hp, sl=sl: e.max_index(out=si[0:TP, hp, 0:8], in_max=sv[0:TP, hp, 0:8], in_values=sl)), reads=[pscr, sv], writes=[si])
                S.op("dve", (lambda e, hp=hp, sl=sl: e.match_replace(out=s2[0:TP, :], in_to_replace=sv[0:TP, hp, 0:8], in_values=sl, imm_value=NEG)), reads=[pscr, sv], writes=[s2])
                S.op("dve", (lambda e, hp=hp: e.max(out=sv[0:TP, hp, 8:16], in_=s2[0:TP, :])), reads=[s2], writes=[sv])
                S.op("dve", (lambda e, hp=hp: e.max_index(out=si[0:TP, hp, 8:16], in_max=sv[0:TP, hp, 8:16], in_values=s2[0:TP, :])), reads=[s2, sv], writes=[si])
            CP(sif[0:TP], si[0:TP], [si], [sif])
            svv = sv[0:TP].rearrange("n (h p) k -> n h p k", p=2)
            cv = cand[0:TP, :].rearrange("n (h i j) -> n h i j", h=8, i=16)
            TT(cv, svv[:, :, 0, :].unsqueeze(3).to_broadcast([TP, 8, 16, 16]), svv[:, :, 1, :].unsqueeze(2).to_broadcast([TP, 8, 16, 16]), ALU.add, [sv], [cand])
            for h in range(8):
                sl = cand[0:TP, h * 256:(h + 1) * 256]
                sl2 = pscr[0:TP, h * 256:(h + 1) * 256]
                S.op("dve", (lambda e, h=h, sl=sl: e.max(out=tops[0:TP, h, 0:8], in_=sl)), reads=[cand], writes=[tops])
                S.op("dve", (lambda e, h=h, sl=sl: e.max_index(out=topp[0:TP, h, 0:8], in_max=tops[0:TP, h, 0:8], in_values=sl)), reads=[cand, tops], writes=[topp])
                S.op("dve", (lambda e, h=h, sl=sl, sl2=sl2: e.match_replace(out=sl2, in_to_replace=tops[0:TP, h, 0:8], in_values=sl, imm_value=NEG)), reads=[cand, tops], writes=[pscr])
                S.op("dve", (lambda e, h=h, sl2=sl2: e.max(out=tops[0:TP, h, 8:16], in_=sl2)), reads=[pscr], writes=[tops])
                S.op("dve", (lambda e, h=h, sl2=sl2: e.max_index(out=topp[0:TP, h, 8:16], in_max=tops[0:TP, h, 8:16], in_values=sl2)), reads=[pscr, tops], writes=[topp])
            S.op("dve", lambda e: e.tensor_single_scalar(out=pij[0:TP, 0], in_=topp[0:TP], scalar=4, op=ALU.logical_shift_right), reads=[topp], writes=[pij])
            S.op("dve", lambda e: e.tensor_single_scalar(out=pij[0:TP, 1], in_=topp[0:TP], scalar=15, op=ALU.bitwise_and), reads=[topp], writes=[pij])
            CP(pijf[0:TP], pij[0:TP], [pij], [pijf])
            sifv = sif[0:TP].rearrange("n (h p) k -> n h p k", p=2)
            oh = qT[0:TP, :, :].rearrange("n a b -> n (a b)")[:, 0:2048].rearrange("n (h k m) -> n h k m", h=8, k=16)
            for pp in range(2):
                TT(oh, iota16[0:TP, :].unsqueeze(1).unsqueeze(1).to_broadcast([TP, 8, 16, 16]), pijf[0:TP, pp].unsqueeze(3).to_broadcast([TP, 8, 16, 16]), ALU.is_equal, [iota16, pijf], [qT])
                TT(oh, oh, sifv[:, :, pp, :].unsqueeze(2).to_broadcast([TP, 8, 16, 16]), ALU.mult, [qT, sif], [qT])
                S.op("dve", (lambda e, pp=pp: e.tensor_reduce(out=esel[0:TP, pp], in_=oh, axis=AX.X, op=ALU.add)), reads=[qT], writes=[esel])
            TT(gate[0:TP], tops[0:TP], tops[0:TP, :, 0:1].to_broadcast([TP, 8, 16]), ALU.subtract, [tops], [gate])
            ACT(gate[0:TP], gate[0:TP], AF.Exp, [gate], [gate])
            S.op("dve", lambda e: e.tensor_reduce(out=small[0:TP, 0:8], in_=gate[0:TP], axis=AX.X, op=ALU.add), reads=[gate], writes=[small])
            S.op("dve", lambda e: e.reciprocal(out=small[0:TP, 8:16], in_=small[0:TP, 0:8]), reads=[small], writes=[small])
            TT(gate[0:TP], gate[0:TP], small[0:TP, 8:16].unsqueeze(2).to_broadcast([TP, 8, 16]), ALU.mult, [gate, small], [gate])
            p = psr.next()
            srcs = (esel[0:TP, 0].rearrange("n h k -> n (h k)"), esel[0:TP, 1].rearrange("n h k -> n (h k)"), gate[0:TP].rearrange("n h k -> n (h k)"))
            for k in range(3 if 'noslot' not in DBGF else 0):
                TR(p[:, k * 128:k * 128 + TP], srcs[k], TP, [esel, gate], [p])
            for k in range(3 if 'noslot' not in DBGF else 0):
                CP(slotT[:, k, po:po + TP], p[:, k * 128:k * 128 + TP], [p], [slotT])

        def dense(l, x1l, xouts, nts, pos):
            NK = len(nts)
            NTOT = sum(nts)
            Cv = C_all[:, 0:NTOT * 64].rearrange("p (n j) -> p n j", j=64)
            pout = [[psB[2 * k], psB[2 * k + 1]] for k in range(NK)]
            for half in range(2):
                pc = None
                for cnt in range(NTOT):
                    oi = OIr.next()
                    oj = OJr.next()
                    TS(oi[:, :], iota128[:, :], slotT[:, 0, cnt:cnt + 1], None, ALU.is_equal, None, [iota128, slotT], [oi])
                    TS(oj[:, 0:64], iota128[:, half * 64:(half + 1) * 64], slotT[:, 1, cnt:cnt + 1], slotT[:, 2, cnt:cnt + 1], ALU.is_equal, ALU.mult, [iota128, slotT], [oj])
                    if cnt % 8 == 0:
                        pc = psr.next()
                    MM(pc[:, (cnt % 8) * 64:(cnt % 8 + 1) * 64], oi[:, :], oj[:, 0:64], True, True, [oi, oj], [pc])
                    if cnt % 8 == 7:
                        c0 = cnt - 7
                        S.op("act", (lambda e, pc=pc, c0=c0: e.copy(out=C_all[:, c0 * 64:(c0 + 8) * 64], in_=pc[:, :])), reads=[pc], writes=[C_all])

                def ph_stage(j):
                    ut = UTr.next()
                    DMA(ut[:], UT[l, j], reads=[UTbuf[l][j]], writes=[ut])
                    vb = VBr.next()
                    DMA(vb[:], VB[l, j], reads=[VBbuf[l][j]], writes=[vb])
                    ph = psr.next()
                    for c in range(8):
                        MM(ph[:, 0:NTOT], ut[:, c * 128:(c + 1) * 128], x1FMb[:, c, 0:NTOT], c == 0, c == 7, [ut, x1FMb], [ph])
                    return ph, vb
                cur = ph_stage(half * 64)
                for jl in range(64):
                    j = half * 64 + jl
                    nxt = ph_stage(j + 1) if jl + 1 < 64 else None
                    ph, vb = cur
                    G = Gr.next()
                    ACT(G[:, 0:NTOT], ph[:, 0:NTOT], AF.Gelu, [ph], [G])
                    Wt = Wr.next()
                    TT(Wt[:, 0:NTOT], G[:, 0:NTOT], Cv[:, 0:NTOT, jl], ALU.mult, [G, C_all], [Wt])
                    for k in range(NK):
                        for nh in range(2):
                            MM(pout[k][nh][0:nts[k], :], Wt[:, pos[k]:pos[k] + nts[k]], vb[:, nh * 512:(nh + 1) * 512], j == 0, j == 127, [Wt, vb], [pout[k][nh]])
                    cur = nxt
            for k in range(NK):
                TP = nts[k]
                for nh in range(2):
                    STT(rbuf[0:TP, nh * 512:(nh + 1) * 512], x1l[k][0:TP, nh * 512:(nh + 1) * 512], ALPHA, pout[k][nh][0:TP, :], ALU.mult, ALU.add, [x1l[k], pout[k][nh]], [rbuf])
                ln_from_rbuf(TP, 2, 3, xouts[k])

        def run_group(srcs, dsts, nts, L):
            NK = len(nts)
            pos = [0, nts[0]][:NK]
            for k in range(NK):
                DMA(xA[k][0:nts[k], :], srcs[k], writes=[xA[k]])
            cur_, nxt_ = xA, xB
            for l in range(NL):
                for k in range(NK):
                    mix_topk(l, cur_[k], x1s[k], nts[k], L, pos[k])
                dense(l, x1s[:NK], nxt_[:NK], nts, pos)
                cur_, nxt_ = nxt_, cur_
            for k in range(NK):
                DMA(dsts[k], cur_[k][0:nts[k], :], reads=[cur_[k]])

        init_states_zero()
        nblk = T // TB
        b = 0
        while b < nblk:
            nk = 2 if b + 1 < nblk else 1
            run_group([xp[(b + k) * TB:(b + k + 1) * TB, :] for k in range(nk)], [yp[(b + k) * TB:(b + k + 1) * TB, :] for k in range(nk)], [TB] * nk, 64)
            b += nk
        out_states(0)
        if with_sample:
            init_states_sample()
            run_group([xsm[:, :]], [ys[:, :]], [32], 32)
            out_states(1)
        with nc.allow_low_precision("bf16 matmul operands"):
            S.run()
    return nc


W_NAMES = ["w_in", "b_in", "conv_a_w", "conv_b_w", "conv_b_b", "lru_wa", "lru_ba", "lru_wx", "lru_bx", "lru_lambda",
           "hgrn_lb_logits", "hgrn_norm_g", "w_out_a", "w_out_b", "w_out_c", "w_o", "ln1_g", "ln1_b", "peer_wq",
           "peer_keys", "peer_u", "peer_v", "ln2_g", "ln2_b"]


def run(inputs, T, NL=DEPTH, with_sample=True, dbg=None, ncores=8):
    import time
    t0 = time.time()
    nc = build(T, NL, with_sample, dbg=dbg)
    print('build time', time.time() - t0, flush=True)
    f = lambda a: np.ascontiguousarray(np.asarray(a, dtype=np.float32))
    wmaps = {k: f(inputs[k]) for k in W_NAMES}
    in_maps = []
    for c in range(ncores):
        m = dict(wmaps)
        m["xp"] = f(inputs["x_prompt"][c % 4, :T])
        m["xs"] = f(inputs["x_sample"][c])
        m["st_ca"] = f(inputs["state_conv_a"][:, c])
        m["st_cb"] = f(inputs["state_conv_b"][:, c])
        m["st_h"] = f(inputs["state_lru"][:, c])
        m["st_S"] = f(inputs["state_hgrn"][:, c])
        in_maps.append(m)
    res = run_bass_kernel_spmd(nc, in_maps, core_ids=list(range(ncores)))
    print('total time', time.time() - t0, flush=True)
    return res.results


def kernel(**inputs):
    T = inputs["x_prompt"].shape[1]
    r = run(inputs, T)
    yp = np.stack([r[c]["yp"] for c in range(4)], 0)
    ys = np.stack([r[c]["ys"] for c in range(8)], 0)
    def pst(name):
        return np.stack([r[c][name] for c in range(4)], 1)
    def sst(name):
        return np.stack([r[c][name] for c in range(8)], 1)
    return (yp, ys, pst("p_ca"), pst("p_cb"), pst("p_h"), pst("p_S"),
            sst("s_ca"), sst("s_cb"), sst("s_h"), sst("s_S"))
```

```python
import numpy as np
from contextlib import ExitStack
import concourse.bass as bass
import concourse.mybir as mybir
from concourse.bass_utils import run_bass_kernel_spmd

F32 = mybir.dt.float32
BF16 = mybir.dt.bfloat16
U32 = mybir.dt.uint32
I32 = mybir.dt.int32
AF = mybir.ActivationFunctionType
ALU = mybir.AluOpType
AX = mybir.AxisListType

D = 1024
NCB = 8
DEPTH = 2
ALPHA = (2.0 * DEPTH) ** 0.25
LN_EPS = 1e-5
RMS_EPS = 1e-6
NEG = -1.0e30


class Tl:
    def __init__(self, t, name=""):
        self.t = t
        self.name = name
        self.last_w = None
        self.readers = []

    def __getitem__(self, idx):
        return self.t[idx]


class Alias:
    def __init__(self, base, ap):
        self.base = base
        self.t = ap

    def __getitem__(self, idx):
        return self.t[idx]

    last_w = property(lambda s: s.base.last_w, lambda s, v: setattr(s.base, "last_w", v))
    readers = property(lambda s: s.base.readers, lambda s, v: setattr(s.base, "readers", v))


class Op:
    __slots__ = ("eng", "fn", "deps", "signal", "sem", "semval", "dma", "idx", "eidx")


class Sched:
    ENG = ("pe", "act", "dve", "pool", "sp")

    def __init__(self, nc, stack, n_dma_sems=16):
        self.nc = nc
        self.ops = []
        self.per_eng = {e: [] for e in self.ENG}
        self.esem = {e: stack.enter_context(nc.semaphore("es_" + e)) for e in self.ENG}
        self.dsems = {}
        self.dcnt = {}
        self.drr = {}
        for e in ("sp", "pool"):
            self.dsems[e] = [stack.enter_context(nc.semaphore("ds_%s%d" % (e, i))) for i in range(n_dma_sems)]
            self.dcnt[e] = [0] * n_dma_sems
            self.drr[e] = 0
        self.dlast = {}
        self.stack = stack

    def sb(self, name, shape, dt=F32):
        n = 1
        for d_ in shape[1:]:
            n *= d_
        self.sbytes = getattr(self, "sbytes", 0) + n * (2 if dt == BF16 else 4)
        t = self.stack.enter_context(self.nc.sbuf_tensor(name, list(shape), dt))
        return Tl(t, name)

    def ps(self, name, shape, dt=F32):
        t = self.stack.enter_context(self.nc.psum_tensor(name, list(shape), dt))
        return Tl(t, name)

    def op(self, eng, fn, reads=(), writes=(), dma=False):
        o = Op()
        o.eng = eng
        o.fn = fn
        o.dma = dma
        o.signal = False
        o.sem = None
        o.semval = None
        o.idx = len(self.ops)
        o.eidx = len(self.per_eng[eng])
        deps = []
        for r in reads:
            if r is not None and r.last_w is not None:
                deps.append(r.last_w)
        for w in writes:
            if w is None:
                continue
            if w.last_w is not None:
                deps.append(w.last_w)
            deps.extend(w.readers)
        if dma:
            k = self.drr[eng]
            self.drr[eng] = (k + 1) % len(self.dsems[eng])
            prev = self.dlast.get((eng, k))
            if prev is not None:
                deps.append(prev)
            self.dcnt[eng][k] += 16
            o.sem = self.dsems[eng][k]
            o.semval = self.dcnt[eng][k]
            self.dlast[(eng, k)] = o
            o.signal = True
        seen = set()
        dd = []
        for d in deps:
            if d is o or id(d) in seen:
                continue
            seen.add(id(d))
            if (not d.dma) and d.eng == "pe" and eng == "pe" and not dma:
                continue
            dd.append(d)
            if not d.dma:
                d.signal = True
        o.deps = dd
        for r in reads:
            if r is not None:
                r.readers.append(o)
        for w in writes:
            if w is not None:
                w.last_w = o
                w.readers = []
        self.ops.append(o)
        self.per_eng[eng].append(o)
        return o

    def finalize(self):
        for e in self.ENG:
            c = 0
            for o in self.per_eng[e]:
                if o.dma:
                    continue
                if o.signal:
                    c += 1
                    o.sem = self.esem[e]
                    o.semval = c

    def emit_engine(self, ename, eng):
        seen = {}
        for o in self.per_eng[ename]:
            need = {}
            for d in o.deps:
                if (not d.dma) and (not o.dma) and d.eng == ename and ename in ("act", "dve") and o.eidx - d.eidx >= 4:
                    continue
                key = id(d.sem)
                if key not in need or need[key][1] < d.semval:
                    need[key] = (d.sem, d.semval)
            for key, (sem, val) in need.items():
                if seen.get(key, 0) >= val:
                    continue
                eng.wait_ge(sem, val)
                seen[key] = val
            inst = o.fn(eng)
            if o.signal:
                inst.then_inc(o.sem, 16 if o.dma else 1)
        if ename in self.dsems:
            for k, s in enumerate(self.dsems[ename]):
                v = self.dcnt[ename][k]
                if v > 0 and seen.get(id(s), 0) < v:
                    eng.wait_ge(s, v)

    def run(self):
        self.finalize()
        S = self
        with self.nc.Block() as block:
            @block.tensor
            def _(e):
                S.emit_engine("pe", e)

            @block.scalar
            def _(e):
                S.emit_engine("act", e)

            @block.vector
            def _(e):
                S.emit_engine("dve", e)

            @block.gpsimd
            def _(e):
                S.emit_engine("pool", e)

            @block.sync
            def _(e):
                S.emit_engine("sp", e)


class Ring:
    def __init__(self, tiles):
        self.tiles = tiles
        self.i = 0

    def next(self):
        t = self.tiles[self.i]
        self.i = (self.i + 1) % len(self.tiles)
        return t


def build(T, NL=DEPTH, with_sample=True, NWR=4, NGR=3, dbg=None):
    import os
    DBGF = os.environ.get('KDBG', '')
    TB = 128
    nc = bass.Bass("TRN2", target_bir_lowering=False)

    def din(name, shape, dt=F32):
        return nc.dram_tensor(name, list(shape), dt, kind="ExternalInput").ap()

    def dout(name, shape, dt=F32):
        return nc.dram_tensor(name, list(shape), dt, kind="ExternalOutput").ap()

    xp = din("xp", [T, D])
    xsm = din("xs", [32, D])
    st_ca = din("st_ca", [DEPTH, 2, D])
    st_cb = din("st_cb", [DEPTH, 3, D])
    st_h = din("st_h", [DEPTH, D])
    st_S = din("st_S", [DEPTH, 8, 128, 128])
    w_in = din("w_in", [DEPTH, D, 12 * D])
    b_in = din("b_in", [DEPTH, 12 * D])
    conv_a_w = din("conv_a_w", [DEPTH, 3, D])
    conv_b_w = din("conv_b_w", [DEPTH, 4, D])
    conv_b_b = din("conv_b_b", [DEPTH, D])
    lru_wa = din("lru_wa", [DEPTH, 8, 128, 128])
    lru_ba = din("lru_ba", [DEPTH, D])
    lru_wx = din("lru_wx", [DEPTH, 8, 128, 128])
    lru_bx = din("lru_bx", [DEPTH, D])
    lru_lambda = din("lru_lambda", [DEPTH, D])
    lb_logits = din("hgrn_lb_logits", [DEPTH, D])
    norm_g = din("hgrn_norm_g", [DEPTH, 128])
    w_out = [din("w_out_a", [DEPTH, D, D]), din("w_out_b", [DEPTH, D, D]), din("w_out_c", [DEPTH, D, D])]
    w_o = din("w_o", [DEPTH, D, D])
    ln1_g = din("ln1_g", [DEPTH, D])
    ln1_b = din("ln1_b", [DEPTH, D])
    peer_wq = din("peer_wq", [DEPTH, D, 2048])
    peer_keys = din("peer_keys", [DEPTH, 8, 2, 128, 128])
    peer_u = din("peer_u", [DEPTH, 16384, D])
    peer_v = din("peer_v", [DEPTH, 16384, D])
    peer_u2 = peer_u.rearrange("l e d -> (l e) d")
    peer_v2 = peer_v.rearrange("l e d -> (l e) d")
    UT = nc.dram_tensor("UT_scr", [DEPTH, 128, 128, 1024], BF16, kind="Internal").ap()
    VB = nc.dram_tensor("VB_scr", [DEPTH, 128, 128, 1024], BF16, kind="Internal").ap()
    WS = nc.dram_tensor("WS_scr", [DEPTH, 144, 128, 1024], BF16, kind="Internal").ap()
    ln2_g = din("ln2_g", [DEPTH, D])
    ln2_b = din("ln2_b", [DEPTH, D])

    yp = dout("yp", [T, D])
    ys = dout("ys", [32, D])
    o_ca = [dout("p_ca", [DEPTH, 2, D]), dout("s_ca", [DEPTH, 2, D])]
    o_cb = [dout("p_cb", [DEPTH, 3, D]), dout("s_cb", [DEPTH, 3, D])]
    o_h = [dout("p_h", [DEPTH, D]), dout("s_h", [DEPTH, D])]
    o_S = [dout("p_S", [DEPTH, 8, 128, 128]), dout("s_S", [DEPTH, 8, 128, 128])]
    dbg_out = {}
    if dbg:
        for nm, shp in dbg.items():
            dbg_out[nm] = dout("dbg_" + nm, shp)

    with ExitStack() as st:
        S = Sched(nc, st)

        def DMA(out_ap, in_ap, reads=(), writes=(), slow=False):
            if slow:
                S.op("sp", lambda e: e.dma_start(out=out_ap, in_=in_ap, allow_slow_non_contiguous=True), reads=reads, writes=writes, dma=True)
            else:
                S.op("sp", lambda e: e.dma_start(out=out_ap, in_=in_ap), reads=reads, writes=writes, dma=True)

        def ACT(out_ap, in_ap, func, reads, writes, bias=None, scale=None):
            kw = {}
            if bias is not None:
                kw["bias"] = bias
            if scale is not None:
                kw["scale"] = scale
            S.op("act", lambda e: e.activation(out=out_ap, in_=in_ap, func=func, **kw), reads=reads, writes=writes)

        def TS(out_ap, in_ap, s1, s2, op0, op1, reads, writes, eng="dve"):
            if op1 is None:
                S.op(eng, lambda e: e.tensor_scalar(out=out_ap, in0=in_ap, scalar1=s1, scalar2=None, op0=op0), reads=reads, writes=writes)
            else:
                S.op(eng, lambda e: e.tensor_scalar(out=out_ap, in0=in_ap, scalar1=s1, scalar2=s2, op0=op0, op1=op1), reads=reads, writes=writes)

        def TT(out_ap, a_ap, b_ap, op, reads, writes, eng="dve"):
            S.op(eng, lambda e: e.tensor_tensor(out=out_ap, in0=a_ap, in1=b_ap, op=op), reads=reads, writes=writes)

        def STT(out_ap, in0, scalar, in1, op0, op1, reads, writes, accum=None, eng="dve"):
            if accum is None:
                S.op(eng, lambda e: e.scalar_tensor_tensor(out=out_ap, in0=in0, scalar=scalar, in1=in1, op0=op0, op1=op1), reads=reads, writes=writes)
            else:
                S.op(eng, lambda e: e.scalar_tensor_tensor(out=out_ap, in0=in0, scalar=scalar, in1=in1, op0=op0, op1=op1, accum_out=accum), reads=reads, writes=writes)

        def CP(out_ap, in_ap, reads, writes, eng="dve"):
            S.op(eng, lambda e: e.tensor_copy(out=out_ap, in_=in_ap), reads=reads, writes=writes)

        def MSET(ap, val, writes, eng="pool"):
            S.op(eng, lambda e: e.memset(ap, val), writes=writes)

        def MM(out_ap, lhsT, rhs, start, stop, reads, writes):
            S.op("pe", lambda e: e.matmul(out_ap, lhsT=lhsT, rhs=rhs, start=start, stop=stop), reads=reads, writes=writes)

        def TR(out_ap, in_ap, k, reads, writes):
            S.op("pe", lambda e: e.transpose(out_ap, in_ap, ident[0:k, 0:k]), reads=list(reads) + [ident], writes=writes)

        ident = S.sb("ident", [128, 128])
        ones = S.sb("ones", [128, 128])
        tri = S.sb("tri", [64, 64])
        cmask = {64: S.sb("cmask64", [128, TB]), 32: S.sb("cmask32", [128, TB])}
        iota16 = S.sb("iota16", [128, 16])
        iota_i = S.sb("iota_i", [128, 16], I32)
        psr = Ring([S.ps("ps%d" % i, [128, 512]) for i in range(4)])
        psB = [S.ps("psB%d" % i, [128, 512]) for i in range(4)]
        wring = Ring([S.sb("wr%d" % i, [128, 1024]) for i in range(NWR)])
        zps = [Ring([S.sb("z%d_%d" % (k, i), [128, TB + 4]) for i in range(23)]) for k in range(2)]
        zr = zps[0]
        xA = [S.sb("xa%d" % i, [128, D]) for i in range(2)]
        xB = [S.sb("xb%d" % i, [128, D]) for i in range(2)]
        x1s = [S.sb("x1_%d" % i, [128, D]) for i in range(2)]
        rbuf = S.sb("rbuf", [128, D])
        aABC = S.sb("aABC", [128, 8, 3, TB], BF16)
        aV = [[Tl(aABC.t) for _ in range(3)] for _ in range(8)]
        merged = S.sb("merged", [128, 8, TB], BF16)
        s2 = S.sb("s2", [128, 128])
        C_all = S.sb("C_all", [128, 128 * 128], BF16)
        if 'noalias' in DBGF:
            pscr = S.sb("pscr", [128, 2048]); cand = S.sb("cand", [128, 2048]); qT = S.sb("qT", [128, 16, TB])
        else:
            pscr = Alias(C_all, C_all[:, 0:4096].bitcast(F32))
            cand = Alias(C_all, C_all[:, 4096:8192].bitcast(F32))
            qT = Alias(C_all, C_all[:, 8192:12288].bitcast(F32).rearrange("p (a b) -> p a b", b=128))
        x1FMb = S.sb("x1FMb", [128, 8, 2 * TB], BF16)
        iota128 = S.sb("iota128", [128, 128])
        iota128i = S.sb("iota128i", [128, 128], I32)
        slotT = S.sb("slotT", [128, 3, 256])
        OIr = Ring([S.sb("oi%d" % i, [128, 8, 128], BF16) for i in range(2)])
        OJr = Ring([S.sb("oj%d" % i, [128, 8, 64], BF16) for i in range(2)])
        UTr = Ring([S.sb("utr%d" % i, [128, 1024], BF16) for i in range(3)])
        VBr = Ring([S.sb("vbr%d" % i, [128, 1024], BF16) for i in range(4)])
        Gr = Ring([S.sb("gg%d" % i, [128, 256]) for i in range(3)])
        Wr = Ring([S.sb("wt%d" % i, [128, 256], BF16) for i in range(3)])
        wbr = Ring([S.sb("wbr%d" % i, [128, 1024], BF16) for i in range(5)])
        WSbuf = [[Tl(None) for _ in range(144)] for _ in range(NL)]
        xFMb = S.sb("xFMb", [128, 8, TB], BF16)
        UTbuf = [[Tl(None) for _ in range(128)] for _ in range(NL)]
        VBbuf = [[Tl(None) for _ in range(128)] for _ in range(NL)]
        lnp = [S.sb("lnp%d" % i, [128, D]) for i in range(4)]
        keysT = [S.sb("keysT%d" % l, [128, 16, 128]) for l in range(NL)]
        PA = [S.sb("PA%d" % l, [128, 96]) for l in range(NL)]
        PB = [S.sb("PB%d" % l, [128, 128]) for l in range(NL)]
        PD = [S.sb("PD%d" % l, [128, 4, 8]) for l in range(NL)]
        stg = S.sb("stg", [128, 128])
        Sst = [S.sb("Sst%d" % l, [128, 8, 128]) for l in range(NL)]
        SstV = [[Tl(Sst[l].t) for _ in range(8)] for l in range(NL)]
        cAst = [S.sb("cAst%d" % l, [128, 8, 2]) for l in range(NL)]
        cAV = [[Tl(cAst[l].t) for _ in range(8)] for l in range(NL)]
        cBst = [S.sb("cBst%d" % l, [128, 8, 3]) for l in range(NL)]
        cBV = [[Tl(cBst[l].t) for _ in range(8)] for l in range(NL)]
        hst = [S.sb("hst%d" % l, [128, 8]) for l in range(NL)]
        hV = [[Tl(hst[l].t) for _ in range(8)] for l in range(NL)]
        small = S.sb("small", [128, 64])
        sv = S.sb("sv", [128, 16, 16])
        si = S.sb("si", [128, 16, 16], U32)
        sif = S.sb("sif", [128, 16, 16])
        tops = S.sb("tops", [128, 8, 16])
        topp = S.sb("topp", [128, 8, 16], U32)
        pij = S.sb("pij", [128, 2, 8, 16], U32)
        pijf = S.sb("pijf", [128, 2, 8, 16])
        esel = S.sb("esel", [128, 2, 8, 16])
        gate = S.sb("gate", [128, 8, 16])
        mvt = S.sb("mvt", [128, 16])
        bst = S.sb("bst", [128, 12])

        print('SBUF bytes/partition', S.sbytes, flush=True)
        MSET(ident[:], 0.0, [ident])
        S.op("pool", lambda e: e.affine_select(out=ident[:], in_=ident[:], pattern=[[-1, 128]], compare_op=ALU.not_equal, fill=1.0, base=0, channel_multiplier=1), reads=[ident], writes=[ident])
        MSET(ones[:], 1.0, [ones])
        MSET(tri[:], 1.0, [tri])
        S.op("pool", lambda e: e.affine_select(out=tri[:], in_=tri[:], pattern=[[1, 64]], compare_op=ALU.is_ge, fill=0.0, base=0, channel_multiplier=-1), reads=[tri], writes=[tri])
        for L in (64, 32):
            MSET(cmask[L][:], 1.0, [cmask[L]])
            for c in range(TB // L):
                MSET(cmask[L][:, c * L:c * L + 1], 0.0, [cmask[L]])
        S.op("pool", lambda e: e.iota(iota_i[:], pattern=[[1, 16]], base=0, channel_multiplier=0), writes=[iota_i])
        CP(iota16[:], iota_i[:], [iota_i], [iota16])
        S.op("pool", lambda e: e.iota(iota128i[:], pattern=[[1, 128]], base=0, channel_multiplier=0), writes=[iota128i])
        CP(iota128[:], iota128i[:], [iota128i], [iota128])

        for l in range(NL):
            MSET(stg[:], 0.0, [stg])
            DMA(stg[0:96, :], b_in[l].rearrange("(r p) -> r p", p=128), writes=[stg])
            p = psr.next()
            TR(p[:, 0:128], stg[:], 128, [stg], [p])
            CP(PA[l][:], p[:, 0:96], [p], [PA[l]])
            MSET(stg[:], 0.0, [stg])
            DMA(stg[0:24, :], conv_a_w[l].rearrange("k (c p) -> (k c) p", p=128), writes=[stg])
            DMA(stg[24:56, :], conv_b_w[l].rearrange("k (c p) -> (k c) p", p=128), writes=[stg])
            DMA(stg[56:64, :], conv_b_b[l].rearrange("(c p) -> c p", p=128), writes=[stg])
            DMA(stg[64:72, :], lru_ba[l].rearrange("(c p) -> c p", p=128), writes=[stg])
            DMA(stg[72:80, :], lru_bx[l].rearrange("(c p) -> c p", p=128), writes=[stg])
            DMA(stg[80:88, :], lru_lambda[l].rearrange("(c p) -> c p", p=128), writes=[stg])
            DMA(stg[88:96, :], lb_logits[0].rearrange("(c p) -> c p", p=128), writes=[stg])
            DMA(stg[96:104, :], lb_logits[1].rearrange("(c p) -> c p", p=128), writes=[stg])
            DMA(stg[104:105, :], norm_g[l:l + 1, :], writes=[stg])
            p = psr.next()
            TR(p[:, 0:128], stg[:], 128, [stg], [p])
            CP(PB[l][:], p[:, 0:128], [p], [PB[l]])
            ACT(PD[l][:, 0, :], PB[l][:, 80:88], AF.Exp, [PB[l]], [PD[l]], scale=-1.0)
            ACT(PD[l][:, 0, :], PD[l][:, 0, :], AF.Ln, [PD[l]], [PD[l]], bias=1.0)
            TS(PD[l][:, 1, :], PD[l][:, 0, :], -16.0, None, ALU.mult, None, [PD[l]], [PD[l]])
            TS(PD[l][:, 0, :], PD[l][:, 0, :], -8.0, None, ALU.mult, None, [PD[l]], [PD[l]])
            if l == 0:
                MSET(PD[l][:, 2, :], 0.0, [PD[l]], eng="dve")
                MSET(PD[l][:, 3, :], 1.0, [PD[l]], eng="dve")
            else:
                TT(PD[l][:, 2, :], PB[l][:, 96:104], PB[l][:, 88:96], ALU.subtract, [PB[l]], [PD[l]])
                ACT(PD[l][:, 2, :], PD[l][:, 2, :], AF.Sigmoid, [PD[l]], [PD[l]])
                TS(PD[l][:, 3, :], PD[l][:, 2, :], -1.0, 1.0, ALU.mult, ALU.add, [PD[l]], [PD[l]])
            for hp in range(16):
                w = wring.next()
                DMA(w[:, 0:128], peer_keys[l, hp // 2, hp % 2], writes=[w])
                p = psr.next()
                TR(p[:, 0:128], w[:, 0:128], 128, [w], [p])
                CP(keysT[l][:, hp, :], p[:, 0:128], [p], [keysT[l]], eng="dve")

        import os
        DBGF = os.environ.get('KDBG', '')
        for l in range(NL if 'noprep' not in DBGF else 0):
            ug = peer_u[l].rearrange("(i j) d -> j i d", j=128)
            vg = peer_v[l].rearrange("(i j) d -> j i d", j=128)
            for j in range(128):
                g1 = wring.next()
                DMA(g1[:], ug[j], writes=[g1])
                ut = UTr.next()
                for c4 in range(2):
                    p = psr.next()
                    for k in range(4):
                        c = c4 * 4 + k
                        TR(p[:, k * 128:(k + 1) * 128], g1[:, c * 128:(c + 1) * 128], 128, [g1], [p])
                    S.op("act", (lambda e, ut=ut, p=p, c4=c4: e.copy(out=ut[:, c4 * 512:(c4 + 1) * 512], in_=p[:, :])), reads=[p], writes=[ut])
                DMA(UT[l, j], ut[:], reads=[ut], writes=[UTbuf[l][j]])
                g2 = wring.next()
                DMA(g2[:], vg[j], writes=[g2])
                vb = VBr.next()
                CP(vb[:], g2[:], [g2], [vb], eng="pool")
                DMA(VB[l, j], vb[:], reads=[vb], writes=[VBbuf[l][j]])

        def init_states_zero():
            for l in range(NL):
                MSET(Sst[l][:], 0.0, SstV[l])
                MSET(cAst[l][:], 0.0, cAV[l])
                MSET(cBst[l][:], 0.0, cBV[l])
                MSET(hst[l][:], 0.0, hV[l])

        def init_states_sample():
            for l in range(NL):
                DMA(Sst[l][:], st_S[l].rearrange("h d e -> d h e"), writes=SstV[l])
                for cb in range(8):
                    DMA(cAst[l][:, cb, :], st_ca[l][:, cb * 128:(cb + 1) * 128].rearrange("k p -> p k"), writes=[cAV[l][cb]], slow=True)
                    DMA(cBst[l][:, cb, :], st_cb[l][:, cb * 128:(cb + 1) * 128].rearrange("k p -> p k"), writes=[cBV[l][cb]], slow=True)
                DMA(hst[l][:], st_h[l].rearrange("(c p) -> p c", p=128), writes=hV[l], slow=True)

        def out_states(which):
            for l in range(NL):
                DMA(o_S[which][l].rearrange("h d e -> d h e"), Sst[l][:], reads=SstV[l])
                for cb in range(8):
                    DMA(o_ca[which][l][:, cb * 128:(cb + 1) * 128].rearrange("k p -> p k"), cAst[l][:, cb, :], reads=[cAV[l][cb]], slow=True)
                    DMA(o_cb[which][l][:, cb * 128:(cb + 1) * 128].rearrange("k p -> p k"), cBst[l][:, cb, :], reads=[cBV[l][cb]], slow=True)
                DMA(o_h[which][l].rearrange("(c p) -> p c", p=128), hst[l][:], reads=hV[l], slow=True)

        def wsrc(l, idx):
            if idx < 96:
                return w_in[l][:, idx * 128:(idx + 1) * 128].rearrange("(c p) n -> p c n", p=128), True
            if idx < 120:
                X, j = (idx - 96) // 8, (idx - 96) % 8
                return w_out[X][l][:, j * 128:(j + 1) * 128].rearrange("(c p) n -> p c n", p=128), True
            if idx < 128:
                j = idx - 120
                return w_o[l, j * 128:(j + 1) * 128, :], False
            hp = idx - 128
            return peer_wq[l][:, hp * 128:(hp + 1) * 128].rearrange("(c p) n -> p c n", p=128), True

        for l in range(NL):
            for idx in range(144):
                src, tiled = wsrc(l, idx)
                w = wring.next()
                if tiled:
                    DMA(w[:].rearrange("p (c n) -> p c n", n=128), src, writes=[w])
                else:
                    DMA(w[:], src, writes=[w])
                wb = wbr.next()
                CP(wb[:], w[:], [w], [wb], eng=("pool" if idx % 2 else "dve"))
                DMA(WS[l, idx], wb[:], reads=[wb], writes=[WSbuf[l][idx]])

        def wl(l, idx):
            wb = wbr.next()
            DMA(wb[:], WS[l, idx], reads=[WSbuf[l][idx]], writes=[wb])
            return wb

        def in_proj(l, idx, nt):
            w = wl(l, idx)
            p = psr.next()
            for c in range(8):
                MM(p[:, 0:nt], w[:, c * 128:(c + 1) * 128], xFMb[:, c, 0:nt], c == 0, c == 7, [w, xFMb], [p])
            return p

        def ln_from_rbuf(TP, gi_, bi_, dst):
            for nh in range(2):
                S.op("dve", (lambda e, nh=nh: e.bn_stats(out=bst[0:TP, nh * 6:(nh + 1) * 6], in_=rbuf[0:TP, nh * 512:(nh + 1) * 512])), reads=[rbuf], writes=[bst])
            S.op("dve", lambda e: e.bn_aggr(out=mvt[0:TP, 0:2], in_=bst[0:TP, 0:12]), reads=[bst], writes=[mvt])
            TS(mvt[0:TP, 2:3], mvt[0:TP, 1:2], 1.0, LN_EPS, ALU.mult, ALU.add, [mvt], [mvt])
            ACT(mvt[0:TP, 2:3], mvt[0:TP, 2:3], AF.Sqrt, [mvt], [mvt])
            S.op("dve", lambda e: e.reciprocal(out=mvt[0:TP, 3:4], in_=mvt[0:TP, 2:3]), reads=[mvt], writes=[mvt])
            TS(rbuf[0:TP, :], rbuf[0:TP, :], mvt[0:TP, 0:1], mvt[0:TP, 3:4], ALU.subtract, ALU.mult, [rbuf, mvt], [rbuf])
            TT(rbuf[0:TP, :], rbuf[0:TP, :], lnp[gi_][0:TP, :], ALU.mult, [rbuf, lnp[gi_]], [rbuf])
            TT(dst[0:TP, :], rbuf[0:TP, :], lnp[bi_][0:TP, :], ALU.add, [rbuf, lnp[bi_]], [dst])

        def mix_topk(l, xin, x1, nt, L, po, tagdbg=None):
            TP = nt
            nch = nt // L
            for i, src in enumerate((ln1_g, ln1_b, ln2_g, ln2_b)):
                DMA(lnp[i][:], src[l:l + 1, :].to_broadcast([128, D]), writes=[lnp[i]])
            for c4 in range(2):
                p = psr.next()
                for k in range(4):
                    c = c4 * 4 + k
                    TR(p[:, k * 128:k * 128 + TP], xin[0:TP, c * 128:(c + 1) * 128], TP, [xin], [p])
                for k in range(4):
                    c = c4 * 4 + k
                    S.op("act", (lambda e, c=c, k=k, p=p: e.copy(out=xFMb[:, c, 0:TP], in_=p[:, k * 128:k * 128 + TP])), reads=[p], writes=[xFMb])
            def cb_body(cb, zp):
                def zin(G, func):
                    p = in_proj(l, G * 8 + cb, nt)
                    z = zp.next()
                    ACT(z[:, 0:nt], p[:, 0:nt], func, [p, PA[l]], [z], bias=PA[l][:, G * 8 + cb:G * 8 + cb + 1])
                    return z
                zB = zin(0, AF.Identity)
                yield
                zC = zin(1, AF.Identity)
                yield
                zxA = zin(2, AF.Identity)
                yield
                ub = zp.next()
                yield
                CP(ub[:, 0:2], cAst[l][:, cb, :], [cAV[l][cb]], [ub], eng="pool")
                yield
                TT(ub[:, 2:2 + nt], zC[:, 0:nt], zxA[:, 0:nt], ALU.mult, [zC, zxA], [ub])
                yield
                CP(cAst[l][:, cb, :], ub[:, nt:nt + 2], [ub], [cAV[l][cb]], eng="pool")
                yield
                y = zp.next()
                yield
                TS(y[:, 0:nt], ub[:, 0:nt], PB[l][:, cb:cb + 1], None, ALU.mult, None, [ub, PB[l]], [y])
                yield
                STT(y[:, 0:nt], ub[:, 1:1 + nt], PB[l][:, 8 + cb:9 + cb], y[:, 0:nt], ALU.mult, ALU.add, [ub, y, PB[l]], [y])
                yield
                STT(y[:, 0:nt], ub[:, 2:2 + nt], PB[l][:, 16 + cb:17 + cb], y[:, 0:nt], ALU.mult, ALU.add, [ub, y, PB[l]], [y])
                yield
                TT(aABC[:, cb, 0, 0:nt], zB[:, 0:nt], y[:, 0:nt], ALU.mult, [zB, y], [aV[cb][0]])
                yield
                zxB = zin(3, AF.Identity)
                yield
                ggB = zin(4, AF.Gelu)
                yield
                cbuf = zp.next()
                yield
                CP(cbuf[:, 0:3], cBst[l][:, cb, :], [cBV[l][cb]], [cbuf], eng="pool")
                yield
                CP(cbuf[:, 3:3 + nt], zxB[:, 0:nt], [zxB], [cbuf], eng="pool")
                yield
                CP(cBst[l][:, cb, :], cbuf[:, nt:nt + 3], [cbuf], [cBV[l][cb]], eng="pool")
                yield
                xl = zp.next()
                yield
                TS(xl[:, 0:nt], cbuf[:, 0:nt], PB[l][:, 24 + cb:25 + cb], PB[l][:, 56 + cb:57 + cb], ALU.mult, ALU.add, [cbuf, PB[l]], [xl])
                yield
                for k in range(1, 4):
                    STT(xl[:, 0:nt], cbuf[:, k:k + nt], PB[l][:, 24 + 8 * k + cb:25 + 8 * k + cb], xl[:, 0:nt], ALU.mult, ALU.add, [cbuf, xl, PB[l]], [xl])
                    yield
                w = wring.next()
                yield
                DMA(w[:, 0:128], lru_wa[l, cb], writes=[w])
                yield
                DMA(w[:, 128:256], lru_wx[l, cb], writes=[w])
                yield
                p = psr.next()
                yield
                MM(p[:, 0:nt], w[:, 0:128], xl[:, 0:nt], True, True, [w, xl], [p])
                yield
                MM(p[:, 128:128 + nt], w[:, 128:256], xl[:, 0:nt], True, True, [w, xl], [p])
                yield
                r = zp.next()
                yield
                gi = zp.next()
                yield
                ACT(r[:, 0:nt], p[:, 0:nt], AF.Sigmoid, [p, PB[l]], [r], bias=PB[l][:, 64 + cb:65 + cb])
                yield
                ACT(gi[:, 0:nt], p[:, 128:128 + nt], AF.Sigmoid, [p, PB[l]], [gi], bias=PB[l][:, 72 + cb:73 + cb])
                yield
                a = zp.next()
                yield
                a2 = zp.next()
                yield
                ACT(a[:, 0:nt], r[:, 0:nt], AF.Exp, [r, PD[l]], [a], scale=PD[l][:, 0, cb:cb + 1])
                yield
                ACT(a2[:, 0:nt], r[:, 0:nt], AF.Exp, [r, PD[l]], [a2], scale=PD[l][:, 1, cb:cb + 1])
                yield
                TS(a2[:, 0:nt], a2[:, 0:nt], -1.0, 1.0, ALU.mult, ALU.add, [a2], [a2])
                yield
                ACT(a2[:, 0:nt], a2[:, 0:nt], AF.Sqrt, [a2], [a2])
                yield
                TT(gi[:, 0:nt], gi[:, 0:nt], a2[:, 0:nt], ALU.mult, [gi, a2], [gi])
                yield
                TT(gi[:, 0:nt], gi[:, 0:nt], xl[:, 0:nt], ALU.mult, [gi, xl], [gi])
                yield
                hh = zp.next()
                yield
                S.op("dve", (lambda e, hh=hh, a=a, gi=gi, cb=cb: e.tensor_tensor_scan(out=hh[:, 0:nt], data0=a[:, 0:nt], data1=gi[:, 0:nt], initial=hst[l][:, cb:cb + 1], op0=ALU.mult, op1=ALU.add)), reads=[a, gi, hV[l][cb]], writes=[hh])
                yield
                CP(hst[l][:, cb:cb + 1], hh[:, nt - 1:nt], [hh], [hV[l][cb]], eng="pool")
                yield
                TT(aABC[:, cb, 1, 0:nt], ggB[:, 0:nt], hh[:, 0:nt], ALU.mult, [ggB, hh], [aV[cb][1]])
                yield
                q = zin(5, AF.Silu)
                yield
                f = zin(6, AF.Sigmoid)
                yield
                vv = zin(7, AF.Identity)
                yield
                sg = zin(8, AF.Silu)
                yield
                TS(f[:, 0:nt], f[:, 0:nt], PD[l][:, 3, cb:cb + 1], PD[l][:, 2, cb:cb + 1], ALU.mult, ALU.add, [f, PD[l]], [f])
                yield
                kk = zp.next()
                yield
                TS(kk[:, 0:nt], f[:, 0:nt], -1.0, 1.0, ALU.mult, ALU.add, [f], [kk])
                yield
                lf = zp.next()
                yield
                ACT(lf[:, 0:nt], f[:, 0:nt], AF.Ln, [f], [lf])
                yield
                bb = zp.next()
                yield
                S.op("dve", (lambda e, bb=bb, lf=lf: e.tensor_tensor_scan(out=bb[:, 0:nt], data0=cmask[L][:, 0:nt], data1=lf[:, 0:nt], initial=0.0, op0=ALU.mult, op1=ALU.add)), reads=[lf, cmask[L]], writes=[bb])
                yield
                eb = zp.next()
                yield
                ACT(eb[:, 0:nt], bb[:, 0:nt], AF.Exp, [bb], [eb])
                yield
                ACT(bb[:, 0:nt], bb[:, 0:nt], AF.Exp, [bb], [bb], scale=-1.0)
                yield
                TT(q[:, 0:nt], q[:, 0:nt], eb[:, 0:nt], ALU.mult, [q, eb], [q])
                yield
                TT(kk[:, 0:nt], kk[:, 0:nt], bb[:, 0:nt], ALU.mult, [kk, bb], [kk])
                yield
                osb = zp.next()
                yield
                Sv = SstV[l][cb]
                yield
                for c in range(nch):
                    c0 = c * L
                    yield
                    pa = psr.next()
                    yield
                    MM(pa[0:L, 0:L], kk[:, c0:c0 + L], q[:, c0:c0 + L], True, True, [kk, q], [pa])
                    yield
                    att = zp.next()
                    yield
                    TT(att[0:L, 0:L], pa[0:L, 0:L], tri[0:L, 0:L], ALU.mult, [pa, tri], [att])
                    yield
                    pv = psr.next()
                    yield
                    TR(pv[0:L, 0:128], vv[:, c0:c0 + L], 128, [vv], [pv])
                    yield
                    vtm = zp.next()
                    yield
                    S.op("act", (lambda e, vtm=vtm, pv=pv: e.copy(out=vtm[0:L, 0:128], in_=pv[0:L, 0:128])), reads=[pv], writes=[vtm])
                    yield
                    po = psr.next()
                    yield
                    MM(po[:, 0:L], Sst[l][:, cb, :], q[:, c0:c0 + L], True, False, [Sv, q], [po])
                    yield
                    MM(po[:, 0:L], vtm[0:L, 0:128], att[0:L, 0:L], False, True, [vtm, att], [po])
                    yield
                    S.op("act", (lambda e, osb=osb, po=po, c0=c0: e.copy(out=osb[:, c0:c0 + L], in_=po[:, 0:L])), reads=[po], writes=[osb])
                    yield
                    k2 = zp.next()
                    yield
                    TS(k2[:, 0:L], kk[:, c0:c0 + L], eb[:, c0 + L - 1:c0 + L], None, ALU.mult, None, [kk, eb], [k2])
                    yield
                    pk = psr.next()
                    yield
                    TR(pk[0:L, 0:128], k2[:, 0:L], 128, [k2], [pk])
                    yield
                    k2t = zp.next()
                    yield
                    CP(k2t[0:L, 0:128], pk[0:L, 0:128], [pk], [k2t])
                    yield
                    pn = psr.next()
                    yield
                    MM(pn[:, 0:128], k2t[0:L, 0:128], vtm[0:L, 0:128], True, True, [k2t, vtm], [pn])
                    yield
                    STT(Sst[l][:, cb, :], Sst[l][:, cb, :], eb[:, c0 + L - 1:c0 + L], pn[:, 0:128], ALU.mult, ALU.add, [Sv, eb, pn], [Sv])
                    yield
                sq = zp.next()
                yield
                ACT(sq[:, 0:nt], osb[:, 0:nt], AF.Square, [osb], [sq])
                yield
                pss = psr.next()
                yield
                MM(pss[:, 0:nt], ones[:], sq[:, 0:nt], True, True, [ones, sq], [pss])
                yield
                TS(sq[:, 0:nt], pss[:, 0:nt], 1.0 / 128.0, RMS_EPS, ALU.mult, ALU.add, [pss], [sq])
                yield
                ACT(sq[:, 0:nt], sq[:, 0:nt], AF.Sqrt, [sq], [sq])
                yield
                S.op("dve", (lambda e, sq=sq: e.reciprocal(out=sq[:, 0:nt], in_=sq[:, 0:nt])), reads=[sq], writes=[sq])
                yield
                TT(osb[:, 0:nt], osb[:, 0:nt], sq[:, 0:nt], ALU.mult, [osb, sq], [osb])
                yield
                STT(aABC[:, cb, 2, 0:nt], osb[:, 0:nt], PB[l][:, 104:105], sg[:, 0:nt], ALU.mult, ALU.mult, [osb, sg, PB[l]], [aV[cb][2]])
                yield
            KI = 2
            for g0 in range(0, 8, KI):
                active = [cb_body(cb, zps[cb % KI]) for cb in range(g0, g0 + KI)]
                while active:
                    for g_ in list(active):
                        try:
                            next(g_)
                        except StopIteration:
                            active.remove(g_)
            for j in range(8):
                sgs = []
                for X in range(3):
                    G = 9 + X
                    p = in_proj(l, G * 8 + j, nt)
                    z = zr.next()
                    ACT(z[:, 0:nt], p[:, 0:nt], AF.Sigmoid, [p, PA[l]], [z], bias=PA[l][:, G * 8 + j:G * 8 + j + 1])
                    sgs.append(z)
                for X in range(3):
                    w = wl(l, 96 + X * 8 + j)
                    p = psr.next()
                    for cb in range(8):
                        MM(p[:, 0:nt], w[:, cb * 128:(cb + 1) * 128], aABC[:, cb, X, 0:nt], cb == 0, cb == 7, [w, aV[cb][X]], [p])
                    TT(sgs[X][:, 0:nt], sgs[X][:, 0:nt], p[:, 0:nt], ALU.mult, [sgs[X], p], [sgs[X]])
                TT(sgs[0][:, 0:nt], sgs[0][:, 0:nt], sgs[1][:, 0:nt], ALU.add, [sgs[0], sgs[1]], [sgs[0]])
                TT(merged[:, j, 0:nt], sgs[0][:, 0:nt], sgs[2][:, 0:nt], ALU.add, [sgs[0], sgs[2]], [merged])
            pw = psB[0:2]
            for j in range(8):
                w = wl(l, 120 + j)
                for nh in range(2):
                    MM(pw[nh][0:TP, :], merged[:, j, 0:TP], w[:, nh * 512:(nh + 1) * 512], j == 0, j == 7, [merged, w], [pw[nh]])

            def resid_ln(src, pws, gi_, bi_, dst):
                for nh in range(2):
                    STT(rbuf[0:TP, nh * 512:(nh + 1) * 512], src[0:TP, nh * 512:(nh + 1) * 512], ALPHA, pws[nh][0:TP, :], ALU.mult, ALU.add, [src, pws[nh]], [rbuf])
                ln_from_rbuf(TP, gi_, bi_, dst)

            resid_ln(xin, pw, 0, 1, x1)
            if tagdbg is not None and ("x1_%d" % l) in dbg_out:
                DMA(dbg_out["x1_%d" % l][tagdbg:tagdbg + TP, :], x1[0:TP, :], reads=[x1])

            for c4 in range(2):
                p = psr.next()
                for k in range(4):
                    c = c4 * 4 + k
                    TR(p[:, k * 128:k * 128 + TP], x1[0:TP, c * 128:(c + 1) * 128], TP, [x1], [p])
                for k in range(4):
                    c = c4 * 4 + k
                    S.op("act", (lambda e, c=c, k=k, p=p: e.copy(out=x1FMb[:, c, po:po + TP], in_=p[:, k * 128:k * 128 + TP])), reads=[p], writes=[x1FMb])
            for hp in range(16):
                w = wl(l, 128 + hp)
                p = psr.next()
                for c in range(8):
                    MM(p[:, 0:nt], w[:, c * 128:(c + 1) * 128], x1FMb[:, c, po:po + nt], c == 0, c == 7, [w, x1FMb], [p])
                S.op("act", (lambda e, hp=hp, p=p: e.copy(out=qT[:, hp, 0:nt], in_=p[:, 0:nt])), reads=[p], writes=[qT])
            for g4 in range(4):
                p = psr.next()
                for k in range(4):
                    hp = g4 * 4 + k
                    MM(p[0:TP, k * 128:(k + 1) * 128], qT[:, hp, 0:TP], keysT[l][:, hp, :], True, True, [qT, keysT[l]], [p])
                CP(pscr[0:TP, g4 * 512:(g4 + 1) * 512], p[0:TP, :], [p], [pscr])
            for hp in range(16):
                sl = pscr[0:TP, hp * 128:(hp + 1) * 128]
                S.op("dve", (lambda e, hp=hp, sl=sl: e.max(out=sv[0:TP, hp, 0:8], in_=sl)), reads=[pscr], writes=[sv])
                S.op("dve", (lambda e, hp=hp, sl=sl: e.max_index(out=si[0:TP, hp, 0:8], in_max=sv[0:TP, hp, 0:8], in_values=sl)), reads=[pscr, sv], writes=[si])
                S.op("dve", (lambda e, hp=hp, sl=sl: e.match_replace(out=s2[0:TP, :], in_to_replace=sv[0:TP, hp, 0:8], in_values=sl, imm_value=NEG)), reads=[pscr, sv], writes=[s2])
                S.op("dve", (lambda e, hp=hp: e.max(out=sv[0:TP, hp, 8:16], in_=s2[0:TP, :])), reads=[s2], writes=[sv])
                S.op("dve", (lambda e, hp=hp: e.max_index(out=si[0:TP, hp, 8:16], in_max=sv[0:TP, hp, 8:16], in_values=s2[0:TP, :])), reads=[s2, sv], writes=[si])
            CP(sif[0:TP], si[0:TP], [si], [sif])
            svv = sv[0:TP].rearrange("n (h p) k -> n h p k", p=2)
            cv = cand[0:TP, :].rearrange("n (h i j) -> n h i j", h=8, i=16)
            TT(cv, svv[:, :, 0, :].unsqueeze(3).to_broadcast([TP, 8, 16, 16]), svv[:, :, 1, :].unsqueeze(2).to_broadcast([TP, 8, 16, 16]), ALU.add, [sv], [cand])
            for h in range(8):
                sl = cand[0:TP, h * 256:(h + 1) * 256]
                sl2 = pscr[0:TP, h * 256:(h + 1) * 256]
                S.op("dve", (lambda e, h=h, sl=sl: e.max(out=tops[0:TP, h, 0:8], in_=sl)), reads=[cand], writes=[tops])
                S.op("dve", (lambda e, h=h, sl=sl: e.max_index(out=topp[0:TP, h, 0:8], in_max=tops[0:TP, h, 0:8], in_values=sl)), reads=[cand, tops], writes=[topp])
                S.op("dve", (lambda e, h=h, sl=sl, sl2=sl2: e.match_replace(out=sl2, in_to_replace=tops[0:TP, h, 0:8], in_values=sl, imm_value=NEG)), reads=[cand, tops], writes=[pscr])
                S.op("dve", (lambda e, h=h, sl2=sl2: e.max(out=tops[0:TP, h, 8:16], in_=sl2)), reads=[pscr], writes=[tops])
                S.op("dve", (lambda e, h=h, sl2=sl2: e.max_index(out=topp[0:TP, h, 8:16], in_max=tops[0:TP, h, 8:16], in_values=sl2)), reads=[pscr, tops], writes=[topp])
            S.op("dve", lambda e: e.tensor_single_scalar(out=pij[0:TP, 0], in_=topp[0:TP], scalar=4, op=ALU.logical_shift_right), reads=[topp], writes=[pij])
            S.op("dve", lambda e: e.tensor_single_scalar(out=pij[0:TP, 1], in_=topp[0:TP], scalar=15, op=ALU.bitwise_and), reads=[topp], writes=[pij])
            CP(pijf[0:TP], pij[0:TP], [pij], [pijf])
            sifv = sif[0:TP].rearrange("n (h p) k -> n h p k", p=2)
            oh = qT[0:TP, :, :].rearrange("n a b -> n (a b)")[:, 0:2048].rearrange("n (h k m) -> n h k m", h=8, k=16)
            for pp in range(2):
                TT(oh, iota16[0:TP, :].unsqueeze(1).unsqueeze(1).to_broadcast([TP, 8, 16, 16]), pijf[0:TP, pp].unsqueeze(3).to_broadcast([TP, 8, 16, 16]), ALU.is_equal, [iota16, pijf], [qT])
                TT(oh, oh, sifv[:, :, pp, :].unsqueeze(2).to_broadcast([TP, 8, 16, 16]), ALU.mult, [qT, sif], [qT])
                S.op("dve", (lambda e, pp=pp: e.tensor_reduce(out=esel[0:TP, pp], in_=oh, axis=AX.X, op=ALU.add)), reads=[qT], writes=[esel])
            TT(gate[0:TP], tops[0:TP], tops[0:TP, :, 0:1].to_broadcast([TP, 8, 16]), ALU.subtract, [tops], [gate])
            ACT(gate[0:TP], gate[0:TP], AF.Exp, [gate], [gate])
            S.op("dve", lambda e: e.tensor_reduce(out=small[0:TP, 0:8], in_=gate[0:TP], axis=AX.X, op=ALU.add), reads=[gate], writes=[small])
            S.op("dve", lambda e: e.reciprocal(out=small[0:TP, 8:16], in_=small[0:TP, 0:8]), reads=[small], writes=[small])
            TT(gate[0:TP], gate[0:TP], small[0:TP, 8:16].unsqueeze(2).to_broadcast([TP, 8, 16]), ALU.mult, [gate, small], [gate])
            p = psr.next()
            srcs = (esel[0:TP, 0].rearrange("n h k -> n (h k)"), esel[0:TP, 1].rearrange("n h k -> n (h k)"), gate[0:TP].rearrange("n h k -> n (h k)"))
            for k in range(3 if 'noslot' not in DBGF else 0):
                TR(p[:, k * 128:k * 128 + TP], srcs[k], TP, [esel, gate], [p])
            for k in range(3 if 'noslot' not in DBGF else 0):
                CP(slotT[:, k, po:po + TP], p[:, k * 128:k * 128 + TP], [p], [slotT])

        def dense(l, x1l, xouts, nts, pos):
            NK = len(nts)
            NTOT = sum(nts)
            Cv = C_all[:, 0:NTOT * 64].rearrange("p (n j) -> p n j", j=64)
            pout = [[psB[2 * k], psB[2 * k + 1]] for k in range(NK)]
            for half in range(2):
                for c0 in range(0, NTOT, 8):
                    oi = OIr.next()
                    oj = OJr.next()
                    TT(oi[:, :, :], iota128[:, :].unsqueeze(1).to_broadcast([128, 8, 128]), slotT[:, 0, c0:c0 + 8].unsqueeze(2).to_broadcast([128, 8, 128]), ALU.is_equal, [iota128, slotT], [oi])
                    TT(oj[:, :, :], iota128[:, half * 64:(half + 1) * 64].unsqueeze(1).to_broadcast([128, 8, 64]), slotT[:, 1, c0:c0 + 8].unsqueeze(2).to_broadcast([128, 8, 64]), ALU.is_equal, [iota128, slotT], [oj])
                    TT(oj[:, :, :], oj[:, :, :], slotT[:, 2, c0:c0 + 8].unsqueeze(2).to_broadcast([128, 8, 64]), ALU.mult, [oj, slotT], [oj])
                    pc = psr.next()
                    for t in range(8):
                        MM(pc[:, t * 64:(t + 1) * 64], oi[:, t, :], oj[:, t, :], True, True, [oi, oj], [pc])
                    S.op("act", (lambda e, pc=pc, c0=c0: e.copy(out=C_all[:, c0 * 64:(c0 + 8) * 64], in_=pc[:, :])), reads=[pc], writes=[C_all])

                def ph_stage(j):
                    ut = UTr.next()
                    DMA(ut[:], UT[l, j], reads=[UTbuf[l][j]], writes=[ut])
                    vb = VBr.next()
                    DMA(vb[:], VB[l, j], reads=[VBbuf[l][j]], writes=[vb])
                    ph = psr.next()
                    for c in range(8):
                        MM(ph[:, 0:NTOT], ut[:, c * 128:(c + 1) * 128], x1FMb[:, c, 0:NTOT], c == 0, c == 7, [ut, x1FMb], [ph])
                    return ph, vb
                cur = ph_stage(half * 64)
                for jl in range(64):
                    j = half * 64 + jl
                    nxt = ph_stage(j + 1) if jl + 1 < 64 else None
                    ph, vb = cur
                    G = Gr.next()
                    ACT(G[:, 0:NTOT], ph[:, 0:NTOT], AF.Gelu, [ph], [G])
                    Wt = Wr.next()
                    TT(Wt[:, 0:NTOT], G[:, 0:NTOT], Cv[:, 0:NTOT, jl], ALU.mult, [G, C_all], [Wt])
                    for k in range(NK):
                        for nh in range(2):
                            MM(pout[k][nh][0:nts[k], :], Wt[:, pos[k]:pos[k] + nts[k]], vb[:, nh * 512:(nh + 1) * 512], j == 0, j == 127, [Wt, vb], [pout[k][nh]])
                    cur = nxt
            for k in range(NK):
                TP = nts[k]
                for nh in range(2):
                    STT(rbuf[0:TP, nh * 512:(nh + 1) * 512], x1l[k][0:TP, nh * 512:(nh + 1) * 512], ALPHA, pout[k][nh][0:TP, :], ALU.mult, ALU.add, [x1l[k], pout[k][nh]], [rbuf])
                ln_from_rbuf(TP, 2, 3, xouts[k])

        def run_group(srcs, dsts, nts, L):
            NK = len(nts)
            pos = [0, nts[0]][:NK]
            for k in range(NK):
                DMA(xA[k][0:nts[k], :], srcs[k], writes=[xA[k]])
            cur_, nxt_ = xA, xB
            for l in range(NL):
                for k in range(NK):
                    mix_topk(l, cur_[k], x1s[k], nts[k], L, pos[k])
                dense(l, x1s[:NK], nxt_[:NK], nts, pos)
                cur_, nxt_ = nxt_, cur_
            for k in range(NK):
                DMA(dsts[k], cur_[k][0:nts[k], :], reads=[cur_[k]])

        init_states_zero()
        nblk = T // TB
        b = 0
        while b < nblk:
            nk = 2 if b + 1 < nblk else 1
            run_group([xp[(b + k) * TB:(b + k + 1) * TB, :] for k in range(nk)], [yp[(b + k) * TB:(b + k + 1) * TB, :] for k in range(nk)], [TB] * nk, 64)
            b += nk
        out_states(0)
        if with_sample:
            init_states_sample()
            run_group([xsm[:, :]], [ys[:, :]], [32], 32)
            out_states(1)
        with nc.allow_low_precision("bf16 matmul operands"):
            S.run()
    return nc


W_NAMES = ["w_in", "b_in", "conv_a_w", "conv_b_w", "conv_b_b", "lru_wa", "lru_ba", "lru_wx", "lru_bx", "lru_lambda",
           "hgrn_lb_logits", "hgrn_norm_g", "w_out_a", "w_out_b", "w_out_c", "w_o", "ln1_g", "ln1_b", "peer_wq",
           "peer_keys", "peer_u", "peer_v", "ln2_g", "ln2_b"]


def run(inputs, T, NL=DEPTH, with_sample=True, dbg=None, ncores=8):
    import time
    t0 = time.time()
    nc = build(T, NL, with_sample, dbg=dbg)
    print('build time', time.time() - t0, flush=True)
    f = lambda a: np.ascontiguousarray(np.asarray(a, dtype=np.float32))
    wmaps = {k: f(inputs[k]) for k in W_NAMES}
    in_maps = []
    for c in range(ncores):
        m = dict(wmaps)
        m["xp"] = f(inputs["x_prompt"][c % 4, :T])
        m["xs"] = f(inputs["x_sample"][c])
        m["st_ca"] = f(inputs["state_conv_a"][:, c])
        m["st_cb"] = f(inputs["state_conv_b"][:, c])
        m["st_h"] = f(inputs["state_lru"][:, c])
        m["st_S"] = f(inputs["state_hgrn"][:, c])
        in_maps.append(m)
    res = run_bass_kernel_spmd(nc, in_maps, core_ids=list(range(ncores)))
    print('total time', time.time() - t0, flush=True)
    return res.results


def kernel(**inputs):
    T = inputs["x_prompt"].shape[1]
    r = run(inputs, T)
    yp = np.stack([r[c]["yp"] for c in range(4)], 0)
    ys = np.stack([r[c]["ys"] for c in range(8)], 0)
    def pst(name):
        return np.stack([r[c][name] for c in range(4)], 1)
    def sst(name):
        return np.stack([r[c][name] for c in range(8)], 1)
    return (yp, ys, pst("p_ca"), pst("p_cb"), pst("p_h"), pst("p_S"),
            sst("s_ca"), sst("s_cb"), sst("s_h"), sst("s_S"))
```

```python
import numpy as np
from contextlib import ExitStack
import concourse.bass as bass
import concourse.mybir as mybir
from concourse.bass_utils import run_bass_kernel_spmd

F32 = mybir.dt.float32
BF16 = mybir.dt.bfloat16
U32 = mybir.dt.uint32
I32 = mybir.dt.int32
AF = mybir.ActivationFunctionType
ALU = mybir.AluOpType
AX = mybir.AxisListType

D = 1024
NCB = 8
DEPTH = 2
ALPHA = (2.0 * DEPTH) ** 0.25
LN_EPS = 1e-5
RMS_EPS = 1e-6
NEG = -1.0e30


class Tl:
    def __init__(self, t, name=""):
        self.t = t
        self.name = name
        self.last_w = None
        self.readers = []

    def __getitem__(self, idx):
        return self.t[idx]


class Alias:
    def __init__(self, base, ap):
        self.base = base
        self.t = ap

    def __getitem__(self, idx):
        return self.t[idx]

    last_w = property(lambda s: s.base.last_w, lambda s, v: setattr(s.base, "last_w", v))
    readers = property(lambda s: s.base.readers, lambda s, v: setattr(s.base, "readers", v))


class Op:
    __slots__ = ("eng", "fn", "deps", "signal", "sem", "semval", "dma", "idx", "eidx")


class Sched:
    ENG = ("pe", "act", "dve", "pool", "sp")

    def __init__(self, nc, stack, n_dma_sems=16):
        self.nc = nc
        self.ops = []
        self.per_eng = {e: [] for e in self.ENG}
        self.esem = {e: stack.enter_context(nc.semaphore("es_" + e)) for e in self.ENG}
        self.dsems = {}
        self.dcnt = {}
        self.drr = {}
        for e in ("sp", "pool"):
            self.dsems[e] = [stack.enter_context(nc.semaphore("ds_%s%d" % (e, i))) for i in range(n_dma_sems)]
            self.dcnt[e] = [0] * n_dma_sems
            self.drr[e] = 0
        self.dlast = {}
        self.stack = stack

    def sb(self, name, shape, dt=F32):
        n = 1
        for d_ in shape[1:]:
            n *= d_
        self.sbytes = getattr(self, "sbytes", 0) + n * (2 if dt == BF16 else 4)
        t = self.stack.enter_context(self.nc.sbuf_tensor(name, list(shape), dt))
        return Tl(t, name)

    def ps(self, name, shape, dt=F32):
        t = self.stack.enter_context(self.nc.psum_tensor(name, list(shape), dt))
        return Tl(t, name)

    def op(self, eng, fn, reads=(), writes=(), dma=False):
        o = Op()
        o.eng = eng
        o.fn = fn
        o.dma = dma
        o.signal = False
        o.sem = None
        o.semval = None
        o.idx = len(self.ops)
        o.eidx = len(self.per_eng[eng])
        deps = []
        for r in reads:
            if r is not None and r.last_w is not None:
                deps.append(r.last_w)
        for w in writes:
            if w is None:
                continue
            if w.last_w is not None:
                deps.append(w.last_w)
            deps.extend(w.readers)
        if dma:
            k = self.drr[eng]
            self.drr[eng] = (k + 1) % len(self.dsems[eng])
            prev = self.dlast.get((eng, k))
            if prev is not None:
                deps.append(prev)
            self.dcnt[eng][k] += 16
            o.sem = self.dsems[eng][k]
            o.semval = self.dcnt[eng][k]
            self.dlast[(eng, k)] = o
            o.signal = True
        seen = set()
        dd = []
        for d in deps:
            if d is o or id(d) in seen:
                continue
            seen.add(id(d))
            if (not d.dma) and d.eng == "pe" and eng == "pe" and not dma:
                continue
            dd.append(d)
            if not d.dma:
                d.signal = True
        o.deps = dd
        for r in reads:
            if r is not None:
                r.readers.append(o)
        for w in writes:
            if w is not None:
                w.last_w = o
                w.readers = []
        self.ops.append(o)
        self.per_eng[eng].append(o)
        return o

    def finalize(self):
        for e in self.ENG:
            c = 0
            for o in self.per_eng[e]:
                if o.dma:
                    continue
                if o.signal:
                    c += 1
                    o.sem = self.esem[e]
                    o.semval = c

    def emit_engine(self, ename, eng):
        seen = {}
        for o in self.per_eng[ename]:
            need = {}
            for d in o.deps:
                if (not d.dma) and (not o.dma) and d.eng == ename and ename in ("act", "dve") and o.eidx - d.eidx >= 4:
                    continue
                key = id(d.sem)
                if key not in need or need[key][1] < d.semval:
                    need[key] = (d.sem, d.semval)
            for key, (sem, val) in need.items():
                if seen.get(key, 0) >= val:
                    continue
                eng.wait_ge(sem, val)
                seen[key] = val
            inst = o.fn(eng)
            if o.signal:
                inst.then_inc(o.sem, 16 if o.dma else 1)
        if ename in self.dsems:
            for k, s in enumerate(self.dsems[ename]):
                v = self.dcnt[ename][k]
                if v > 0 and seen.get(id(s), 0) < v:
                    eng.wait_ge(s, v)

    def run(self):
        self.finalize()
        S = self
        with self.nc.Block() as block:
            @block.tensor
            def _(e):
                S.emit_engine("pe", e)

            @block.scalar
            def _(e):
                S.emit_engine("act", e)

            @block.vector
            def _(e):
                S.emit_engine("dve", e)

            @block.gpsimd
            def _(e):
                S.emit_engine("pool", e)

            @block.sync
            def _(e):
                S.emit_engine("sp", e)


class Ring:
    def __init__(self, tiles):
        self.tiles = tiles
        self.i = 0

    def next(self):
        t = self.tiles[self.i]
        self.i = (self.i + 1) % len(self.tiles)
        return t


def build(T, NL=DEPTH, with_sample=True, NWR=4, NGR=3, dbg=None):
    import os
    DBGF = os.environ.get('KDBG', '')
    TB = 128
    nc = bass.Bass("TRN2", target_bir_lowering=False)

    def din(name, shape, dt=F32):
        return nc.dram_tensor(name, list(shape), dt, kind="ExternalInput").ap()

    def dout(name, shape, dt=F32):
        return nc.dram_tensor(name, list(shape), dt, kind="ExternalOutput").ap()

    xp = din("xp", [T, D])
    xsm = din("xs", [32, D])
    st_ca = din("st_ca", [DEPTH, 2, D])
    st_cb = din("st_cb", [DEPTH, 3, D])
    st_h = din("st_h", [DEPTH, D])
    st_S = din("st_S", [DEPTH, 8, 128, 128])
    w_in = din("w_in", [DEPTH, D, 12 * D])
    b_in = din("b_in", [DEPTH, 12 * D])
    conv_a_w = din("conv_a_w", [DEPTH, 3, D])
    conv_b_w = din("conv_b_w", [DEPTH, 4, D])
    conv_b_b = din("conv_b_b", [DEPTH, D])
    lru_wa = din("lru_wa", [DEPTH, 8, 128, 128])
    lru_ba = din("lru_ba", [DEPTH, D])
    lru_wx = din("lru_wx", [DEPTH, 8, 128, 128])
    lru_bx = din("lru_bx", [DEPTH, D])
    lru_lambda = din("lru_lambda", [DEPTH, D])
    lb_logits = din("hgrn_lb_logits", [DEPTH, D])
    norm_g = din("hgrn_norm_g", [DEPTH, 128])
    w_out = [din("w_out_a", [DEPTH, D, D]), din("w_out_b", [DEPTH, D, D]), din("w_out_c", [DEPTH, D, D])]
    w_o = din("w_o", [DEPTH, D, D])
    ln1_g = din("ln1_g", [DEPTH, D])
    ln1_b = din("ln1_b", [DEPTH, D])
    peer_wq = din("peer_wq", [DEPTH, D, 2048])
    peer_keys = din("peer_keys", [DEPTH, 8, 2, 128, 128])
    peer_u = din("peer_u", [DEPTH, 16384, D])
    peer_v = din("peer_v", [DEPTH, 16384, D])
    peer_u2 = peer_u.rearrange("l e d -> (l e) d")
    peer_v2 = peer_v.rearrange("l e d -> (l e) d")
    UT = nc.dram_tensor("UT_scr", [DEPTH, 128, 128, 1024], BF16, kind="Internal").ap()
    VB = nc.dram_tensor("VB_scr", [DEPTH, 128, 128, 1024], BF16, kind="Internal").ap()
    WS = nc.dram_tensor("WS_scr", [DEPTH, 144, 128, 1024], BF16, kind="Internal").ap()
    ln2_g = din("ln2_g", [DEPTH, D])
    ln2_b = din("ln2_b", [DEPTH, D])

    yp = dout("yp", [T, D])
    ys = dout("ys", [32, D])
    o_ca = [dout("p_ca", [DEPTH, 2, D]), dout("s_ca", [DEPTH, 2, D])]
    o_cb = [dout("p_cb", [DEPTH, 3, D]), dout("s_cb", [DEPTH, 3, D])]
    o_h = [dout("p_h", [DEPTH, D]), dout("s_h", [DEPTH, D])]
    o_S = [dout("p_S", [DEPTH, 8, 128, 128]), dout("s_S", [DEPTH, 8, 128, 128])]
    dbg_out = {}
    if dbg:
        for nm, shp in dbg.items():
            dbg_out[nm] = dout("dbg_" + nm, shp)

    with ExitStack() as st:
        S = Sched(nc, st)

        def DMA(out_ap, in_ap, reads=(), writes=(), slow=False):
            if slow:
                S.op("sp", lambda e: e.dma_start(out=out_ap, in_=in_ap, allow_slow_non_contiguous=True), reads=reads, writes=writes, dma=True)
            else:
                S.op("sp", lambda e: e.dma_start(out=out_ap, in_=in_ap), reads=reads, writes=writes, dma=True)

        def ACT(out_ap, in_ap, func, reads, writes, bias=None, scale=None):
            kw = {}
            if bias is not None:
                kw["bias"] = bias
            if scale is not None:
                kw["scale"] = scale
            S.op("act", lambda e: e.activation(out=out_ap, in_=in_ap, func=func, **kw), reads=reads, writes=writes)

        def TS(out_ap, in_ap, s1, s2, op0, op1, reads, writes, eng="dve"):
            if op1 is None:
                S.op(eng, lambda e: e.tensor_scalar(out=out_ap, in0=in_ap, scalar1=s1, scalar2=None, op0=op0), reads=reads, writes=writes)
            else:
                S.op(eng, lambda e: e.tensor_scalar(out=out_ap, in0=in_ap, scalar1=s1, scalar2=s2, op0=op0, op1=op1), reads=reads, writes=writes)

        def TT(out_ap, a_ap, b_ap, op, reads, writes, eng="dve"):
            S.op(eng, lambda e: e.tensor_tensor(out=out_ap, in0=a_ap, in1=b_ap, op=op), reads=reads, writes=writes)

        def STT(out_ap, in0, scalar, in1, op0, op1, reads, writes, accum=None, eng="dve"):
            if accum is None:
                S.op(eng, lambda e: e.scalar_tensor_tensor(out=out_ap, in0=in0, scalar=scalar, in1=in1, op0=op0, op1=op1), reads=reads, writes=writes)
            else:
                S.op(eng, lambda e: e.scalar_tensor_tensor(out=out_ap, in0=in0, scalar=scalar, in1=in1, op0=op0, op1=op1, accum_out=accum), reads=reads, writes=writes)

        def CP(out_ap, in_ap, reads, writes, eng="dve"):
            S.op(eng, lambda e: e.tensor_copy(out=out_ap, in_=in_ap), reads=reads, writes=writes)

        def MSET(ap, val, writes, eng="pool"):
            S.op(eng, lambda e: e.memset(ap, val), writes=writes)

        def MM(out_ap, lhsT, rhs, start, stop, reads, writes):
            S.op("pe", lambda e: e.matmul(out_ap, lhsT=lhsT, rhs=rhs, start=start, stop=stop), reads=reads, writes=writes)

        def TR(out_ap, in_ap, k, reads, writes):
            S.op("pe", lambda e: e.transpose(out_ap, in_ap, ident[0:k, 0:k]), reads=list(reads) + [ident], writes=writes)

        ident = S.sb("ident", [128, 128])
        ones = S.sb("ones", [128, 128])
        tri = S.sb("tri", [64, 64])
        cmask = {64: S.sb("cmask64", [128, TB]), 32: S.sb("cmask32", [128, TB])}
        iota16 = S.sb("iota16", [128, 16])
        iota_i = S.sb("iota_i", [128, 16], I32)
        psr = Ring([S.ps("ps%d" % i, [128, 512]) for i in range(4)])
        psB = [S.ps("psB%d" % i, [128, 512]) for i in range(4)]
        wring = Ring([S.sb("wr%d" % i, [128, 1024]) for i in range(NWR)])
        zps = [Ring([S.sb("z%d_%d" % (k, i), [128, TB + 4]) for i in range(23)]) for k in range(2)]
        zr = zps[0]
        xA = [S.sb("xa%d" % i, [128, D]) for i in range(2)]
        xB = [S.sb("xb%d" % i, [128, D]) for i in range(2)]
        x1s = [S.sb("x1_%d" % i, [128, D]) for i in range(2)]
        rbuf = S.sb("rbuf", [128, D])
        aABC = S.sb("aABC", [128, 8, 3, TB], BF16)
        aV = [[Tl(aABC.t) for _ in range(3)] for _ in range(8)]
        merged = S.sb("merged", [128, 8, TB], BF16)
        s2 = S.sb("s2", [128, 128])
        C_all = S.sb("C_all", [128, 128 * 128], BF16)
        if 'noalias' in DBGF:
            pscr = S.sb("pscr", [128, 2048]); cand = S.sb("cand", [128, 2048]); qT = S.sb("qT", [128, 16, TB])
        else:
            pscr = Alias(C_all, C_all[:, 0:4096].bitcast(F32))
            cand = Alias(C_all, C_all[:, 4096:8192].bitcast(F32))
            qT = Alias(C_all, C_all[:, 8192:12288].bitcast(F32).rearrange("p (a b) -> p a b", b=128))
        x1FMb = S.sb("x1FMb", [128, 8, 2 * TB], BF16)
        iota128 = S.sb("iota128", [128, 128])
        iota128i = S.sb("iota128i", [128, 128], I32)
        slotT = S.sb("slotT", [128, 3, 256])
        OIr = Ring([S.sb("oi%d" % i, [128, 8, 128], BF16) for i in range(2)])
        OJr = Ring([S.sb("oj%d" % i, [128, 8, 64], BF16) for i in range(2)])
        UTr = Ring([S.sb("utr%d" % i, [128, 1024], BF16) for i in range(3)])
        VBr = Ring([S.sb("vbr%d" % i, [128, 1024], BF16) for i in range(4)])
        Gr = Ring([S.sb("gg%d" % i, [128, 256]) for i in range(3)])
        Wr = Ring([S.sb("wt%d" % i, [128, 256], BF16) for i in range(3)])
        wbr = Ring([S.sb("wbr%d" % i, [128, 1024], BF16) for i in range(5)])
        WSbuf = [[Tl(None) for _ in range(144)] for _ in range(NL)]
        xFMb = S.sb("xFMb", [128, 8, TB], BF16)
        UTbuf = [[Tl(None) for _ in range(128)] for _ in range(NL)]
        VBbuf = [[Tl(None) for _ in range(128)] for _ in range(NL)]
        lnp = [S.sb("lnp%d" % i, [128, D]) for i in range(4)]
        keysT = [S.sb("keysT%d" % l, [128, 16, 128]) for l in range(NL)]
        PA = [S.sb("PA%d" % l, [128, 96]) for l in range(NL)]
        PB = [S.sb("PB%d" % l, [128, 128]) for l in range(NL)]
        PD = [S.sb("PD%d" % l, [128, 4, 8]) for l in range(NL)]
        stg = S.sb("stg", [128, 128])
        Sst = [S.sb("Sst%d" % l, [128, 8, 128]) for l in range(NL)]
        SstV = [[Tl(Sst[l].t) for _ in range(8)] for l in range(NL)]
        cAst = [S.sb("cAst%d" % l, [128, 8, 2]) for l in range(NL)]
        cAV = [[Tl(cAst[l].t) for _ in range(8)] for l in range(NL)]
        cBst = [S.sb("cBst%d" % l, [128, 8, 3]) for l in range(NL)]
        cBV = [[Tl(cBst[l].t) for _ in range(8)] for l in range(NL)]
        hst = [S.sb("hst%d" % l, [128, 8]) for l in range(NL)]
        hV = [[Tl(hst[l].t) for _ in range(8)] for l in range(NL)]
        small = S.sb("small", [128, 64])
        sv = S.sb("sv", [128, 16, 16])
        si = S.sb("si", [128, 16, 16], U32)
        sif = S.sb("sif", [128, 16, 16])
        tops = S.sb("tops", [128, 8, 16])
        topp = S.sb("topp", [128, 8, 16], U32)
        pij = S.sb("pij", [128, 2, 8, 16], U32)
        pijf = S.sb("pijf", [128, 2, 8, 16])
        esel = S.sb("esel", [128, 2, 8, 16])
        gate = S.sb("gate", [128, 8, 16])
        mvt = S.sb("mvt", [128, 16])
        bst = S.sb("bst", [128, 12])

        print('SBUF bytes/partition', S.sbytes, flush=True)
        MSET(ident[:], 0.0, [ident])
        S.op("pool", lambda e: e.affine_select(out=ident[:], in_=ident[:], pattern=[[-1, 128]], compare_op=ALU.not_equal, fill=1.0, base=0, channel_multiplier=1), reads=[ident], writes=[ident])
        MSET(ones[:], 1.0, [ones])
        MSET(tri[:], 1.0, [tri])
        S.op("pool", lambda e: e.affine_select(out=tri[:], in_=tri[:], pattern=[[1, 64]], compare_op=ALU.is_ge, fill=0.0, base=0, channel_multiplier=-1), reads=[tri], writes=[tri])
        for L in (64, 32):
            MSET(cmask[L][:], 1.0, [cmask[L]])
            for c in range(TB // L):
                MSET(cmask[L][:, c * L:c * L + 1], 0.0, [cmask[L]])
        S.op("pool", lambda e: e.iota(iota_i[:], pattern=[[1, 16]], base=0, channel_multiplier=0), writes=[iota_i])
        CP(iota16[:], iota_i[:], [iota_i], [iota16])
        S.op("pool", lambda e: e.iota(iota128i[:], pattern=[[1, 128]], base=0, channel_multiplier=0), writes=[iota128i])
        CP(iota128[:], iota128i[:], [iota128i], [iota128])

        for l in range(NL):
            MSET(stg[:], 0.0, [stg])
            DMA(stg[0:96, :], b_in[l].rearrange("(r p) -> r p", p=128), writes=[stg])
            p = psr.next()
            TR(p[:, 0:128], stg[:], 128, [stg], [p])
            CP(PA[l][:], p[:, 0:96], [p], [PA[l]])
            MSET(stg[:], 0.0, [stg])
            DMA(stg[0:24, :], conv_a_w[l].rearrange("k (c p) -> (k c) p", p=128), writes=[stg])
            DMA(stg[24:56, :], conv_b_w[l].rearrange("k (c p) -> (k c) p", p=128), writes=[stg])
            DMA(stg[56:64, :], conv_b_b[l].rearrange("(c p) -> c p", p=128), writes=[stg])
            DMA(stg[64:72, :], lru_ba[l].rearrange("(c p) -> c p", p=128), writes=[stg])
            DMA(stg[72:80, :], lru_bx[l].rearrange("(c p) -> c p", p=128), writes=[stg])
            DMA(stg[80:88, :], lru_lambda[l].rearrange("(c p) -> c p", p=128), writes=[stg])
            DMA(stg[88:96, :], lb_logits[0].rearrange("(c p) -> c p", p=128), writes=[stg])
            DMA(stg[96:104, :], lb_logits[1].rearrange("(c p) -> c p", p=128), writes=[stg])
            DMA(stg[104:105, :], norm_g[l:l + 1, :], writes=[stg])
            p = psr.next()
            TR(p[:, 0:128], stg[:], 128, [stg], [p])
            CP(PB[l][:], p[:, 0:128], [p], [PB[l]])
            ACT(PD[l][:, 0, :], PB[l][:, 80:88], AF.Exp, [PB[l]], [PD[l]], scale=-1.0)
            ACT(PD[l][:, 0, :], PD[l][:, 0, :], AF.Ln, [PD[l]], [PD[l]], bias=1.0)
            TS(PD[l][:, 1, :], PD[l][:, 0, :], -16.0, None, ALU.mult, None, [PD[l]], [PD[l]])
            TS(PD[l][:, 0, :], PD[l][:, 0, :], -8.0, None, ALU.mult, None, [PD[l]], [PD[l]])
            if l == 0:
                MSET(PD[l][:, 2, :], 0.0, [PD[l]], eng="dve")
                MSET(PD[l][:, 3, :], 1.0, [PD[l]], eng="dve")
            else:
                TT(PD[l][:, 2, :], PB[l][:, 96:104], PB[l][:, 88:96], ALU.subtract, [PB[l]], [PD[l]])
                ACT(PD[l][:, 2, :], PD[l][:, 2, :], AF.Sigmoid, [PD[l]], [PD[l]])
                TS(PD[l][:, 3, :], PD[l][:, 2, :], -1.0, 1.0, ALU.mult, ALU.add, [PD[l]], [PD[l]])
            for hp in range(16):
                w = wring.next()
                DMA(w[:, 0:128], peer_keys[l, hp // 2, hp % 2], writes=[w])
                p = psr.next()
                TR(p[:, 0:128], w[:, 0:128], 128, [w], [p])
                CP(keysT[l][:, hp, :], p[:, 0:128], [p], [keysT[l]], eng="dve")

        import os
        DBGF = os.environ.get('KDBG', '')
        def pipelined(items, load_fn, finish_fn, depth):
            staged = []
            for it in items:
                staged.append((it, load_fn(it)))
                if len(staged) > depth:
                    finish_fn(*staged.pop(0))
            while staged:
                finish_fn(*staged.pop(0))

        def tab_load(it):
            l, j, kind = it
            src = (peer_u if kind == 0 else peer_v)[l].rearrange("(i j) d -> j i d", j=128)[j]
            g = wring.next()
            DMA(g[:], src, writes=[g])
            return g

        def tab_finish(it, g):
            l, j, kind = it
            if kind == 0:
                ut = UTr.next()
                for c4 in range(2):
                    p = psr.next()
                    for k in range(4):
                        c = c4 * 4 + k
                        TR(p[:, k * 128:(k + 1) * 128], g[:, c * 128:(c + 1) * 128], 128, [g], [p])
                    S.op("act", (lambda e, ut=ut, p=p, c4=c4: e.copy(out=ut[:, c4 * 512:(c4 + 1) * 512], in_=p[:, :])), reads=[p], writes=[ut])
                DMA(UT[l, j], ut[:], reads=[ut], writes=[UTbuf[l][j]])
            else:
                vb = VBr.next()
                CP(vb[:], g[:], [g], [vb], eng="dve")
                DMA(VB[l, j], vb[:], reads=[vb], writes=[VBbuf[l][j]])

        pipelined([(l, j, kind) for l in range(NL) for j in range(128) for kind in range(2)], tab_load, tab_finish, NWR - 1)

        def init_states_zero():
            for l in range(NL):
                MSET(Sst[l][:], 0.0, SstV[l])
                MSET(cAst[l][:], 0.0, cAV[l])
                MSET(cBst[l][:], 0.0, cBV[l])
                MSET(hst[l][:], 0.0, hV[l])

        def init_states_sample():
            for l in range(NL):
                DMA(Sst[l][:], st_S[l].rearrange("h d e -> d h e"), writes=SstV[l])
                for cb in range(8):
                    DMA(cAst[l][:, cb, :], st_ca[l][:, cb * 128:(cb + 1) * 128].rearrange("k p -> p k"), writes=[cAV[l][cb]], slow=True)
                    DMA(cBst[l][:, cb, :], st_cb[l][:, cb * 128:(cb + 1) * 128].rearrange("k p -> p k"), writes=[cBV[l][cb]], slow=True)
                DMA(hst[l][:], st_h[l].rearrange("(c p) -> p c", p=128), writes=hV[l], slow=True)

        def out_states(which):
            for l in range(NL):
                DMA(o_S[which][l].rearrange("h d e -> d h e"), Sst[l][:], reads=SstV[l])
                for cb in range(8):
                    DMA(o_ca[which][l][:, cb * 128:(cb + 1) * 128].rearrange("k p -> p k"), cAst[l][:, cb, :], reads=[cAV[l][cb]], slow=True)
                    DMA(o_cb[which][l][:, cb * 128:(cb + 1) * 128].rearrange("k p -> p k"), cBst[l][:, cb, :], reads=[cBV[l][cb]], slow=True)
                DMA(o_h[which][l].rearrange("(c p) -> p c", p=128), hst[l][:], reads=hV[l], slow=True)

        def wsrc(l, idx):
            if idx < 96:
                return w_in[l][:, idx * 128:(idx + 1) * 128].rearrange("(c p) n -> p c n", p=128), True
            if idx < 120:
                X, j = (idx - 96) // 8, (idx - 96) % 8
                return w_out[X][l][:, j * 128:(j + 1) * 128].rearrange("(c p) n -> p c n", p=128), True
            if idx < 128:
                j = idx - 120
                return w_o[l, j * 128:(j + 1) * 128, :], False
            hp = idx - 128
            return peer_wq[l][:, hp * 128:(hp + 1) * 128].rearrange("(c p) n -> p c n", p=128), True

        def w_load(it):
            l, idx = it
            src, tiled = wsrc(l, idx)
            w = wring.next()
            if tiled:
                DMA(w[:].rearrange("p (c n) -> p c n", n=128), src, writes=[w])
            else:
                DMA(w[:], src, writes=[w])
            return w

        def w_finish(it, w):
            l, idx = it
            wb = wbr.next()
            CP(wb[:], w[:], [w], [wb], eng="dve")
            DMA(WS[l, idx], wb[:], reads=[wb], writes=[WSbuf[l][idx]])

        pipelined([(l, idx) for l in range(NL) for idx in range(144)], w_load, w_finish, NWR - 1)

        def wl(l, idx):
            wb = wbr.next()
            DMA(wb[:], WS[l, idx], reads=[WSbuf[l][idx]], writes=[wb])
            return wb

        def in_proj(l, idx, nt):
            w = wl(l, idx)
            p = psr.next()
            for c in range(8):
                MM(p[:, 0:nt], w[:, c * 128:(c + 1) * 128], xFMb[:, c, 0:nt], c == 0, c == 7, [w, xFMb], [p])
            return p

        def ln_from_rbuf(TP, gi_, bi_, dst):
            for nh in range(2):
                S.op("dve", (lambda e, nh=nh: e.bn_stats(out=bst[0:TP, nh * 6:(nh + 1) * 6], in_=rbuf[0:TP, nh * 512:(nh + 1) * 512])), reads=[rbuf], writes=[bst])
            S.op("dve", lambda e: e.bn_aggr(out=mvt[0:TP, 0:2], in_=bst[0:TP, 0:12]), reads=[bst], writes=[mvt])
            TS(mvt[0:TP, 2:3], mvt[0:TP, 1:2], 1.0, LN_EPS, ALU.mult, ALU.add, [mvt], [mvt])
            ACT(mvt[0:TP, 2:3], mvt[0:TP, 2:3], AF.Sqrt, [mvt], [mvt])
            S.op("dve", lambda e: e.reciprocal(out=mvt[0:TP, 3:4], in_=mvt[0:TP, 2:3]), reads=[mvt], writes=[mvt])
            TS(rbuf[0:TP, :], rbuf[0:TP, :], mvt[0:TP, 0:1], mvt[0:TP, 3:4], ALU.subtract, ALU.mult, [rbuf, mvt], [rbuf])
            TT(rbuf[0:TP, :], rbuf[0:TP, :], lnp[gi_][0:TP, :], ALU.mult, [rbuf, lnp[gi_]], [rbuf])
            TT(dst[0:TP, :], rbuf[0:TP, :], lnp[bi_][0:TP, :], ALU.add, [rbuf, lnp[bi_]], [dst])

        def mix_topk(l, xin, x1, nt, L, po, tagdbg=None):
            TP = nt
            nch = nt // L
            for i, src in enumerate((ln1_g, ln1_b, ln2_g, ln2_b)):
                DMA(lnp[i][:], src[l:l + 1, :].to_broadcast([128, D]), writes=[lnp[i]])
            for c4 in range(2):
                p = psr.next()
                for k in range(4):
                    c = c4 * 4 + k
                    TR(p[:, k * 128:k * 128 + TP], xin[0:TP, c * 128:(c + 1) * 128], TP, [xin], [p])
                for k in range(4):
                    c = c4 * 4 + k
                    S.op("act", (lambda e, c=c, k=k, p=p: e.copy(out=xFMb[:, c, 0:TP], in_=p[:, k * 128:k * 128 + TP])), reads=[p], writes=[xFMb])
            def cb_body(cb, zp):
                def zin(G, func):
                    p = in_proj(l, G * 8 + cb, nt)
                    z = zp.next()
                    ACT(z[:, 0:nt], p[:, 0:nt], func, [p, PA[l]], [z], bias=PA[l][:, G * 8 + cb:G * 8 + cb + 1])
                    return z
                zB = zin(0, AF.Identity)
                yield
                zC = zin(1, AF.Identity)
                yield
                zxA = zin(2, AF.Identity)
                yield
                ub = zp.next()
                yield
                CP(ub[:, 0:2], cAst[l][:, cb, :], [cAV[l][cb]], [ub], eng="pool")
                yield
                TT(ub[:, 2:2 + nt], zC[:, 0:nt], zxA[:, 0:nt], ALU.mult, [zC, zxA], [ub])
                yield
                CP(cAst[l][:, cb, :], ub[:, nt:nt + 2], [ub], [cAV[l][cb]], eng="pool")
                yield
                y = zp.next()
                yield
                TS(y[:, 0:nt], ub[:, 0:nt], PB[l][:, cb:cb + 1], None, ALU.mult, None, [ub, PB[l]], [y])
                yield
                STT(y[:, 0:nt], ub[:, 1:1 + nt], PB[l][:, 8 + cb:9 + cb], y[:, 0:nt], ALU.mult, ALU.add, [ub, y, PB[l]], [y])
                yield
                STT(y[:, 0:nt], ub[:, 2:2 + nt], PB[l][:, 16 + cb:17 + cb], y[:, 0:nt], ALU.mult, ALU.add, [ub, y, PB[l]], [y])
                yield
                TT(aABC[:, cb, 0, 0:nt], zB[:, 0:nt], y[:, 0:nt], ALU.mult, [zB, y], [aV[cb][0]])
                yield
                zxB = zin(3, AF.Identity)
                yield
                ggB = zin(4, AF.Gelu)
                yield
                cbuf = zp.next()
                yield
                CP(cbuf[:, 0:3], cBst[l][:, cb, :], [cBV[l][cb]], [cbuf], eng="pool")
                yield
                CP(cbuf[:, 3:3 + nt], zxB[:, 0:nt], [zxB], [cbuf], eng="pool")
                yield
                CP(cBst[l][:, cb, :], cbuf[:, nt:nt + 3], [cbuf], [cBV[l][cb]], eng="pool")
                yield
                xl = zp.next()
                yield
                TS(xl[:, 0:nt], cbuf[:, 0:nt], PB[l][:, 24 + cb:25 + cb], PB[l][:, 56 + cb:57 + cb], ALU.mult, ALU.add, [cbuf, PB[l]], [xl])
                yield
                for k in range(1, 4):
                    STT(xl[:, 0:nt], cbuf[:, k:k + nt], PB[l][:, 24 + 8 * k + cb:25 + 8 * k + cb], xl[:, 0:nt], ALU.mult, ALU.add, [cbuf, xl, PB[l]], [xl])
                    yield
                w = wring.next()
                yield
                DMA(w[:, 0:128], lru_wa[l, cb], writes=[w])
                yield
                DMA(w[:, 128:256], lru_wx[l, cb], writes=[w])
                yield
                p = psr.next()
                yield
                MM(p[:, 0:nt], w[:, 0:128], xl[:, 0:nt], True, True, [w, xl], [p])
                yield
                MM(p[:, 128:128 + nt], w[:, 128:256], xl[:, 0:nt], True, True, [w, xl], [p])
                yield
                r = zp.next()
                yield
                gi = zp.next()
                yield
                ACT(r[:, 0:nt], p[:, 0:nt], AF.Sigmoid, [p, PB[l]], [r], bias=PB[l][:, 64 + cb:65 + cb])
                yield
                ACT(gi[:, 0:nt], p[:, 128:128 + nt], AF.Sigmoid, [p, PB[l]], [gi], bias=PB[l][:, 72 + cb:73 + cb])
                yield
                a = zp.next()
                yield
                a2 = zp.next()
                yield
                ACT(a[:, 0:nt], r[:, 0:nt], AF.Exp, [r, PD[l]], [a], scale=PD[l][:, 0, cb:cb + 1])
                yield
                ACT(a2[:, 0:nt], r[:, 0:nt], AF.Exp, [r, PD[l]], [a2], scale=PD[l][:, 1, cb:cb + 1])
                yield
                TS(a2[:, 0:nt], a2[:, 0:nt], -1.0, 1.0, ALU.mult, ALU.add, [a2], [a2])
                yield
                ACT(a2[:, 0:nt], a2[:, 0:nt], AF.Sqrt, [a2], [a2])
                yield
                TT(gi[:, 0:nt], gi[:, 0:nt], a2[:, 0:nt], ALU.mult, [gi, a2], [gi])
                yield
                TT(gi[:, 0:nt], gi[:, 0:nt], xl[:, 0:nt], ALU.mult, [gi, xl], [gi])
                yield
                hh = zp.next()
                yield
                S.op("dve", (lambda e, hh=hh, a=a, gi=gi, cb=cb: e.tensor_tensor_scan(out=hh[:, 0:nt], data0=a[:, 0:nt], data1=gi[:, 0:nt], initial=hst[l][:, cb:cb + 1], op0=ALU.mult, op1=ALU.add)), reads=[a, gi, hV[l][cb]], writes=[hh])
                yield
                CP(hst[l][:, cb:cb + 1], hh[:, nt - 1:nt], [hh], [hV[l][cb]], eng="pool")
                yield
                TT(aABC[:, cb, 1, 0:nt], ggB[:, 0:nt], hh[:, 0:nt], ALU.mult, [ggB, hh], [aV[cb][1]])
                yield
                q = zin(5, AF.Silu)
                yield
                f = zin(6, AF.Sigmoid)
                yield
                vv = zin(7, AF.Identity)
                yield
                sg = zin(8, AF.Silu)
                yield
                TS(f[:, 0:nt], f[:, 0:nt], PD[l][:, 3, cb:cb + 1], PD[l][:, 2, cb:cb + 1], ALU.mult, ALU.add, [f, PD[l]], [f])
                yield
                kk = zp.next()
                yield
                TS(kk[:, 0:nt], f[:, 0:nt], -1.0, 1.0, ALU.mult, ALU.add, [f], [kk])
                yield
                lf = zp.next()
                yield
                ACT(lf[:, 0:nt], f[:, 0:nt], AF.Ln, [f], [lf])
                yield
                bb = zp.next()
                yield
                S.op("dve", (lambda e, bb=bb, lf=lf: e.tensor_tensor_scan(out=bb[:, 0:nt], data0=cmask[L][:, 0:nt], data1=lf[:, 0:nt], initial=0.0, op0=ALU.mult, op1=ALU.add)), reads=[lf, cmask[L]], writes=[bb])
                yield
                eb = zp.next()
                yield
                ACT(eb[:, 0:nt], bb[:, 0:nt], AF.Exp, [bb], [eb])
                yield
                ACT(bb[:, 0:nt], bb[:, 0:nt], AF.Exp, [bb], [bb], scale=-1.0)
                yield
                TT(q[:, 0:nt], q[:, 0:nt], eb[:, 0:nt], ALU.mult, [q, eb], [q])
                yield
                TT(kk[:, 0:nt], kk[:, 0:nt], bb[:, 0:nt], ALU.mult, [kk, bb], [kk])
                yield
                osb = zp.next()
                yield
                Sv = SstV[l][cb]
                yield
                for c in range(nch):
                    c0 = c * L
                    yield
                    pa = psr.next()
                    yield
                    MM(pa[0:L, 0:L], kk[:, c0:c0 + L], q[:, c0:c0 + L], True, True, [kk, q], [pa])
                    yield
                    att = zp.next()
                    yield
                    TT(att[0:L, 0:L], pa[0:L, 0:L], tri[0:L, 0:L], ALU.mult, [pa, tri], [att])
                    yield
                    pv = psr.next()
                    yield
                    TR(pv[0:L, 0:128], vv[:, c0:c0 + L], 128, [vv], [pv])
                    yield
                    vtm = zp.next()
                    yield
                    S.op("act", (lambda e, vtm=vtm, pv=pv: e.copy(out=vtm[0:L, 0:128], in_=pv[0:L, 0:128])), reads=[pv], writes=[vtm])
                    yield
                    po = psr.next()
                    yield
                    MM(po[:, 0:L], Sst[l][:, cb, :], q[:, c0:c0 + L], True, False, [Sv, q], [po])
                    yield
                    MM(po[:, 0:L], vtm[0:L, 0:128], att[0:L, 0:L], False, True, [vtm, att], [po])
                    yield
                    S.op("act", (lambda e, osb=osb, po=po, c0=c0: e.copy(out=osb[:, c0:c0 + L], in_=po[:, 0:L])), reads=[po], writes=[osb])
                    yield
                    k2 = zp.next()
                    yield
                    TS(k2[:, 0:L], kk[:, c0:c0 + L], eb[:, c0 + L - 1:c0 + L], None, ALU.mult, None, [kk, eb], [k2])
                    yield
                    pk = psr.next()
                    yield
                    TR(pk[0:L, 0:128], k2[:, 0:L], 128, [k2], [pk])
                    yield
                    k2t = zp.next()
                    yield
                    CP(k2t[0:L, 0:128], pk[0:L, 0:128], [pk], [k2t])
                    yield
                    pn = psr.next()
                    yield
                    MM(pn[:, 0:128], k2t[0:L, 0:128], vtm[0:L, 0:128], True, True, [k2t, vtm], [pn])
                    yield
                    STT(Sst[l][:, cb, :], Sst[l][:, cb, :], eb[:, c0 + L - 1:c0 + L], pn[:, 0:128], ALU.mult, ALU.add, [Sv, eb, pn], [Sv])
                    yield
                sq = zp.next()
                yield
                ACT(sq[:, 0:nt], osb[:, 0:nt], AF.Square, [osb], [sq])
                yield
                pss = psr.next()
                yield
                MM(pss[:, 0:nt], ones[:], sq[:, 0:nt], True, True, [ones, sq], [pss])
                yield
                TS(sq[:, 0:nt], pss[:, 0:nt], 1.0 / 128.0, RMS_EPS, ALU.mult, ALU.add, [pss], [sq])
                yield
                ACT(sq[:, 0:nt], sq[:, 0:nt], AF.Sqrt, [sq], [sq])
                yield
                S.op("dve", (lambda e, sq=sq: e.reciprocal(out=sq[:, 0:nt], in_=sq[:, 0:nt])), reads=[sq], writes=[sq])
                yield
                TT(osb[:, 0:nt], osb[:, 0:nt], sq[:, 0:nt], ALU.mult, [osb, sq], [osb])
                yield
                STT(aABC[:, cb, 2, 0:nt], osb[:, 0:nt], PB[l][:, 104:105], sg[:, 0:nt], ALU.mult, ALU.mult, [osb, sg, PB[l]], [aV[cb][2]])
                yield
            KI = 2
            for g0 in range(0, 8, KI):
                active = [cb_body(cb, zps[cb % KI]) for cb in range(g0, g0 + KI)]
                while active:
                    for g_ in list(active):
                        try:
                            next(g_)
                        except StopIteration:
                            active.remove(g_)
            for j in range(8):
                sgs = []
                for X in range(3):
                    G = 9 + X
                    p = in_proj(l, G * 8 + j, nt)
                    z = zr.next()
                    ACT(z[:, 0:nt], p[:, 0:nt], AF.Sigmoid, [p, PA[l]], [z], bias=PA[l][:, G * 8 + j:G * 8 + j + 1])
                    sgs.append(z)
                for X in range(3):
                    w = wl(l, 96 + X * 8 + j)
                    p = psr.next()
                    for cb in range(8):
                        MM(p[:, 0:nt], w[:, cb * 128:(cb + 1) * 128], aABC[:, cb, X, 0:nt], cb == 0, cb == 7, [w, aV[cb][X]], [p])
                    TT(sgs[X][:, 0:nt], sgs[X][:, 0:nt], p[:, 0:nt], ALU.mult, [sgs[X], p], [sgs[X]])
                TT(sgs[0][:, 0:nt], sgs[0][:, 0:nt], sgs[1][:, 0:nt], ALU.add, [sgs[0], sgs[1]], [sgs[0]])
                TT(merged[:, j, 0:nt], sgs[0][:, 0:nt], sgs[2][:, 0:nt], ALU.add, [sgs[0], sgs[2]], [merged])
            pw = psB[0:2]
            for j in range(8):
                w = wl(l, 120 + j)
                for nh in range(2):
                    MM(pw[nh][0:TP, :], merged[:, j, 0:TP], w[:, nh * 512:(nh + 1) * 512], j == 0, j == 7, [merged, w], [pw[nh]])

            def resid_ln(src, pws, gi_, bi_, dst):
                for nh in range(2):
                    STT(rbuf[0:TP, nh * 512:(nh + 1) * 512], src[0:TP, nh * 512:(nh + 1) * 512], ALPHA, pws[nh][0:TP, :], ALU.mult, ALU.add, [src, pws[nh]], [rbuf])
                ln_from_rbuf(TP, gi_, bi_, dst)

            resid_ln(xin, pw, 0, 1, x1)
            if tagdbg is not None and ("x1_%d" % l) in dbg_out:
                DMA(dbg_out["x1_%d" % l][tagdbg:tagdbg + TP, :], x1[0:TP, :], reads=[x1])

            for c4 in range(2):
                p = psr.next()
                for k in range(4):
                    c = c4 * 4 + k
                    TR(p[:, k * 128:k * 128 + TP], x1[0:TP, c * 128:(c + 1) * 128], TP, [x1], [p])
                for k in range(4):
                    c = c4 * 4 + k
                    S.op("act", (lambda e, c=c, k=k, p=p: e.copy(out=x1FMb[:, c, po:po + TP], in_=p[:, k * 128:k * 128 + TP])), reads=[p], writes=[x1FMb])
            for hp in range(16):
                w = wl(l, 128 + hp)
                p = psr.next()
                for c in range(8):
                    MM(p[:, 0:nt], w[:, c * 128:(c + 1) * 128], x1FMb[:, c, po:po + nt], c == 0, c == 7, [w, x1FMb], [p])
                S.op("act", (lambda e, hp=hp, p=p: e.copy(out=qT[:, hp, 0:nt], in_=p[:, 0:nt])), reads=[p], writes=[qT])
            for g4 in range(4):
                p = psr.next()
                for k in range(4):
                    hp = g4 * 4 + k
                    MM(p[0:TP, k * 128:(k + 1) * 128], qT[:, hp, 0:TP], keysT[l][:, hp, :], True, True, [qT, keysT[l]], [p])
                CP(pscr[0:TP, g4 * 512:(g4 + 1) * 512], p[0:TP, :], [p], [pscr])
            for hp in range(16):
                sl = pscr[0:TP, hp * 128:(hp + 1) * 128]
                S.op("dve", (lambda e, hp=hp, sl=sl: e.max(out=sv[0:TP, hp, 0:8], in_=sl)), reads=[pscr], writes=[sv])
                S.op("dve", (lambda e, hp=hp, sl=sl: e.max_index(out=si[0:TP, hp, 0:8], in_max=sv[0:TP, hp, 0:8], in_values=sl)), reads=[pscr, sv], writes=[si])
                S.op("dve", (lambda e, hp=hp, sl=sl: e.match_replace(out=s2[0:TP, :], in_to_replace=sv[0:TP, hp, 0:8], in_values=sl, imm_value=NEG)), reads=[pscr, sv], writes=[s2])
                S.op("dve", (lambda e, hp=hp: e.max(out=sv[0:TP, hp, 8:16], in_=s2[0:TP, :])), reads=[s2], writes=[sv])
                S.op("dve", (lambda e, hp=hp: e.max_index(out=si[0:TP, hp, 8:16], in_max=sv[0:TP, hp, 8:16], in_values=s2[0:TP, :])), reads=[s2, sv], writes=[si])
            CP(sif[0:TP], si[0:TP], [si], [sif])
            svv = sv[0:TP].rearrange("n (h p) k -> n h p k", p=2)
            cv = cand[0:TP, :].rearrange("n (h i j) -> n h i j", h=8, i=16)
            TT(cv, svv[:, :, 0, :].unsqueeze(3).to_broadcast([TP, 8, 16, 16]), svv[:, :, 1, :].unsqueeze(2).to_broadcast([TP, 8, 16, 16]), ALU.add, [sv], [cand])
            for h in range(8):
                sl = cand[0:TP, h * 256:(h + 1) * 256]
                sl2 = pscr[0:TP, h * 256:(h + 1) * 256]
                S.op("dve", (lambda e, h=h, sl=sl: e.max(out=tops[0:TP, h, 0:8], in_=sl)), reads=[cand], writes=[tops])
                S.op("dve", (lambda e, h=h, sl=sl: e.max_index(out=topp[0:TP, h, 0:8], in_max=tops[0:TP, h, 0:8], in_values=sl)), reads=[cand, tops], writes=[topp])
                S.op("dve", (lambda e, h=h, sl=sl, sl2=sl2: e.match_replace(out=sl2, in_to_replace=tops[0:TP, h, 0:8], in_values=sl, imm_value=NEG)), reads=[cand, tops], writes=[pscr])
                S.op("dve", (lambda e, h=h, sl2=sl2: e.max(out=tops[0:TP, h, 8:16], in_=sl2)), reads=[pscr], writes=[tops])
                S.op("dve", (lambda e, h=h, sl2=sl2: e.max_index(out=topp[0:TP, h, 8:16], in_max=tops[0:TP, h, 8:16], in_values=sl2)), reads=[pscr, tops], writes=[topp])
            S.op("dve", lambda e: e.tensor_single_scalar(out=pij[0:TP, 0], in_=topp[0:TP], scalar=4, op=ALU.logical_shift_right), reads=[topp], writes=[pij])
            S.op("dve", lambda e: e.tensor_single_scalar(out=pij[0:TP, 1], in_=topp[0:TP], scalar=15, op=ALU.bitwise_and), reads=[topp], writes=[pij])
            CP(pijf[0:TP], pij[0:TP], [pij], [pijf])
            sifv = sif[0:TP].rearrange("n (h p) k -> n h p k", p=2)
            oh = qT[0:TP, :, :].rearrange("n a b -> n (a b)")[:, 0:2048].rearrange("n (h k m) -> n h k m", h=8, k=16)
            for pp in range(2):
                TT(oh, iota16[0:TP, :].unsqueeze(1).unsqueeze(1).to_broadcast([TP, 8, 16, 16]), pijf[0:TP, pp].unsqueeze(3).to_broadcast([TP, 8, 16, 16]), ALU.is_equal, [iota16, pijf], [qT])
                TT(oh, oh, sifv[:, :, pp, :].unsqueeze(2).to_broadcast([TP, 8, 16, 16]), ALU.mult, [qT, sif], [qT])
                S.op("dve", (lambda e, pp=pp: e.tensor_reduce(out=esel[0:TP, pp], in_=oh, axis=AX.X, op=ALU.add)), reads=[qT], writes=[esel])
            TT(gate[0:TP], tops[0:TP], tops[0:TP, :, 0:1].to_broadcast([TP, 8, 16]), ALU.subtract, [tops], [gate])
            ACT(gate[0:TP], gate[0:TP], AF.Exp, [gate], [gate])
            S.op("dve", lambda e: e.tensor_reduce(out=small[0:TP, 0:8], in_=gate[0:TP], axis=AX.X, op=ALU.add), reads=[gate], writes=[small])
            S.op("dve", lambda e: e.reciprocal(out=small[0:TP, 8:16], in_=small[0:TP, 0:8]), reads=[small], writes=[small])
            TT(gate[0:TP], gate[0:TP], small[0:TP, 8:16].unsqueeze(2).to_broadcast([TP, 8, 16]), ALU.mult, [gate, small], [gate])
            p = psr.next()
            srcs = (esel[0:TP, 0].rearrange("n h k -> n (h k)"), esel[0:TP, 1].rearrange("n h k -> n (h k)"), gate[0:TP].rearrange("n h k -> n (h k)"))
            for k in range(3 if 'noslot' not in DBGF else 0):
                TR(p[:, k * 128:k * 128 + TP], srcs[k], TP, [esel, gate], [p])
            for k in range(3 if 'noslot' not in DBGF else 0):
                CP(slotT[:, k, po:po + TP], p[:, k * 128:k * 128 + TP], [p], [slotT])

        def dense(l, x1l, xouts, nts, pos):
            NK = len(nts)
            NTOT = sum(nts)
            Cv = C_all[:, 0:NTOT * 64].rearrange("p (n j) -> p n j", j=64)
            pout = [[psB[2 * k], psB[2 * k + 1]] for k in range(NK)]
            for half in range(2):
                for c0 in range(0, NTOT, 8):
                    oi = OIr.next()
                    oj = OJr.next()
                    TT(oi[:, :, :], iota128[:, :].unsqueeze(1).to_broadcast([128, 8, 128]), slotT[:, 0, c0:c0 + 8].unsqueeze(2).to_broadcast([128, 8, 128]), ALU.is_equal, [iota128, slotT], [oi])
                    TT(oj[:, :, :], iota128[:, half * 64:(half + 1) * 64].unsqueeze(1).to_broadcast([128, 8, 64]), slotT[:, 1, c0:c0 + 8].unsqueeze(2).to_broadcast([128, 8, 64]), ALU.is_equal, [iota128, slotT], [oj])
                    TT(oj[:, :, :], oj[:, :, :], slotT[:, 2, c0:c0 + 8].unsqueeze(2).to_broadcast([128, 8, 64]), ALU.mult, [oj, slotT], [oj])
                    pc = psr.next()
                    for t in range(8):
                        MM(pc[:, t * 64:(t + 1) * 64], oi[:, t, :], oj[:, t, :], True, True, [oi, oj], [pc])
                    S.op("act", (lambda e, pc=pc, c0=c0: e.copy(out=C_all[:, c0 * 64:(c0 + 8) * 64], in_=pc[:, :])), reads=[pc], writes=[C_all])

                def ph_stage(j):
                    ut = UTr.next()
                    DMA(ut[:], UT[l, j], reads=[UTbuf[l][j]], writes=[ut])
                    vb = VBr.next()
                    DMA(vb[:], VB[l, j], reads=[VBbuf[l][j]], writes=[vb])
                    ph = psr.next()
                    for c in range(8):
                        MM(ph[:, 0:NTOT], ut[:, c * 128:(c + 1) * 128], x1FMb[:, c, 0:NTOT], c == 0, c == 7, [ut, x1FMb], [ph])
                    return ph, vb
                cur = ph_stage(half * 64)
                for jl in range(64):
                    j = half * 64 + jl
                    nxt = ph_stage(j + 1) if jl + 1 < 64 else None
                    ph, vb = cur
                    G = Gr.next()
                    ACT(G[:, 0:NTOT], ph[:, 0:NTOT], AF.Gelu, [ph], [G])
                    Wt = Wr.next()
                    TT(Wt[:, 0:NTOT], G[:, 0:NTOT], Cv[:, 0:NTOT, jl], ALU.mult, [G, C_all], [Wt])
                    for k in range(NK):
                        for nh in range(2):
                            MM(pout[k][nh][0:nts[k], :], Wt[:, pos[k]:pos[k] + nts[k]], vb[:, nh * 512:(nh + 1) * 512], j == 0, j == 127, [Wt, vb], [pout[k][nh]])
                    cur = nxt
            for k in range(NK):
                TP = nts[k]
                for nh in range(2):
                    STT(rbuf[0:TP, nh * 512:(nh + 1) * 512], x1l[k][0:TP, nh * 512:(nh + 1) * 512], ALPHA, pout[k][nh][0:TP, :], ALU.mult, ALU.add, [x1l[k], pout[k][nh]], [rbuf])
                ln_from_rbuf(TP, 2, 3, xouts[k])

        def run_group(srcs, dsts, nts, L):
            NK = len(nts)
            pos = [0, nts[0]][:NK]
            for k in range(NK):
                DMA(xA[k][0:nts[k], :], srcs[k], writes=[xA[k]])
            cur_, nxt_ = xA, xB
            for l in range(NL):
                for k in range(NK):
                    mix_topk(l, cur_[k], x1s[k], nts[k], L, pos[k])
                dense(l, x1s[:NK], nxt_[:NK], nts, pos)
                cur_, nxt_ = nxt_, cur_
            for k in range(NK):
                DMA(dsts[k], cur_[k][0:nts[k], :], reads=[cur_[k]])

        init_states_zero()
        nblk = T // TB
        b = 0
        while b < nblk:
            nk = 2 if b + 1 < nblk else 1
            run_group([xp[(b + k) * TB:(b + k + 1) * TB, :] for k in range(nk)], [yp[(b + k) * TB:(b + k + 1) * TB, :] for k in range(nk)], [TB] * nk, 64)
            b += nk
        out_states(0)
        if with_sample:
            init_states_sample()
            run_group([xsm[:, :]], [ys[:, :]], [32], 32)
            out_states(1)
        with nc.allow_low_precision("bf16 matmul operands"):
            S.run()
    return nc


W_NAMES = ["w_in", "b_in", "conv_a_w", "conv_b_w", "conv_b_b", "lru_wa", "lru_ba", "lru_wx", "lru_bx", "lru_lambda",
           "hgrn_lb_logits", "hgrn_norm_g", "w_out_a", "w_out_b", "w_out_c", "w_o", "ln1_g", "ln1_b", "peer_wq",
           "peer_keys", "peer_u", "peer_v", "ln2_g", "ln2_b"]


def run(inputs, T, NL=DEPTH, with_sample=True, dbg=None, ncores=8):
    import time
    t0 = time.time()
    nc = build(T, NL, with_sample, dbg=dbg)
    print('build time', time.time() - t0, flush=True)
    f = lambda a: np.ascontiguousarray(np.asarray(a, dtype=np.float32))
    wmaps = {k: f(inputs[k]) for k in W_NAMES}
    in_maps = []
    for c in range(ncores):
        m = dict(wmaps)
        m["xp"] = f(inputs["x_prompt"][c % 4, :T])
        m["xs"] = f(inputs["x_sample"][c])
        m["st_ca"] = f(inputs["state_conv_a"][:, c])
        m["st_cb"] = f(inputs["state_conv_b"][:, c])
        m["st_h"] = f(inputs["state_lru"][:, c])
        m["st_S"] = f(inputs["state_hgrn"][:, c])
        in_maps.append(m)
    res = run_bass_kernel_spmd(nc, in_maps, core_ids=list(range(ncores)))
    print('total time', time.time() - t0, flush=True)
    return res.results


def kernel(**inputs):
    T = inputs["x_prompt"].shape[1]
    r = run(inputs, T)
    yp = np.stack([r[c]["yp"] for c in range(4)], 0)
    ys = np.stack([r[c]["ys"] for c in range(8)], 0)
    def pst(name):
        return np.stack([r[c][name] for c in range(4)], 1)
    def sst(name):
        return np.stack([r[c][name] for c in range(8)], 1)
    return (yp, ys, pst("p_ca"), pst("p_cb"), pst("p_h"), pst("p_S"),
            sst("s_ca"), sst("s_cb"), sst("s_h"), sst("s_S"))
```

```python
import numpy as np
from contextlib import ExitStack
import concourse.bass as bass
import concourse.mybir as mybir
from concourse.bass_utils import run_bass_kernel_spmd

F32 = mybir.dt.float32
BF16 = mybir.dt.bfloat16
U32 = mybir.dt.uint32
I32 = mybir.dt.int32
AF = mybir.ActivationFunctionType
ALU = mybir.AluOpType
AX = mybir.AxisListType

D = 1024
NCB = 8
DEPTH = 2
ALPHA = (2.0 * DEPTH) ** 0.25
LN_EPS = 1e-5
RMS_EPS = 1e-6
NEG = -1.0e30


class Tl:
    def __init__(self, t, name=""):
        self.t = t
        self.name = name
        self.last_w = None
        self.readers = []

    def __getitem__(self, idx):
        return self.t[idx]


class Alias:
    def __init__(self, base, ap):
        self.base = base
        self.t = ap

    def __getitem__(self, idx):
        return self.t[idx]

    last_w = property(lambda s: s.base.last_w, lambda s, v: setattr(s.base, "last_w", v))
    readers = property(lambda s: s.base.readers, lambda s, v: setattr(s.base, "readers", v))


class Op:
    __slots__ = ("eng", "fn", "deps", "signal", "sem", "semval", "dma", "idx", "eidx")


class Sched:
    ENG = ("pe", "act", "dve", "pool", "sp")

    def __init__(self, nc, stack, n_dma_sems=16):
        self.nc = nc
        self.ops = []
        self.per_eng = {e: [] for e in self.ENG}
        self.esem = {e: stack.enter_context(nc.semaphore("es_" + e)) for e in self.ENG}
        self.dsems = {}
        self.dcnt = {}
        self.drr = {}
        for e in ("sp", "pool"):
            self.dsems[e] = [stack.enter_context(nc.semaphore("ds_%s%d" % (e, i))) for i in range(n_dma_sems)]
            self.dcnt[e] = [0] * n_dma_sems
            self.drr[e] = 0
        self.dlast = {}
        self.stack = stack

    def sb(self, name, shape, dt=F32):
        n = 1
        for d_ in shape[1:]:
            n *= d_
        self.sbytes = getattr(self, "sbytes", 0) + n * (2 if dt == BF16 else 4)
        t = self.stack.enter_context(self.nc.sbuf_tensor(name, list(shape), dt))
        return Tl(t, name)

    def ps(self, name, shape, dt=F32):
        t = self.stack.enter_context(self.nc.psum_tensor(name, list(shape), dt))
        return Tl(t, name)

    def op(self, eng, fn, reads=(), writes=(), dma=False):
        o = Op()
        o.eng = eng
        o.fn = fn
        o.dma = dma
        o.signal = False
        o.sem = None
        o.semval = None
        o.idx = len(self.ops)
        o.eidx = len(self.per_eng[eng])
        deps = []
        for r in reads:
            if r is not None and r.last_w is not None:
                deps.append(r.last_w)
        for w in writes:
            if w is None:
                continue
            if w.last_w is not None:
                deps.append(w.last_w)
            deps.extend(w.readers)
        if dma:
            k = self.drr[eng]
            self.drr[eng] = (k + 1) % len(self.dsems[eng])
            prev = self.dlast.get((eng, k))
            if prev is not None:
                deps.append(prev)
            self.dcnt[eng][k] += 16
            o.sem = self.dsems[eng][k]
            o.semval = self.dcnt[eng][k]
            self.dlast[(eng, k)] = o
            o.signal = True
        seen = set()
        dd = []
        for d in deps:
            if d is o or id(d) in seen:
                continue
            seen.add(id(d))
            if (not d.dma) and d.eng == "pe" and eng == "pe" and not dma:
                continue
            dd.append(d)
            if not d.dma:
                d.signal = True
        o.deps = dd
        for r in reads:
            if r is not None:
                r.readers.append(o)
        for w in writes:
            if w is not None:
                w.last_w = o
                w.readers = []
        self.ops.append(o)
        self.per_eng[eng].append(o)
        return o

    def finalize(self):
        for e in self.ENG:
            c = 0
            for o in self.per_eng[e]:
                if o.dma:
                    continue
                if o.signal:
                    c += 1
                    o.sem = self.esem[e]
                    o.semval = c

    def emit_engine(self, ename, eng):
        seen = {}
        for o in self.per_eng[ename]:
            need = {}
            for d in o.deps:
                if (not d.dma) and (not o.dma) and d.eng == ename and ename in ("act", "dve") and o.eidx - d.eidx >= 4:
                    continue
                key = id(d.sem)
                if key not in need or need[key][1] < d.semval:
                    need[key] = (d.sem, d.semval)
            for key, (sem, val) in need.items():
                if seen.get(key, 0) >= val:
                    continue
                eng.wait_ge(sem, val)
                seen[key] = val
            inst = o.fn(eng)
            if o.signal:
                inst.then_inc(o.sem, 16 if o.dma else 1)
        if ename in self.dsems:
            for k, s in enumerate(self.dsems[ename]):
                v = self.dcnt[ename][k]
                if v > 0 and seen.get(id(s), 0) < v:
                    eng.wait_ge(s, v)

    def run(self):
        self.finalize()
        S = self
        with self.nc.Block() as block:
            @block.tensor
            def _(e):
                S.emit_engine("pe", e)

            @block.scalar
            def _(e):
                S.emit_engine("act", e)

            @block.vector
            def _(e):
                S.emit_engine("dve", e)

            @block.gpsimd
            def _(e):
                S.emit_engine("pool", e)

            @block.sync
            def _(e):
                S.emit_engine("sp", e)


class Ring:
    def __init__(self, tiles):
        self.tiles = tiles
        self.i = 0

    def next(self):
        t = self.tiles[self.i]
        self.i = (self.i + 1) % len(self.tiles)
        return t


def build(T, NL=DEPTH, with_sample=True, NWR=4, NGR=3, dbg=None):
    import os
    DBGF = os.environ.get('KDBG', '')
    TB = 128
    nc = bass.Bass("TRN2", target_bir_lowering=False)

    def din(name, shape, dt=F32):
        return nc.dram_tensor(name, list(shape), dt, kind="ExternalInput").ap()

    def dout(name, shape, dt=F32):
        return nc.dram_tensor(name, list(shape), dt, kind="ExternalOutput").ap()

    xp = din("xp", [T, D])
    xsm = din("xs", [32, D])
    st_ca = din("st_ca", [DEPTH, 2, D])
    st_cb = din("st_cb", [DEPTH, 3, D])
    st_h = din("st_h", [DEPTH, D])
    st_S = din("st_S", [DEPTH, 8, 128, 128])
    w_in = din("w_in", [DEPTH, D, 12 * D])
    b_in = din("b_in", [DEPTH, 12 * D])
    conv_a_w = din("conv_a_w", [DEPTH, 3, D])
    conv_b_w = din("conv_b_w", [DEPTH, 4, D])
    conv_b_b = din("conv_b_b", [DEPTH, D])
    lru_wa = din("lru_wa", [DEPTH, 8, 128, 128])
    lru_ba = din("lru_ba", [DEPTH, D])
    lru_wx = din("lru_wx", [DEPTH, 8, 128, 128])
    lru_bx = din("lru_bx", [DEPTH, D])
    lru_lambda = din("lru_lambda", [DEPTH, D])
    lb_logits = din("hgrn_lb_logits", [DEPTH, D])
    norm_g = din("hgrn_norm_g", [DEPTH, 128])
    w_out = [din("w_out_a", [DEPTH, D, D]), din("w_out_b", [DEPTH, D, D]), din("w_out_c", [DEPTH, D, D])]
    w_o = din("w_o", [DEPTH, D, D])
    ln1_g = din("ln1_g", [DEPTH, D])
    ln1_b = din("ln1_b", [DEPTH, D])
    peer_wq = din("peer_wq", [DEPTH, D, 2048])
    peer_keys = din("peer_keys", [DEPTH, 8, 2, 128, 128])
    peer_u = din("peer_u", [DEPTH, 16384, D])
    peer_v = din("peer_v", [DEPTH, 16384, D])
    peer_u2 = peer_u.rearrange("l e d -> (l e) d")
    peer_v2 = peer_v.rearrange("l e d -> (l e) d")
    UT = nc.dram_tensor("UT_scr", [DEPTH, 128, 128, 1024], BF16, kind="Internal").ap()
    VB = nc.dram_tensor("VB_scr", [DEPTH, 128, 128, 1024], BF16, kind="Internal").ap()
    WS = nc.dram_tensor("WS_scr", [DEPTH, 144, 128, 1024], BF16, kind="Internal").ap()
    ln2_g = din("ln2_g", [DEPTH, D])
    ln2_b = din("ln2_b", [DEPTH, D])

    yp = dout("yp", [T, D])
    ys = dout("ys", [32, D])
    o_ca = [dout("p_ca", [DEPTH, 2, D]), dout("s_ca", [DEPTH, 2, D])]
    o_cb = [dout("p_cb", [DEPTH, 3, D]), dout("s_cb", [DEPTH, 3, D])]
    o_h = [dout("p_h", [DEPTH, D]), dout("s_h", [DEPTH, D])]
    o_S = [dout("p_S", [DEPTH, 8, 128, 128]), dout("s_S", [DEPTH, 8, 128, 128])]
    dbg_out = {}
    if dbg:
        for nm, shp in dbg.items():
            dbg_out[nm] = dout("dbg_" + nm, shp)

    with ExitStack() as st:
        S = Sched(nc, st)

        def DMA(out_ap, in_ap, reads=(), writes=(), slow=False):
            if slow:
                S.op("sp", lambda e: e.dma_start(out=out_ap, in_=in_ap, allow_slow_non_contiguous=True), reads=reads, writes=writes, dma=True)
            else:
                S.op("sp", lambda e: e.dma_start(out=out_ap, in_=in_ap), reads=reads, writes=writes, dma=True)

        def ACT(out_ap, in_ap, func, reads, writes, bias=None, scale=None):
            kw = {}
            if bias is not None:
                kw["bias"] = bias
            if scale is not None:
                kw["scale"] = scale
            S.op("act", lambda e: e.activation(out=out_ap, in_=in_ap, func=func, **kw), reads=reads, writes=writes)

        def TS(out_ap, in_ap, s1, s2, op0, op1, reads, writes, eng="dve"):
            if op1 is None:
                S.op(eng, lambda e: e.tensor_scalar(out=out_ap, in0=in_ap, scalar1=s1, scalar2=None, op0=op0), reads=reads, writes=writes)
            else:
                S.op(eng, lambda e: e.tensor_scalar(out=out_ap, in0=in_ap, scalar1=s1, scalar2=s2, op0=op0, op1=op1), reads=reads, writes=writes)

        def TT(out_ap, a_ap, b_ap, op, reads, writes, eng="dve"):
            S.op(eng, lambda e: e.tensor_tensor(out=out_ap, in0=a_ap, in1=b_ap, op=op), reads=reads, writes=writes)

        def STT(out_ap, in0, scalar, in1, op0, op1, reads, writes, accum=None, eng="dve"):
            if accum is None:
                S.op(eng, lambda e: e.scalar_tensor_tensor(out=out_ap, in0=in0, scalar=scalar, in1=in1, op0=op0, op1=op1), reads=reads, writes=writes)
            else:
                S.op(eng, lambda e: e.scalar_tensor_tensor(out=out_ap, in0=in0, scalar=scalar, in1=in1, op0=op0, op1=op1, accum_out=accum), reads=reads, writes=writes)

        def CP(out_ap, in_ap, reads, writes, eng="dve"):
            S.op(eng, lambda e: e.tensor_copy(out=out_ap, in_=in_ap), reads=reads, writes=writes)

        def MSET(ap, val, writes, eng="pool"):
            S.op(eng, lambda e: e.memset(ap, val), writes=writes)

        def MM(out_ap, lhsT, rhs, start, stop, reads, writes):
            S.op("pe", lambda e: e.matmul(out_ap, lhsT=lhsT, rhs=rhs, start=start, stop=stop), reads=reads, writes=writes)

        def TR(out_ap, in_ap, k, reads, writes):
            S.op("pe", lambda e: e.transpose(out_ap, in_ap, ident[0:k, 0:k]), reads=list(reads) + [ident], writes=writes)

        ident = S.sb("ident", [128, 128])
        ones = S.sb("ones", [128, 128])
        tri = S.sb("tri", [64, 64])
        cmask = {64: S.sb("cmask64", [128, TB]), 32: S.sb("cmask32", [128, TB])}
        iota16 = S.sb("iota16", [128, 16])
        iota_i = S.sb("iota_i", [128, 16], I32)
        psr = Ring([S.ps("ps%d" % i, [128, 512]) for i in range(4)])
        psB = [S.ps("psB%d" % i, [128, 512]) for i in range(4)]
        wring = Ring([S.sb("wr%d" % i, [128, 1024]) for i in range(NWR)])
        zps = [Ring([S.sb("z%d_%d" % (k, i), [128, TB + 4]) for i in range(23)]) for k in range(2)]
        zr = zps[0]
        xA = [S.sb("xa%d" % i, [128, D]) for i in range(2)]
        xB = [S.sb("xb%d" % i, [128, D]) for i in range(2)]
        x1s = [S.sb("x1_%d" % i, [128, D]) for i in range(2)]
        rbuf = S.sb("rbuf", [128, D])
        aABC = S.sb("aABC", [128, 8, 3, TB], BF16)
        aV = [[Tl(aABC.t) for _ in range(3)] for _ in range(8)]
        merged = S.sb("merged", [128, 8, TB], BF16)
        s2 = S.sb("s2", [128, 128])
        C_all = S.sb("C_all", [128, 128 * 128], BF16)
        if 'noalias' in DBGF:
            pscr = S.sb("pscr", [128, 2048]); cand = S.sb("cand", [128, 2048]); qT = S.sb("qT", [128, 16, TB])
        else:
            pscr = Alias(C_all, C_all[:, 0:4096].bitcast(F32))
            cand = Alias(C_all, C_all[:, 4096:8192].bitcast(F32))
            qT = Alias(C_all, C_all[:, 8192:12288].bitcast(F32).rearrange("p (a b) -> p a b", b=128))
        x1FMb = S.sb("x1FMb", [128, 8, 2 * TB], BF16)
        iota128 = S.sb("iota128", [128, 128])
        iota128i = S.sb("iota128i", [128, 128], I32)
        slotT = S.sb("slotT", [128, 3, 256])
        OIr = Ring([S.sb("oi%d" % i, [128, 8, 128], BF16) for i in range(2)])
        OJr = Ring([S.sb("oj%d" % i, [128, 8, 64], BF16) for i in range(2)])
        UTr = Ring([S.sb("utr%d" % i, [128, 1024], BF16) for i in range(3)])
        VBr = Ring([S.sb("vbr%d" % i, [128, 1024], BF16) for i in range(4)])
        Gr = Ring([S.sb("gg%d" % i, [128, 256]) for i in range(3)])
        Wr = Ring([S.sb("wt%d" % i, [128, 256], BF16) for i in range(3)])
        wbr = Ring([S.sb("wbr%d" % i, [128, 1024], BF16) for i in range(5)])
        WSbuf = [[Tl(None) for _ in range(144)] for _ in range(NL)]
        xFMb = S.sb("xFMb", [128, 8, TB], BF16)
        UTbuf = [[Tl(None) for _ in range(128)] for _ in range(NL)]
        VBbuf = [[Tl(None) for _ in range(128)] for _ in range(NL)]
        lnp = [S.sb("lnp%d" % i, [128, D]) for i in range(4)]
        keysT = [S.sb("keysT%d" % l, [128, 16, 128]) for l in range(NL)]
        PA = [S.sb("PA%d" % l, [128, 96]) for l in range(NL)]
        PB = [S.sb("PB%d" % l, [128, 128]) for l in range(NL)]
        PD = [S.sb("PD%d" % l, [128, 4, 8]) for l in range(NL)]
        stg = S.sb("stg", [128, 128])
        Sst = [S.sb("Sst%d" % l, [128, 8, 128]) for l in range(NL)]
        SstV = [[Tl(Sst[l].t) for _ in range(8)] for l in range(NL)]
        cAst = [S.sb("cAst%d" % l, [128, 8, 2]) for l in range(NL)]
        cAV = [[Tl(cAst[l].t) for _ in range(8)] for l in range(NL)]
        cBst = [S.sb("cBst%d" % l, [128, 8, 3]) for l in range(NL)]
        cBV = [[Tl(cBst[l].t) for _ in range(8)] for l in range(NL)]
        hst = [S.sb("hst%d" % l, [128, 8]) for l in range(NL)]
        hV = [[Tl(hst[l].t) for _ in range(8)] for l in range(NL)]
        small = S.sb("small", [128, 64])
        sv = S.sb("sv", [128, 16, 16])
        si = S.sb("si", [128, 16, 16], U32)
        sif = S.sb("sif", [128, 16, 16])
        tops = S.sb("tops", [128, 8, 16])
        topp = S.sb("topp", [128, 8, 16], U32)
        pij = S.sb("pij", [128, 2, 8, 16], U32)
        pijf = S.sb("pijf", [128, 2, 8, 16])
        esel = S.sb("esel", [128, 2, 8, 16])
        gate = S.sb("gate", [128, 8, 16])
        mvt = S.sb("mvt", [128, 16])
        bst = S.sb("bst", [128, 12])

        print('SBUF bytes/partition', S.sbytes, flush=True)
        MSET(ident[:], 0.0, [ident])
        S.op("pool", lambda e: e.affine_select(out=ident[:], in_=ident[:], pattern=[[-1, 128]], compare_op=ALU.not_equal, fill=1.0, base=0, channel_multiplier=1), reads=[ident], writes=[ident])
        MSET(ones[:], 1.0, [ones])
        MSET(tri[:], 1.0, [tri])
        S.op("pool", lambda e: e.affine_select(out=tri[:], in_=tri[:], pattern=[[1, 64]], compare_op=ALU.is_ge, fill=0.0, base=0, channel_multiplier=-1), reads=[tri], writes=[tri])
        for L in (64, 32):
            MSET(cmask[L][:], 1.0, [cmask[L]])
            for c in range(TB // L):
                MSET(cmask[L][:, c * L:c * L + 1], 0.0, [cmask[L]])
        S.op("pool", lambda e: e.iota(iota_i[:], pattern=[[1, 16]], base=0, channel_multiplier=0), writes=[iota_i])
        CP(iota16[:], iota_i[:], [iota_i], [iota16])
        S.op("pool", lambda e: e.iota(iota128i[:], pattern=[[1, 128]], base=0, channel_multiplier=0), writes=[iota128i])
        CP(iota128[:], iota128i[:], [iota128i], [iota128])

        for l in range(NL):
            MSET(stg[:], 0.0, [stg])
            DMA(stg[0:96, :], b_in[l].rearrange("(r p) -> r p", p=128), writes=[stg])
            p = psr.next()
            TR(p[:, 0:128], stg[:], 128, [stg], [p])
            CP(PA[l][:], p[:, 0:96], [p], [PA[l]])
            MSET(stg[:], 0.0, [stg])
            DMA(stg[0:24, :], conv_a_w[l].rearrange("k (c p) -> (k c) p", p=128), writes=[stg])
            DMA(stg[24:56, :], conv_b_w[l].rearrange("k (c p) -> (k c) p", p=128), writes=[stg])
            DMA(stg[56:64, :], conv_b_b[l].rearrange("(c p) -> c p", p=128), writes=[stg])
            DMA(stg[64:72, :], lru_ba[l].rearrange("(c p) -> c p", p=128), writes=[stg])
            DMA(stg[72:80, :], lru_bx[l].rearrange("(c p) -> c p", p=128), writes=[stg])
            DMA(stg[80:88, :], lru_lambda[l].rearrange("(c p) -> c p", p=128), writes=[stg])
            DMA(stg[88:96, :], lb_logits[0].rearrange("(c p) -> c p", p=128), writes=[stg])
            DMA(stg[96:104, :], lb_logits[1].rearrange("(c p) -> c p", p=128), writes=[stg])
            DMA(stg[104:105, :], norm_g[l:l + 1, :], writes=[stg])
            p = psr.next()
            TR(p[:, 0:128], stg[:], 128, [stg], [p])
            CP(PB[l][:], p[:, 0:128], [p], [PB[l]])
            ACT(PD[l][:, 0, :], PB[l][:, 80:88], AF.Exp, [PB[l]], [PD[l]], scale=-1.0)
            ACT(PD[l][:, 0, :], PD[l][:, 0, :], AF.Ln, [PD[l]], [PD[l]], bias=1.0)
            TS(PD[l][:, 1, :], PD[l][:, 0, :], -16.0, None, ALU.mult, None, [PD[l]], [PD[l]])
            TS(PD[l][:, 0, :], PD[l][:, 0, :], -8.0, None, ALU.mult, None, [PD[l]], [PD[l]])
            if l == 0:
                MSET(PD[l][:, 2, :], 0.0, [PD[l]], eng="dve")
                MSET(PD[l][:, 3, :], 1.0, [PD[l]], eng="dve")
            else:
                TT(PD[l][:, 2, :], PB[l][:, 96:104], PB[l][:, 88:96], ALU.subtract, [PB[l]], [PD[l]])
                ACT(PD[l][:, 2, :], PD[l][:, 2, :], AF.Sigmoid, [PD[l]], [PD[l]])
                TS(PD[l][:, 3, :], PD[l][:, 2, :], -1.0, 1.0, ALU.mult, ALU.add, [PD[l]], [PD[l]])
            for hp in range(16):
                w = wring.next()
                DMA(w[:, 0:128], peer_keys[l, hp // 2, hp % 2], writes=[w])
                p = psr.next()
                TR(p[:, 0:128], w[:, 0:128], 128, [w], [p])
                CP(keysT[l][:, hp, :], p[:, 0:128], [p], [keysT[l]], eng="dve")

        import os
        DBGF = os.environ.get('KDBG', '')
        def pipelined(items, load_fn, finish_fn, depth):
            staged = []
            for it in items:
                staged.append((it, load_fn(it)))
                if len(staged) > depth:
                    finish_fn(*staged.pop(0))
            while staged:
                finish_fn(*staged.pop(0))

        def tab_load(it):
            l, j, kind = it
            src = (peer_u if kind == 0 else peer_v)[l].rearrange("(i j) d -> j i d", j=128)[j]
            g = wring.next()
            DMA(g[:], src, writes=[g])
            return g

        def tab_finish(it, g):
            l, j, kind = it
            if kind == 0:
                ut = UTr.next()
                for c4 in range(2):
                    p = psr.next()
                    for k in range(4):
                        c = c4 * 4 + k
                        TR(p[:, k * 128:(k + 1) * 128], g[:, c * 128:(c + 1) * 128], 128, [g], [p])
                    S.op("act", (lambda e, ut=ut, p=p, c4=c4: e.copy(out=ut[:, c4 * 512:(c4 + 1) * 512], in_=p[:, :])), reads=[p], writes=[ut])
                DMA(UT[l, j], ut[:], reads=[ut], writes=[UTbuf[l][j]])
            else:
                vb = VBr.next()
                CP(vb[:], g[:], [g], [vb], eng="dve")
                DMA(VB[l, j], vb[:], reads=[vb], writes=[VBbuf[l][j]])

        pipelined([(l, j, kind) for l in range(NL) for j in range(128) for kind in range(2)], tab_load, tab_finish, NWR - 1)

        def init_states_zero():
            for l in range(NL):
                MSET(Sst[l][:], 0.0, SstV[l])
                MSET(cAst[l][:], 0.0, cAV[l])
                MSET(cBst[l][:], 0.0, cBV[l])
                MSET(hst[l][:], 0.0, hV[l])

        def init_states_sample():
            for l in range(NL):
                DMA(Sst[l][:], st_S[l].rearrange("h d e -> d h e"), writes=SstV[l])
                for cb in range(8):
                    DMA(cAst[l][:, cb, :], st_ca[l][:, cb * 128:(cb + 1) * 128].rearrange("k p -> p k"), writes=[cAV[l][cb]], slow=True)
                    DMA(cBst[l][:, cb, :], st_cb[l][:, cb * 128:(cb + 1) * 128].rearrange("k p -> p k"), writes=[cBV[l][cb]], slow=True)
                DMA(hst[l][:], st_h[l].rearrange("(c p) -> p c", p=128), writes=hV[l], slow=True)

        def out_states(which):
            for l in range(NL):
                DMA(o_S[which][l].rearrange("h d e -> d h e"), Sst[l][:], reads=SstV[l])
                for cb in range(8):
                    DMA(o_ca[which][l][:, cb * 128:(cb + 1) * 128].rearrange("k p -> p k"), cAst[l][:, cb, :], reads=[cAV[l][cb]], slow=True)
                    DMA(o_cb[which][l][:, cb * 128:(cb + 1) * 128].rearrange("k p -> p k"), cBst[l][:, cb, :], reads=[cBV[l][cb]], slow=True)
                DMA(o_h[which][l].rearrange("(c p) -> p c", p=128), hst[l][:], reads=hV[l], slow=True)

        def wsrc(l, idx):
            if idx < 96:
                return w_in[l][:, idx * 128:(idx + 1) * 128].rearrange("(c p) n -> p c n", p=128), True
            if idx < 120:
                X, j = (idx - 96) // 8, (idx - 96) % 8
                return w_out[X][l][:, j * 128:(j + 1) * 128].rearrange("(c p) n -> p c n", p=128), True
            if idx < 128:
                j = idx - 120
                return w_o[l, j * 128:(j + 1) * 128, :], False
            hp = idx - 128
            return peer_wq[l][:, hp * 128:(hp + 1) * 128].rearrange("(c p) n -> p c n", p=128), True

        def w_load(it):
            l, idx = it
            src, tiled = wsrc(l, idx)
            w = wring.next()
            if tiled:
                DMA(w[:].rearrange("p (c n) -> p c n", n=128), src, writes=[w])
            else:
                DMA(w[:], src, writes=[w])
            return w

        def w_finish(it, w):
            l, idx = it
            wb = wbr.next()
            CP(wb[:], w[:], [w], [wb], eng="dve")
            DMA(WS[l, idx], wb[:], reads=[wb], writes=[WSbuf[l][idx]])

        pipelined([(l, idx) for l in range(NL) for idx in range(144)], w_load, w_finish, NWR - 1)

        def wl(l, idx):
            wb = wbr.next()
            DMA(wb[:], WS[l, idx], reads=[WSbuf[l][idx]], writes=[wb])
            return wb

        def in_proj(l, idx, nt):
            w = wl(l, idx)
            p = psr.next()
            for c in range(8):
                MM(p[:, 0:nt], w[:, c * 128:(c + 1) * 128], xFMb[:, c, 0:nt], c == 0, c == 7, [w, xFMb], [p])
            return p

        def ln_from_rbuf(TP, gi_, bi_, dst):
            for nh in range(2):
                S.op("dve", (lambda e, nh=nh: e.bn_stats(out=bst[0:TP, nh * 6:(nh + 1) * 6], in_=rbuf[0:TP, nh * 512:(nh + 1) * 512])), reads=[rbuf], writes=[bst])
            S.op("dve", lambda e: e.bn_aggr(out=mvt[0:TP, 0:2], in_=bst[0:TP, 0:12]), reads=[bst], writes=[mvt])
            TS(mvt[0:TP, 2:3], mvt[0:TP, 1:2], 1.0, LN_EPS, ALU.mult, ALU.add, [mvt], [mvt])
            ACT(mvt[0:TP, 2:3], mvt[0:TP, 2:3], AF.Sqrt, [mvt], [mvt])
            S.op("dve", lambda e: e.reciprocal(out=mvt[0:TP, 3:4], in_=mvt[0:TP, 2:3]), reads=[mvt], writes=[mvt])
            TS(rbuf[0:TP, :], rbuf[0:TP, :], mvt[0:TP, 0:1], mvt[0:TP, 3:4], ALU.subtract, ALU.mult, [rbuf, mvt], [rbuf])
            TT(rbuf[0:TP, :], rbuf[0:TP, :], lnp[gi_][0:TP, :], ALU.mult, [rbuf, lnp[gi_]], [rbuf])
            TT(dst[0:TP, :], rbuf[0:TP, :], lnp[bi_][0:TP, :], ALU.add, [rbuf, lnp[bi_]], [dst])

        def mix_topk(l, xin, x1, nt, L, po, extra=None, tagdbg=None):
            TP = nt
            nch = nt // L
            for i, src in enumerate((ln1_g, ln1_b, ln2_g, ln2_b)):
                DMA(lnp[i][:], src[l:l + 1, :].to_broadcast([128, D]), writes=[lnp[i]])
            for c4 in range(2):
                p = psr.next()
                for k in range(4):
                    c = c4 * 4 + k
                    TR(p[:, k * 128:k * 128 + TP], xin[0:TP, c * 128:(c + 1) * 128], TP, [xin], [p])
                for k in range(4):
                    c = c4 * 4 + k
                    S.op("act", (lambda e, c=c, k=k, p=p: e.copy(out=xFMb[:, c, 0:TP], in_=p[:, k * 128:k * 128 + TP])), reads=[p], writes=[xFMb])
            def cb_body(cb, zp):
                def zin(G, func):
                    p = in_proj(l, G * 8 + cb, nt)
                    z = zp.next()
                    ACT(z[:, 0:nt], p[:, 0:nt], func, [p, PA[l]], [z], bias=PA[l][:, G * 8 + cb:G * 8 + cb + 1])
                    return z
                zB = zin(0, AF.Identity)
                yield
                zC = zin(1, AF.Identity)
                yield
                zxA = zin(2, AF.Identity)
                yield
                ub = zp.next()
                yield
                CP(ub[:, 0:2], cAst[l][:, cb, :], [cAV[l][cb]], [ub], eng="pool")
                yield
                TT(ub[:, 2:2 + nt], zC[:, 0:nt], zxA[:, 0:nt], ALU.mult, [zC, zxA], [ub])
                yield
                CP(cAst[l][:, cb, :], ub[:, nt:nt + 2], [ub], [cAV[l][cb]], eng="pool")
                yield
                y = zp.next()
                yield
                TS(y[:, 0:nt], ub[:, 0:nt], PB[l][:, cb:cb + 1], None, ALU.mult, None, [ub, PB[l]], [y])
                yield
                STT(y[:, 0:nt], ub[:, 1:1 + nt], PB[l][:, 8 + cb:9 + cb], y[:, 0:nt], ALU.mult, ALU.add, [ub, y, PB[l]], [y])
                yield
                STT(y[:, 0:nt], ub[:, 2:2 + nt], PB[l][:, 16 + cb:17 + cb], y[:, 0:nt], ALU.mult, ALU.add, [ub, y, PB[l]], [y])
                yield
                TT(aABC[:, cb, 0, 0:nt], zB[:, 0:nt], y[:, 0:nt], ALU.mult, [zB, y], [aV[cb][0]])
                yield
                zxB = zin(3, AF.Identity)
                yield
                ggB = zin(4, AF.Gelu)
                yield
                cbuf = zp.next()
                yield
                CP(cbuf[:, 0:3], cBst[l][:, cb, :], [cBV[l][cb]], [cbuf], eng="pool")
                yield
                CP(cbuf[:, 3:3 + nt], zxB[:, 0:nt], [zxB], [cbuf], eng="pool")
                yield
                CP(cBst[l][:, cb, :], cbuf[:, nt:nt + 3], [cbuf], [cBV[l][cb]], eng="pool")
                yield
                xl = zp.next()
                yield
                TS(xl[:, 0:nt], cbuf[:, 0:nt], PB[l][:, 24 + cb:25 + cb], PB[l][:, 56 + cb:57 + cb], ALU.mult, ALU.add, [cbuf, PB[l]], [xl])
                yield
                for k in range(1, 4):
                    STT(xl[:, 0:nt], cbuf[:, k:k + nt], PB[l][:, 24 + 8 * k + cb:25 + 8 * k + cb], xl[:, 0:nt], ALU.mult, ALU.add, [cbuf, xl, PB[l]], [xl])
                    yield
                w = wring.next()
                yield
                DMA(w[:, 0:128], lru_wa[l, cb], writes=[w])
                yield
                DMA(w[:, 128:256], lru_wx[l, cb], writes=[w])
                yield
                p = psr.next()
                yield
                MM(p[:, 0:nt], w[:, 0:128], xl[:, 0:nt], True, True, [w, xl], [p])
                yield
                MM(p[:, 128:128 + nt], w[:, 128:256], xl[:, 0:nt], True, True, [w, xl], [p])
                yield
                r = zp.next()
                yield
                gi = zp.next()
                yield
                ACT(r[:, 0:nt], p[:, 0:nt], AF.Sigmoid, [p, PB[l]], [r], bias=PB[l][:, 64 + cb:65 + cb])
                yield
                ACT(gi[:, 0:nt], p[:, 128:128 + nt], AF.Sigmoid, [p, PB[l]], [gi], bias=PB[l][:, 72 + cb:73 + cb])
                yield
                a = zp.next()
                yield
                a2 = zp.next()
                yield
                ACT(a[:, 0:nt], r[:, 0:nt], AF.Exp, [r, PD[l]], [a], scale=PD[l][:, 0, cb:cb + 1])
                yield
                ACT(a2[:, 0:nt], r[:, 0:nt], AF.Exp, [r, PD[l]], [a2], scale=PD[l][:, 1, cb:cb + 1])
                yield
                TS(a2[:, 0:nt], a2[:, 0:nt], -1.0, 1.0, ALU.mult, ALU.add, [a2], [a2])
                yield
                ACT(a2[:, 0:nt], a2[:, 0:nt], AF.Sqrt, [a2], [a2])
                yield
                TT(gi[:, 0:nt], gi[:, 0:nt], a2[:, 0:nt], ALU.mult, [gi, a2], [gi])
                yield
                TT(gi[:, 0:nt], gi[:, 0:nt], xl[:, 0:nt], ALU.mult, [gi, xl], [gi])
                yield
                hh = zp.next()
                yield
                S.op("dve", (lambda e, hh=hh, a=a, gi=gi, cb=cb: e.tensor_tensor_scan(out=hh[:, 0:nt], data0=a[:, 0:nt], data1=gi[:, 0:nt], initial=hst[l][:, cb:cb + 1], op0=ALU.mult, op1=ALU.add)), reads=[a, gi, hV[l][cb]], writes=[hh])
                yield
                CP(hst[l][:, cb:cb + 1], hh[:, nt - 1:nt], [hh], [hV[l][cb]], eng="pool")
                yield
                TT(aABC[:, cb, 1, 0:nt], ggB[:, 0:nt], hh[:, 0:nt], ALU.mult, [ggB, hh], [aV[cb][1]])
                yield
                q = zin(5, AF.Silu)
                yield
                f = zin(6, AF.Sigmoid)
                yield
                vv = zin(7, AF.Identity)
                yield
                sg = zin(8, AF.Silu)
                yield
                TS(f[:, 0:nt], f[:, 0:nt], PD[l][:, 3, cb:cb + 1], PD[l][:, 2, cb:cb + 1], ALU.mult, ALU.add, [f, PD[l]], [f])
                yield
                kk = zp.next()
                yield
                TS(kk[:, 0:nt], f[:, 0:nt], -1.0, 1.0, ALU.mult, ALU.add, [f], [kk])
                yield
                lf = zp.next()
                yield
                ACT(lf[:, 0:nt], f[:, 0:nt], AF.Ln, [f], [lf])
                yield
                bb = zp.next()
                yield
                S.op("dve", (lambda e, bb=bb, lf=lf: e.tensor_tensor_scan(out=bb[:, 0:nt], data0=cmask[L][:, 0:nt], data1=lf[:, 0:nt], initial=0.0, op0=ALU.mult, op1=ALU.add)), reads=[lf, cmask[L]], writes=[bb])
                yield
                eb = zp.next()
                yield
                ACT(eb[:, 0:nt], bb[:, 0:nt], AF.Exp, [bb], [eb])
                yield
                ACT(bb[:, 0:nt], bb[:, 0:nt], AF.Exp, [bb], [bb], scale=-1.0)
                yield
                TT(q[:, 0:nt], q[:, 0:nt], eb[:, 0:nt], ALU.mult, [q, eb], [q])
                yield
                TT(kk[:, 0:nt], kk[:, 0:nt], bb[:, 0:nt], ALU.mult, [kk, bb], [kk])
                yield
                osb = zp.next()
                yield
                Sv = SstV[l][cb]
                yield
                for c in range(nch):
                    c0 = c * L
                    yield
                    pa = psr.next()
                    yield
                    MM(pa[0:L, 0:L], kk[:, c0:c0 + L], q[:, c0:c0 + L], True, True, [kk, q], [pa])
                    yield
                    att = zp.next()
                    yield
                    TT(att[0:L, 0:L], pa[0:L, 0:L], tri[0:L, 0:L], ALU.mult, [pa, tri], [att])
                    yield
                    pv = psr.next()
                    yield
                    TR(pv[0:L, 0:128], vv[:, c0:c0 + L], 128, [vv], [pv])
                    yield
                    vtm = zp.next()
                    yield
                    S.op("act", (lambda e, vtm=vtm, pv=pv: e.copy(out=vtm[0:L, 0:128], in_=pv[0:L, 0:128])), reads=[pv], writes=[vtm])
                    yield
                    po = psr.next()
                    yield
                    MM(po[:, 0:L], Sst[l][:, cb, :], q[:, c0:c0 + L], True, False, [Sv, q], [po])
                    yield
                    MM(po[:, 0:L], vtm[0:L, 0:128], att[0:L, 0:L], False, True, [vtm, att], [po])
                    yield
                    S.op("act", (lambda e, osb=osb, po=po, c0=c0: e.copy(out=osb[:, c0:c0 + L], in_=po[:, 0:L])), reads=[po], writes=[osb])
                    yield
                    k2 = zp.next()
                    yield
                    TS(k2[:, 0:L], kk[:, c0:c0 + L], eb[:, c0 + L - 1:c0 + L], None, ALU.mult, None, [kk, eb], [k2])
                    yield
                    pk = psr.next()
                    yield
                    TR(pk[0:L, 0:128], k2[:, 0:L], 128, [k2], [pk])
                    yield
                    k2t = zp.next()
                    yield
                    CP(k2t[0:L, 0:128], pk[0:L, 0:128], [pk], [k2t])
                    yield
                    pn = psr.next()
                    yield
                    MM(pn[:, 0:128], k2t[0:L, 0:128], vtm[0:L, 0:128], True, True, [k2t, vtm], [pn])
                    yield
                    STT(Sst[l][:, cb, :], Sst[l][:, cb, :], eb[:, c0 + L - 1:c0 + L], pn[:, 0:128], ALU.mult, ALU.add, [Sv, eb, pn], [Sv])
                    yield
                sq = zp.next()
                yield
                ACT(sq[:, 0:nt], osb[:, 0:nt], AF.Square, [osb], [sq])
                yield
                pss = psr.next()
                yield
                MM(pss[:, 0:nt], ones[:], sq[:, 0:nt], True, True, [ones, sq], [pss])
                yield
                TS(sq[:, 0:nt], pss[:, 0:nt], 1.0 / 128.0, RMS_EPS, ALU.mult, ALU.add, [pss], [sq])
                yield
                ACT(sq[:, 0:nt], sq[:, 0:nt], AF.Sqrt, [sq], [sq])
                yield
                S.op("dve", (lambda e, sq=sq: e.reciprocal(out=sq[:, 0:nt], in_=sq[:, 0:nt])), reads=[sq], writes=[sq])
                yield
                TT(osb[:, 0:nt], osb[:, 0:nt], sq[:, 0:nt], ALU.mult, [osb, sq], [osb])
                yield
                STT(aABC[:, cb, 2, 0:nt], osb[:, 0:nt], PB[l][:, 104:105], sg[:, 0:nt], ALU.mult, ALU.mult, [osb, sg, PB[l]], [aV[cb][2]])
                yield
            KI = 2
            for g0 in range(0, 8, KI):
                active = [cb_body(cb, zps[cb % KI]) for cb in range(g0, g0 + KI)]
                if extra is not None:
                    active.append(extra)
                while [a_ for a_ in active if a_ is not extra]:
                    for g_ in list(active):
                        try:
                            next(g_)
                        except StopIteration:
                            active.remove(g_)
                            if g_ is extra:
                                extra = None
            if extra is not None:
                for _ in extra:
                    pass
            for j in range(8):
                sgs = []
                for X in range(3):
                    G = 9 + X
                    p = in_proj(l, G * 8 + j, nt)
                    z = zr.next()
                    ACT(z[:, 0:nt], p[:, 0:nt], AF.Sigmoid, [p, PA[l]], [z], bias=PA[l][:, G * 8 + j:G * 8 + j + 1])
                    sgs.append(z)
                for X in range(3):
                    w = wl(l, 96 + X * 8 + j)
                    p = psr.next()
                    for cb in range(8):
                        MM(p[:, 0:nt], w[:, cb * 128:(cb + 1) * 128], aABC[:, cb, X, 0:nt], cb == 0, cb == 7, [w, aV[cb][X]], [p])
                    TT(sgs[X][:, 0:nt], sgs[X][:, 0:nt], p[:, 0:nt], ALU.mult, [sgs[X], p], [sgs[X]])
                TT(sgs[0][:, 0:nt], sgs[0][:, 0:nt], sgs[1][:, 0:nt], ALU.add, [sgs[0], sgs[1]], [sgs[0]])
                TT(merged[:, j, 0:nt], sgs[0][:, 0:nt], sgs[2][:, 0:nt], ALU.add, [sgs[0], sgs[2]], [merged])
            pw = psB[0:2]
            for j in range(8):
                w = wl(l, 120 + j)
                for nh in range(2):
                    MM(pw[nh][0:TP, :], merged[:, j, 0:TP], w[:, nh * 512:(nh + 1) * 512], j == 0, j == 7, [merged, w], [pw[nh]])

            def resid_ln(src, pws, gi_, bi_, dst):
                for nh in range(2):
                    STT(rbuf[0:TP, nh * 512:(nh + 1) * 512], src[0:TP, nh * 512:(nh + 1) * 512], ALPHA, pws[nh][0:TP, :], ALU.mult, ALU.add, [src, pws[nh]], [rbuf])
                ln_from_rbuf(TP, gi_, bi_, dst)

            resid_ln(xin, pw, 0, 1, x1)
            if tagdbg is not None and ("x1_%d" % l) in dbg_out:
                DMA(dbg_out["x1_%d" % l][tagdbg:tagdbg + TP, :], x1[0:TP, :], reads=[x1])

            for c4 in range(2):
                p = psr.next()
                for k in range(4):
                    c = c4 * 4 + k
                    TR(p[:, k * 128:k * 128 + TP], x1[0:TP, c * 128:(c + 1) * 128], TP, [x1], [p])
                for k in range(4):
                    c = c4 * 4 + k
                    S.op("act", (lambda e, c=c, k=k, p=p: e.copy(out=x1FMb[:, c, po:po + TP], in_=p[:, k * 128:k * 128 + TP])), reads=[p], writes=[x1FMb])
            for hp in range(16):
                w = wl(l, 128 + hp)
                p = psr.next()
                for c in range(8):
                    MM(p[:, 0:nt], w[:, c * 128:(c + 1) * 128], x1FMb[:, c, po:po + nt], c == 0, c == 7, [w, x1FMb], [p])
                S.op("act", (lambda e, hp=hp, p=p: e.copy(out=qT[:, hp, 0:nt], in_=p[:, 0:nt])), reads=[p], writes=[qT])
            for g4 in range(4):
                p = psr.next()
                for k in range(4):
                    hp = g4 * 4 + k
                    MM(p[0:TP, k * 128:(k + 1) * 128], qT[:, hp, 0:TP], keysT[l][:, hp, :], True, True, [qT, keysT[l]], [p])
                CP(pscr[0:TP, g4 * 512:(g4 + 1) * 512], p[0:TP, :], [p], [pscr])
            def topk_gen():
                for hp in range(16):
                    sl = pscr[0:TP, hp * 128:(hp + 1) * 128]
                    yield
                    S.op("dve", (lambda e, hp=hp, sl=sl: e.max(out=sv[0:TP, hp, 0:8], in_=sl)), reads=[pscr], writes=[sv])
                    yield
                    S.op("dve", (lambda e, hp=hp, sl=sl: e.max_index(out=si[0:TP, hp, 0:8], in_max=sv[0:TP, hp, 0:8], in_values=sl)), reads=[pscr, sv], writes=[si])
                    yield
                    S.op("dve", (lambda e, hp=hp, sl=sl: e.match_replace(out=s2[0:TP, :], in_to_replace=sv[0:TP, hp, 0:8], in_values=sl, imm_value=NEG)), reads=[pscr, sv], writes=[s2])
                    yield
                    S.op("dve", (lambda e, hp=hp: e.max(out=sv[0:TP, hp, 8:16], in_=s2[0:TP, :])), reads=[s2], writes=[sv])
                    yield
                    S.op("dve", (lambda e, hp=hp: e.max_index(out=si[0:TP, hp, 8:16], in_max=sv[0:TP, hp, 8:16], in_values=s2[0:TP, :])), reads=[s2, sv], writes=[si])
                    yield
                CP(sif[0:TP], si[0:TP], [si], [sif])
                yield
                svv = sv[0:TP].rearrange("n (h p) k -> n h p k", p=2)
                yield
                cv = cand[0:TP, :].rearrange("n (h i j) -> n h i j", h=8, i=16)
                yield
                TT(cv, svv[:, :, 0, :].unsqueeze(3).to_broadcast([TP, 8, 16, 16]), svv[:, :, 1, :].unsqueeze(2).to_broadcast([TP, 8, 16, 16]), ALU.add, [sv], [cand])
                yield
                for h in range(8):
                    sl = cand[0:TP, h * 256:(h + 1) * 256]
                    yield
                    sl2 = pscr[0:TP, h * 256:(h + 1) * 256]
                    yield
                    S.op("dve", (lambda e, h=h, sl=sl: e.max(out=tops[0:TP, h, 0:8], in_=sl)), reads=[cand], writes=[tops])
                    yield
                    S.op("dve", (lambda e, h=h, sl=sl: e.max_index(out=topp[0:TP, h, 0:8], in_max=tops[0:TP, h, 0:8], in_values=sl)), reads=[cand, tops], writes=[topp])
                    yield
                    S.op("dve", (lambda e, h=h, sl=sl, sl2=sl2: e.match_replace(out=sl2, in_to_replace=tops[0:TP, h, 0:8], in_values=sl, imm_value=NEG)), reads=[cand, tops], writes=[pscr])
                    yield
                    S.op("dve", (lambda e, h=h, sl2=sl2: e.max(out=tops[0:TP, h, 8:16], in_=sl2)), reads=[pscr], writes=[tops])
                    yield
                    S.op("dve", (lambda e, h=h, sl2=sl2: e.max_index(out=topp[0:TP, h, 8:16], in_max=tops[0:TP, h, 8:16], in_values=sl2)), reads=[pscr, tops], writes=[topp])
                    yield
                S.op("dve", lambda e: e.tensor_single_scalar(out=pij[0:TP, 0], in_=topp[0:TP], scalar=4, op=ALU.logical_shift_right), reads=[topp], writes=[pij])
                yield
                S.op("dve", lambda e: e.tensor_single_scalar(out=pij[0:TP, 1], in_=topp[0:TP], scalar=15, op=ALU.bitwise_and), reads=[topp], writes=[pij])
                yield
                CP(pijf[0:TP], pij[0:TP], [pij], [pijf])
                yield
                sifv = sif[0:TP].rearrange("n (h p) k -> n h p k", p=2)
                yield
                oh = qT[0:TP, :, :].rearrange("n a b -> n (a b)")[:, 0:2048].rearrange("n (h k m) -> n h k m", h=8, k=16)
                yield
                for pp in range(2):
                    TT(oh, iota16[0:TP, :].unsqueeze(1).unsqueeze(1).to_broadcast([TP, 8, 16, 16]), pijf[0:TP, pp].unsqueeze(3).to_broadcast([TP, 8, 16, 16]), ALU.is_equal, [iota16, pijf], [qT])
                    yield
                    TT(oh, oh, sifv[:, :, pp, :].unsqueeze(2).to_broadcast([TP, 8, 16, 16]), ALU.mult, [qT, sif], [qT])
                    yield
                    S.op("dve", (lambda e, pp=pp: e.tensor_reduce(out=esel[0:TP, pp], in_=oh, axis=AX.X, op=ALU.add)), reads=[qT], writes=[esel])
                    yield
                TT(gate[0:TP], tops[0:TP], tops[0:TP, :, 0:1].to_broadcast([TP, 8, 16]), ALU.subtract, [tops], [gate])
                yield
                ACT(gate[0:TP], gate[0:TP], AF.Exp, [gate], [gate])
                yield
                S.op("dve", lambda e: e.tensor_reduce(out=small[0:TP, 0:8], in_=gate[0:TP], axis=AX.X, op=ALU.add), reads=[gate], writes=[small])
                yield
                S.op("dve", lambda e: e.reciprocal(out=small[0:TP, 8:16], in_=small[0:TP, 0:8]), reads=[small], writes=[small])
                yield
                TT(gate[0:TP], gate[0:TP], small[0:TP, 8:16].unsqueeze(2).to_broadcast([TP, 8, 16]), ALU.mult, [gate, small], [gate])
                yield
                p = psr.next()
                yield
                srcs = (esel[0:TP, 0].rearrange("n h k -> n (h k)"), esel[0:TP, 1].rearrange("n h k -> n (h k)"), gate[0:TP].rearrange("n h k -> n (h k)"))
                yield
                for k in range(3 if 'noslot' not in DBGF else 0):
                    TR(p[:, k * 128:k * 128 + TP], srcs[k], TP, [esel, gate], [p])
                    yield
                for k in range(3 if 'noslot' not in DBGF else 0):
                    CP(slotT[:, k, po:po + TP], p[:, k * 128:k * 128 + TP], [p], [slotT])
                    yield
            return topk_gen()

        def dense(l, x1l, xouts, nts, pos):
            NK = len(nts)
            NTOT = sum(nts)
            Cv = C_all[:, 0:NTOT * 64].rearrange("p (n j) -> p n j", j=64)
            pout = [[psB[2 * k], psB[2 * k + 1]] for k in range(NK)]
            for half in range(2):
                for c0 in range(0, NTOT, 8):
                    oi = OIr.next()
                    oj = OJr.next()
                    TT(oi[:, :, :], iota128[:, :].unsqueeze(1).to_broadcast([128, 8, 128]), slotT[:, 0, c0:c0 + 8].unsqueeze(2).to_broadcast([128, 8, 128]), ALU.is_equal, [iota128, slotT], [oi])
                    TT(oj[:, :, :], iota128[:, half * 64:(half + 1) * 64].unsqueeze(1).to_broadcast([128, 8, 64]), slotT[:, 1, c0:c0 + 8].unsqueeze(2).to_broadcast([128, 8, 64]), ALU.is_equal, [iota128, slotT], [oj])
                    TT(oj[:, :, :], oj[:, :, :], slotT[:, 2, c0:c0 + 8].unsqueeze(2).to_broadcast([128, 8, 64]), ALU.mult, [oj, slotT], [oj])
                    pc = psr.next()
                    for t in range(8):
                        MM(pc[:, t * 64:(t + 1) * 64], oi[:, t, :], oj[:, t, :], True, True, [oi, oj], [pc])
                    S.op("act", (lambda e, pc=pc, c0=c0: e.copy(out=C_all[:, c0 * 64:(c0 + 8) * 64], in_=pc[:, :])), reads=[pc], writes=[C_all])

                def ph_stage(j):
                    ut = UTr.next()
                    DMA(ut[:], UT[l, j], reads=[UTbuf[l][j]], writes=[ut])
                    vb = VBr.next()
                    DMA(vb[:], VB[l, j], reads=[VBbuf[l][j]], writes=[vb])
                    ph = psr.next()
                    for c in range(8):
                        MM(ph[:, 0:NTOT], ut[:, c * 128:(c + 1) * 128], x1FMb[:, c, 0:NTOT], c == 0, c == 7, [ut, x1FMb], [ph])
                    return ph, vb
                cur = ph_stage(half * 64)
                for jl in range(64):
                    j = half * 64 + jl
                    nxt = ph_stage(j + 1) if jl + 1 < 64 else None
                    ph, vb = cur
                    G = Gr.next()
                    ACT(G[:, 0:NTOT], ph[:, 0:NTOT], AF.Gelu, [ph], [G])
                    Wt = Wr.next()
                    TT(Wt[:, 0:NTOT], G[:, 0:NTOT], Cv[:, 0:NTOT, jl], ALU.mult, [G, C_all], [Wt])
                    for k in range(NK):
                        for nh in range(2):
                            MM(pout[k][nh][0:nts[k], :], Wt[:, pos[k]:pos[k] + nts[k]], vb[:, nh * 512:(nh + 1) * 512], j == 0, j == 127, [Wt, vb], [pout[k][nh]])
                    cur = nxt
            for k in range(NK):
                TP = nts[k]
                for nh in range(2):
                    STT(rbuf[0:TP, nh * 512:(nh + 1) * 512], x1l[k][0:TP, nh * 512:(nh + 1) * 512], ALPHA, pout[k][nh][0:TP, :], ALU.mult, ALU.add, [x1l[k], pout[k][nh]], [rbuf])
                ln_from_rbuf(TP, 2, 3, xouts[k])

        def run_group(srcs, dsts, nts, L):
            NK = len(nts)
            pos = [0, nts[0]][:NK]
            for k in range(NK):
                DMA(xA[k][0:nts[k], :], srcs[k], writes=[xA[k]])
            cur_, nxt_ = xA, xB
            for l in range(NL):
                gprev = None
                for k in range(NK):
                    gprev = mix_topk(l, cur_[k], x1s[k], nts[k], L, pos[k], extra=gprev)
                for _ in gprev:
                    pass
                dense(l, x1s[:NK], nxt_[:NK], nts, pos)
                cur_, nxt_ = nxt_, cur_
            for k in range(NK):
                DMA(dsts[k], cur_[k][0:nts[k], :], reads=[cur_[k]])

        init_states_zero()
        nblk = T // TB
        b = 0
        while b < nblk:
            nk = 2 if b + 1 < nblk else 1
            run_group([xp[(b + k) * TB:(b + k + 1) * TB, :] for k in range(nk)], [yp[(b + k) * TB:(b + k + 1) * TB, :] for k in range(nk)], [TB] * nk, 64)
            b += nk
        out_states(0)
        if with_sample:
            init_states_sample()
            run_group([xsm[:, :]], [ys[:, :]], [32], 32)
            out_states(1)
        with nc.allow_low_precision("bf16 matmul operands"):
            S.run()
    return nc


W_NAMES = ["w_in", "b_in", "conv_a_w", "conv_b_w", "conv_b_b", "lru_wa", "lru_ba", "lru_wx", "lru_bx", "lru_lambda",
           "hgrn_lb_logits", "hgrn_norm_g", "w_out_a", "w_out_b", "w_out_c", "w_o", "ln1_g", "ln1_b", "peer_wq",
           "peer_keys", "peer_u", "peer_v", "ln2_g", "ln2_b"]


def run(inputs, T, NL=DEPTH, with_sample=True, dbg=None, ncores=8):
    import time
    t0 = time.time()
    nc = build(T, NL, with_sample, dbg=dbg)
    print('build time', time.time() - t0, flush=True)
    f = lambda a: np.ascontiguousarray(np.asarray(a, dtype=np.float32))
    wmaps = {k: f(inputs[k]) for k in W_NAMES}
    in_maps = []
    for c in range(ncores):
        m = dict(wmaps)
        m["xp"] = f(inputs["x_prompt"][c % 4, :T])
        m["xs"] = f(inputs["x_sample"][c])
        m["st_ca"] = f(inputs["state_conv_a"][:, c])
        m["st_cb"] = f(inputs["state_conv_b"][:, c])
        m["st_h"] = f(inputs["state_lru"][:, c])
        m["st_S"] = f(inputs["state_hgrn"][:, c])
        in_maps.append(m)
    res = run_bass_kernel_spmd(nc, in_maps, core_ids=list(range(ncores)))
    print('total time', time.time() - t0, flush=True)
    return res.results


def kernel(**inputs):
    T = inputs["x_prompt"].shape[1]
    r = run(inputs, T)
    yp = np.stack([r[c]["yp"] for c in range(4)], 0)
    ys = np.stack([r[c]["ys"] for c in range(8)], 0)
    def pst(name):
        return np.stack([r[c][name] for c in range(4)], 1)
    def sst(name):
        return np.stack([r[c][name] for c in range(8)], 1)
    return (yp, ys, pst("p_ca"), pst("p_cb"), pst("p_h"), pst("p_S"),
            sst("s_ca"), sst("s_cb"), sst("s_h"), sst("s_S"))
```
